# Optimizing a Trainium2 kernel written in Bass

```python
import jax, jax.numpy as jnp
from jax import lax
import numpy as np

D_MODEL = 2048
BATCH = 2
SEQ = 4096
DEPTH = 1
DEC_BATCH = 4
DEC_SEQ = 4096
PAST_LEN = 128

D_FF = 5504
MEM_LEN = 256
MEM_HEADS = 4
MEM_DH = D_MODEL // MEM_HEADS
GLA_HEADS = 4
GLA_DK = 128
GLA_DV = 256
GLA_GATE_RANK = 16
GLA_GATE_TEMP = 16.0
GLA_CHUNK = 64
MLA_HEADS = 8
MLA_Q_RANK = 512
MLA_KV_RANK = 256
MLA_NOPE = 128
MLA_ROPE = 64
MLA_DV = 128
ROPE_THETA = 10000.0
Q_BLOCK = 128
EPS = 1e-6

GLA_QK = GLA_HEADS * GLA_DK
GLA_V = GLA_HEADS * GLA_DV
MLA_V = MLA_HEADS * MLA_DV
MIX_WIDTH = GLA_V + MLA_V
MLA_QK_HEAD = MLA_NOPE + MLA_ROPE
IN_SPLITS = (GLA_QK, GLA_QK, GLA_V, GLA_V, GLA_GATE_RANK, GLA_GATE_RANK, MLA_Q_RANK, MLA_KV_RANK, MLA_ROPE)
D_IN = 2 * GLA_QK + 2 * GLA_V + 2 * GLA_GATE_RANK + MLA_Q_RANK + MLA_KV_RANK + MLA_ROPE

kernel_name = "hybrid_gla_mla_macaron_encoder"


def rms_norm(x, g):
    xf = x.astype(jnp.float32)
    y = xf * lax.rsqrt(jnp.mean(xf * xf, axis=-1, keepdims=True) + EPS)
    return (y * g.astype(jnp.float32)).astype(x.dtype)


def swiglu(x, w_gate, w_up, w_down):
    return (jax.nn.silu(x @ w_gate) * (x @ w_up)) @ w_down


def split_in(z):
    parts, start = [], 0
    for width in IN_SPLITS:
        parts.append(z[..., start:start + width])
        start += width
    return parts


def rotary(x, pos):
    half = x.shape[-1] // 2
    inv = ROPE_THETA ** (-jnp.arange(half, dtype=jnp.float32) / half)
    ang = pos.astype(jnp.float32)[:, None] * inv[None, :]
    cos = jnp.cos(ang)[None, :, None, :]
    sin = jnp.sin(ang)[None, :, None, :]
    xf = x.astype(jnp.float32)
    x1, x2 = xf[..., :half], xf[..., half:]
    return jnp.concatenate([x1 * cos - x2 * sin, x2 * cos + x1 * sin], axis=-1).astype(x.dtype)


def gla_scan(q, k, v, log_a):
    B, S, H, dk = q.shape
    dv = v.shape[-1]
    C = GLA_CHUNK
    N = S // C
    q = q.astype(jnp.float32).reshape(B, N, C, H, dk)
    k = k.astype(jnp.float32).reshape(B, N, C, H, dk)
    v = v.astype(jnp.float32).reshape(B, N, C, H, dv)
    b = jnp.cumsum(log_a.astype(jnp.float32).reshape(B, N, C, H, dk), axis=2)
    b_last = b[:, :, -1]
    q_t = q * jnp.exp(b)
    k_t = k * jnp.exp(-b)
    k_s = k * jnp.exp(b_last[:, :, None] - b)
    A = jnp.einsum('bnihd,bnjhd->bnhij', q_t, k_t)
    mask = jnp.tril(jnp.ones((C, C), dtype=bool))
    A = jnp.where(mask, A, 0.0)
    o_intra = jnp.einsum('bnhij,bnjhv->bnihv', A, v)

    def step(state, xs):
        qt_n, ks_n, v_n, dec_n = xs
        o_n = jnp.einsum('bihd,bhdv->bihv', qt_n, state)
        state = jnp.exp(dec_n)[..., None] * state + jnp.einsum('bjhd,bjhv->bhdv', ks_n, v_n)
        return state, o_n

    xs = (jnp.moveaxis(q_t, 1, 0), jnp.moveaxis(k_s, 1, 0), jnp.moveaxis(v, 1, 0), jnp.moveaxis(b_last, 1, 0))
    init = jnp.zeros((B, H, dk, dv), jnp.float32)
    _, o_inter = lax.scan(step, init, xs)
    o = o_intra + jnp.moveaxis(o_inter, 0, 1)
    return o.reshape(B, S, H, dv)


def gla_group(gq, gk, gv, gr, ga_f, ga_b, wa2_f, ba_f, wa2_b, ba_b, out_norm):
    B, S, _ = gq.shape
    q = gq.reshape(B, S, GLA_HEADS, GLA_DK) * (GLA_DK ** -0.5)
    k = gk.reshape(B, S, GLA_HEADS, GLA_DK)
    v = gv.reshape(B, S, GLA_HEADS, GLA_DV)

    def log_gate(c, w2, bias):
        logits = (c @ w2 + bias).astype(jnp.float32)
        return (jax.nn.log_sigmoid(logits) / GLA_GATE_TEMP).reshape(B, S, GLA_HEADS, GLA_DK)

    flip = lambda t: jnp.flip(t, axis=1)
    o_f = gla_scan(q, k, v, log_gate(ga_f, wa2_f, ba_f))
    o_b = flip(gla_scan(flip(q), flip(k), flip(v), flip(log_gate(ga_b, wa2_b, ba_b))))
    o = o_f + o_b
    o = o * lax.rsqrt(jnp.mean(o * o, axis=-1, keepdims=True) + EPS)
    o = o * out_norm.astype(jnp.float32).reshape(GLA_HEADS, GLA_DV)
    return o.reshape(B, S, GLA_V).astype(gr.dtype) * jax.nn.silu(gr)


def block_attention(q, k, v):
    B, S, H, dh = q.shape
    dv = v.shape[-1]
    nb = S // Q_BLOCK
    scale = dh ** -0.5
    qb = q.reshape(B, nb, Q_BLOCK, H, dh).transpose(1, 0, 2, 3, 4)

    def one(qi):
        s = jnp.einsum('bqhd,bkhd->bhqk', qi, k).astype(jnp.float32) * scale
        p = jax.nn.softmax(s, axis=-1).astype(v.dtype)
        return jnp.einsum('bhqk,bkhv->bqhv', p, v)

    o = lax.map(one, qb)
    return o.transpose(1, 0, 2, 3, 4).reshape(B, S, H, dv)


def mla_group(cq, ckv, kpe, q_norm, w_uq, kv_norm, w_ukv, qk_q_norm, qk_k_norm, pos):
    B, S, _ = cq.shape
    q = (rms_norm(cq, q_norm) @ w_uq).reshape(B, S, MLA_HEADS, MLA_QK_HEAD)
    kv = (rms_norm(ckv, kv_norm) @ w_ukv).reshape(B, S, MLA_HEADS, MLA_NOPE + MLA_DV)
    k_nope, v = kv[..., :MLA_NOPE], kv[..., MLA_NOPE:]
    k_rope = jnp.broadcast_to(kpe[:, :, None, :], (B, S, MLA_HEADS, MLA_ROPE))
    k = jnp.concatenate([k_nope, k_rope], axis=-1)
    q = rms_norm(q, qk_q_norm)
    k = rms_norm(k, qk_k_norm)
    q = jnp.concatenate([q[..., :MLA_NOPE], rotary(q[..., MLA_NOPE:], pos)], axis=-1)
    k = jnp.concatenate([k[..., :MLA_NOPE], rotary(k[..., MLA_NOPE:], pos)], axis=-1)
    return block_attention(q, k, v).reshape(B, S, MLA_V)


def memory_xattn(xn, memn, wq, wk, wv, qn, kn, wo):
    B, S, _ = xn.shape
    M = memn.shape[1]
    q = rms_norm((xn @ wq).reshape(B, S, MEM_HEADS, MEM_DH), qn)
    k = rms_norm((memn @ wk).reshape(B, M, MEM_HEADS, MEM_DH), kn)
    v = (memn @ wv).reshape(B, M, MEM_HEADS, MEM_DH)
    s = jnp.einsum('bqhd,bkhd->bhqk', q, k).astype(jnp.float32) * (MEM_DH ** -0.5)
    p = jax.nn.softmax(s, axis=-1).astype(v.dtype)
    o = jnp.einsum('bhqk,bkhv->bqhv', p, v).reshape(B, S, D_MODEL)
    return o @ wo


def encoder_layer(x, mem, pos, P, l):
    h = x + 0.5 * swiglu(rms_norm(x, P['ffn1_norm'][l]), P['ffn1_w_gate'][l], P['ffn1_w_up'][l], P['ffn1_w_down'][l])
    n = rms_norm(h, P['mix_norm'][l])
    z = n @ P['w_in'][l]
    gq, gk, gv, gr, ga_f, ga_b, cq, ckv, kpe = split_in(z)
    y_gla = gla_group(gq, gk, gv, gr, ga_f, ga_b,
                      P['gla_wa2_fwd'][l], P['gla_ba_fwd'][l], P['gla_wa2_bwd'][l], P['gla_ba_bwd'][l],
                      P['gla_out_norm'][l])
    y_mla = mla_group(cq, ckv, kpe, P['mla_q_norm'][l], P['mla_w_uq'][l], P['mla_kv_norm'][l],
                      P['mla_w_ukv'][l], P['mla_qk_q_norm'][l], P['mla_qk_k_norm'][l], pos)
    h = h + jnp.concatenate([y_gla, y_mla], axis=-1) @ P['w_out'][l]
    h = h + memory_xattn(rms_norm(h, P['xattn_norm'][l]), rms_norm(mem, P['mem_norm'][l]),
                         P['xattn_wq'][l], P['xattn_wk'][l], P['xattn_wv'][l],
                         P['xattn_q_norm'][l], P['xattn_k_norm'][l], P['xattn_wo'][l])
    h = h + 0.5 * swiglu(rms_norm(h, P['ffn2_norm'][l]), P['ffn2_w_gate'][l], P['ffn2_w_up'][l], P['ffn2_w_down'][l])
    return h


def run_trunk(x, mem, P):
    pos = jnp.arange(x.shape[1], dtype=jnp.int32)
    h = x
    for l in range(DEPTH):
        h = encoder_layer(h, mem, pos, P, l)
    return h


def setup_inputs(seed: int = 0) -> dict:
    key = jax.random.key(seed)
    ks = iter(jax.random.split(key, 64))

    def w(shape, fan_in):
        return jax.random.normal(next(ks), shape, jnp.float32) * (fan_in ** -0.5)

    def gain(shape):
        return 1.0 + 0.02 * jax.random.normal(next(ks), shape, jnp.float32)

    def bias(shape):
        return 0.1 * jax.random.normal(next(ks), shape, jnp.float32)

    L = DEPTH
    return {
        "x_prompt": jax.random.normal(next(ks), (BATCH, SEQ, D_MODEL), jnp.float32),
        "x_sample": jax.random.normal(next(ks), (DEC_BATCH, DEC_SEQ, D_MODEL), jnp.float32),
        "mem_prompt": jax.random.normal(next(ks), (BATCH, MEM_LEN, D_MODEL), jnp.float32),
        "mem_sample": jax.random.normal(next(ks), (DEC_BATCH, MEM_LEN, D_MODEL), jnp.float32),
        "ffn1_norm": gain((L, D_MODEL)),
        "ffn1_w_gate": w((L, D_MODEL, D_FF), D_MODEL),
        "ffn1_w_up": w((L, D_MODEL, D_FF), D_MODEL),
        "ffn1_w_down": w((L, D_FF, D_MODEL), D_FF),
        "mix_norm": gain((L, D_MODEL)),
        "w_in": w((L, D_MODEL, D_IN), D_MODEL),
        "gla_wa2_fwd": w((L, GLA_GATE_RANK, GLA_QK), GLA_GATE_RANK),
        "gla_ba_fwd": bias((L, GLA_QK)),
        "gla_wa2_bwd": w((L, GLA_GATE_RANK, GLA_QK), GLA_GATE_RANK),
        "gla_ba_bwd": bias((L, GLA_QK)),
        "gla_out_norm": gain((L, GLA_V)),
        "mla_q_norm": gain((L, MLA_Q_RANK)),
        "mla_w_uq": w((L, MLA_Q_RANK, MLA_HEADS * MLA_QK_HEAD), MLA_Q_RANK),
        "mla_kv_norm": gain((L, MLA_KV_RANK)),
        "mla_w_ukv": w((L, MLA_KV_RANK, MLA_HEADS * (MLA_NOPE + MLA_DV)), MLA_KV_RANK),
        "mla_qk_q_norm": gain((L, MLA_QK_HEAD)),
        "mla_qk_k_norm": gain((L, MLA_QK_HEAD)),
        "w_out": w((L, MIX_WIDTH, D_MODEL), MIX_WIDTH),
        "xattn_norm": gain((L, D_MODEL)),
        "mem_norm": gain((L, D_MODEL)),
        "xattn_wq": w((L, D_MODEL, D_MODEL), D_MODEL),
        "xattn_wk": w((L, D_MODEL, D_MODEL), D_MODEL),
        "xattn_wv": w((L, D_MODEL, D_MODEL), D_MODEL),
        "xattn_q_norm": gain((L, MEM_DH)),
        "xattn_k_norm": gain((L, MEM_DH)),
        "xattn_wo": w((L, D_MODEL, D_MODEL), D_MODEL),
        "ffn2_norm": gain((L, D_MODEL)),
        "ffn2_w_gate": w((L, D_MODEL, D_FF), D_MODEL),
        "ffn2_w_up": w((L, D_MODEL, D_FF), D_MODEL),
        "ffn2_w_down": w((L, D_FF, D_MODEL), D_FF),
    }


def reference(x_prompt, x_sample, mem_prompt, mem_sample,
              ffn1_norm, ffn1_w_gate, ffn1_w_up, ffn1_w_down,
              mix_norm, w_in,
              gla_wa2_fwd, gla_ba_fwd, gla_wa2_bwd, gla_ba_bwd, gla_out_norm,
              mla_q_norm, mla_w_uq, mla_kv_norm, mla_w_ukv, mla_qk_q_norm, mla_qk_k_norm,
              w_out,
              xattn_norm, mem_norm, xattn_wq, xattn_wk, xattn_wv, xattn_q_norm, xattn_k_norm, xattn_wo,
              ffn2_norm, ffn2_w_gate, ffn2_w_up, ffn2_w_down):
    P = dict(
        ffn1_norm=ffn1_norm, ffn1_w_gate=ffn1_w_gate, ffn1_w_up=ffn1_w_up, ffn1_w_down=ffn1_w_down,
        mix_norm=mix_norm, w_in=w_in,
        gla_wa2_fwd=gla_wa2_fwd, gla_ba_fwd=gla_ba_fwd, gla_wa2_bwd=gla_wa2_bwd, gla_ba_bwd=gla_ba_bwd,
        gla_out_norm=gla_out_norm,
        mla_q_norm=mla_q_norm, mla_w_uq=mla_w_uq, mla_kv_norm=mla_kv_norm, mla_w_ukv=mla_w_ukv,
        mla_qk_q_norm=mla_qk_q_norm, mla_qk_k_norm=mla_qk_k_norm,
        w_out=w_out,
        xattn_norm=xattn_norm, mem_norm=mem_norm, xattn_wq=xattn_wq, xattn_wk=xattn_wk, xattn_wv=xattn_wv,
        xattn_q_norm=xattn_q_norm, xattn_k_norm=xattn_k_norm, xattn_wo=xattn_wo,
        ffn2_norm=ffn2_norm, ffn2_w_gate=ffn2_w_gate, ffn2_w_up=ffn2_w_up, ffn2_w_down=ffn2_w_down,
    )
    y_prompt = run_trunk(x_prompt, mem_prompt, P)
    y_sample = run_trunk(x_sample, mem_sample, P)
    return (y_prompt, y_sample)
```

```python
import numpy as np
from contextlib import ExitStack
import concourse.bass as bass
import concourse.mybir as mybir
from concourse.bass_utils import run_bass_kernel_spmd

F32 = mybir.dt.float32
BF16 = mybir.dt.bfloat16
AF = mybir.ActivationFunctionType
ALU = mybir.AluOpType

D = 2048
DFF = 5504
DIN = 3936
EPS = 1e-6
SEQ = 4096
MEM = 256

WSHAPES = {
    "ffn1_w_gate": (D, DFF), "ffn1_w_up": (D, DFF), "ffn1_w_down": (DFF, D),
    "w_in": (D, DIN), "mla_w_uq": (512, 1536), "mla_w_ukv": (256, 2048),
    "w_out": (D, D), "xattn_wq": (D, D), "xattn_wk": (D, D), "xattn_wv": (D, D),
    "xattn_wo": (D, D),
    "ffn2_w_gate": (D, DFF), "ffn2_w_up": (D, DFF), "ffn2_w_down": (DFF, D),
}
W_EARLY = ["ffn1_w_gate", "ffn1_w_up", "ffn1_w_down", "w_in", "mla_w_uq", "mla_w_ukv"]
W_LATE = ["w_out", "xattn_wk", "xattn_wv", "xattn_wq", "xattn_wo",
          "ffn2_w_gate", "ffn2_w_up", "ffn2_w_down"]


class Buf:
    __slots__ = ("name", "w", "r", "dram")

    def __init__(self, name="", dram=False):
        self.name = name
        self.w = {}
        self.r = {}
        self.dram = dram


class DmaSem:
    __slots__ = ("sem", "val", "key")

    def __init__(self, sem, key):
        self.sem = sem
        self.val = 0
        self.key = key


class Sched:
    def __init__(self, nc, stack):
        self.nc = nc
        self.stack = stack
        self.eng = {"pe": nc.tensor, "act": nc.scalar, "dve": nc.vector,
                    "pool": nc.gpsimd, "sp": nc.sync}
        self.esem = {}
        self.ecnt = {}
        self.known = {}
        self.nsem = 0
        self.dsems = []
        self.dmap = {}
        for e in self.eng:
            self.esem[e] = self._sem("e_" + e)
            self.ecnt[e] = 0
            self.known[e] = {}
        self.n_wait = 0
        self.n_ins = 0

    def _sem(self, name):
        self.nsem += 1
        return self.stack.enter_context(self.nc.semaphore("%s_%d" % (name, self.nsem)))

    def dma_sem(self, name):
        s = self._sem("d_" + name)
        d = DmaSem(s, "d_%s_%d" % (name, self.nsem))
        self.dsems.append(d)
        self.dmap[d.key] = d
        return d

    def _wait_for(self, e, reads, writes, skip_waw=None):
        need = {}
        own = "e_" + e
        for b in reads:
            for k, sv in b.w.items():
                if k not in need or need[k][1] < sv[1]:
                    need[k] = sv
        for b in writes:
            for k, sv in b.w.items():
                if k == skip_waw or (b.dram and k in self.dmap):
                    continue
                if k not in need or need[k][1] < sv[1]:
                    need[k] = sv
            for k, sv in b.r.items():
                if k == own:
                    continue
                if k not in need or need[k][1] < sv[1]:
                    need[k] = sv
        kn = self.known[e]
        for k, (s, v) in need.items():
            if k == own and e == "pe":
                continue
            if kn.get(k, 0) >= v:
                continue
            if k in self.dmap:
                v = self.dmap[k].val
            self.eng[e].wait_ge(s, v)
            self.n_wait += 1
            kn[k] = v

    def op(self, e, fn, reads=(), writes=()):
        self._wait_for(e, reads, writes)
        ins = fn(self.eng[e])
        self.n_ins += 1
        self.ecnt[e] += 1
        ins.then_inc(self.esem[e], 1)
        key = "e_" + e
        ev = (self.esem[e], self.ecnt[e])
        for b in reads:
            b.r[key] = ev
        for b in writes:
            b.w = {key: ev}
            b.r = {}
        return ins

    def mm(self, fn, reads=(), writes=(), inc=True):
        self._wait_for("pe", reads, writes)
        ins = fn(self.eng["pe"])
        self.n_ins += 1
        key = "e_pe"
        if inc:
            self.ecnt["pe"] += 1
            ins.then_inc(self.esem["pe"], 1)
        ev = (self.esem["pe"], self.ecnt["pe"] + (0 if inc else 1))
        for b in reads:
            b.r[key] = ev
        for b in writes:
            b.w = {key: ev}
            b.r = {}
        return ins

    def dma(self, e, ds, out, in_, reads=(), writes=(), **kw):
        self._wait_for(e, reads, writes, skip_waw=ds.key)
        ins = self.eng[e].dma_start(out=out, in_=in_, **kw)
        self.n_ins += 1
        ds.val += 16
        ins.then_inc(ds.sem, 16)
        ev = (ds.sem, ds.val)
        for b in reads:
            b.r[ds.key] = ev
        for b in writes:
            if b.dram:
                b.w[ds.key] = ev
            else:
                b.w = {ds.key: ev}
                b.r = {}
        return ins

    def barrier(self):
        for e in self.eng:
            kn = self.known[e]
            for f in self.eng:
                if f == "sp":
                    continue
                k = "e_" + f
                v = self.ecnt[f]
                if v > 0 and kn.get(k, 0) < v and not (f == e):
                    self.eng[e].wait_ge(self.esem[f], v)
                    kn[k] = v
            for d in self.dsems:
                if d.val > 0 and kn.get(d.key, 0) < d.val:
                    self.eng[e].wait_ge(d.sem, d.val)
                    kn[d.key] = d.val


def build(S=SEQ, dbg=False, stop=None):
    NT = S // 512
    NCH = S // 128
    nc = bass.Bass("TRN2", target_bir_lowering=False)

    def din(name, shape, dt=F32):
        return nc.dram_tensor(name, list(shape), dt, kind="ExternalInput").ap()

    def dscr(name, shape, dt):
        return nc.dram_tensor(name, list(shape), dt,
                              kind="ExternalOutput" if dbg else "Internal").ap()

    x_d = din("x", (S, D))
    mem_d = din("mem", (MEM, D))
    y_d = nc.dram_tensor("y", [S, D], F32, kind="ExternalOutput").ap()
    w32 = {n: din(n, sh) for n, sh in WSHAPES.items()}
    wbf = {n: nc.dram_tensor("bf_" + n, list(sh), BF16, kind="Internal").ap()
           for n, sh in WSHAPES.items()}
    wv = {n: wbf[n].rearrange("(kc p) c -> p kc c", p=128) for n in WSHAPES}
    cst = {
        "ident": din("c_ident", (128, 128)),
        "g_ffn1": din("g_ffn1", (128, 16)), "g_mix": din("g_mix", (128, 16)),
        "g_xn": din("g_xn", (128, 16)), "g_mem": din("g_mem", (128, 16)),
        "g_ffn2": din("g_ffn2", (128, 16)),
        "g_mq": din("g_mq", (128, 4)), "g_mkv": din("g_mkv", (128, 2)),
        "g_qq": din("g_qq", (128, 2)), "g_qk": din("g_qk", (128, 2)),
        "g_xq": din("g_xq", (128, 4)), "g_xk": din("g_xk", (128, 4)),
        "pswap": din("c_pswap", (64, 64)),
    }
    cos_d = din("c_cos", (64, S))
    sin_d = din("c_sin", (64, S))
    gla_gain_d = din("gla_gain_bc", (128, 1024))
    wa2c_d = [din("wa2c_f", (32, 512)), din("wa2c_b", (32, 512))]
    ba_d = [din("ba_f", (1, 512)), din("ba_b", (1, 512))]
    tri_d = {n: din("c_" + n, (128, 128)) for n in ["LnF", "UnF", "LnB", "UnB"]}
    mask_d = [din("c_maskF", (128, 512)), din("c_maskB", (128, 512))]

    hT_s = dscr("s_hT", (D, S), F32)
    gqT_s = dscr("s_gqT", (512, S), F32)
    gkT_s = dscr("s_gkT", (512, S), F32)
    gk_s = dscr("s_gk", (S, 512), F32)
    gv_s = dscr("s_gv", (S, 1024), BF16)
    gsr_s = dscr("s_gsr", (S, 1024), F32)
    gaT_s = dscr("s_gaT", (32, S), F32)
    QnT_s = dscr("s_QnT", (1024, S), BF16)
    QrT_s = dscr("s_QrT", (512, S), BF16)
    KnT_s = dscr("s_KnT", (1024, S), BF16)
    KrT_s = dscr("s_KrT", (512, S), BF16)
    V_s = dscr("s_V", (S, 1024), BF16)
    of_s = dscr("s_of", (S, 1024), F32)
    ymT_s = dscr("s_ymT", (D, S), BF16)

    with ExitStack() as G:
        sc = Sched(nc, G)

        sb_cnt = [0]

        def sbuf(stack, name, shape, dt):
            sb_cnt[0] += 1
            return stack.enter_context(nc.sbuf_tensor("%s_%d" % (name, sb_cnt[0]), list(shape), dt))

        ps = G.enter_context(nc.psum_tensor("ps", [128, 8, 512], F32))
        Bps = [Buf("ps%d" % i) for i in range(8)]
        bank_i = [0]

        bank_pool = [list(range(8))]

        def bank():
            p = bank_pool[0]
            b = p[bank_i[0] % len(p)]
            bank_i[0] += 1
            return b

        cs = {}
        Bc = Buf("consts")
        dc = sc.dma_sem("const")
        for n, ap in cst.items():
            cs[n] = sbuf(G, "k_" + n, ap.shape, F32)
            sc.dma("sp", dc, cs[n][:], ap, writes=[Bc])
        ones_bf = sbuf(G, "ones_bf", (128, 128), BF16)
        ones_f = sbuf(G, "ones_f", (128, 128), F32)
        sc.op("dve", lambda e: e.memset(ones_bf[:], 1.0), writes=[Bc])
        sc.op("dve", lambda e: e.memset(ones_f[:], 1.0), writes=[Bc])
        ident = cs["ident"]

        Bw = {n: Buf("w_" + n, dram=True) for n in WSHAPES}
        dcast = {n: sc.dma_sem("cast_" + n) for n in WSHAPES}

        def cast(names, after=()):
            if after:
                sc._wait_for("pool", list(after), ())
            for n in names:
                R, C = WSHAPES[n]
                for r0 in range(0, R, 512):
                    r1 = min(R, r0 + 512)
                    sc.dma("pool", dcast[n], wbf[n][r0:r1, :], w32[n][r0:r1, :],
                           writes=[Bw[n]], max_dma_last_dim=8192)

        Bwp = {}
        for cg in range(3):
            c0 = cg * 2048
            c1 = min(DFF, c0 + 2048)
            for n in ("ffn1_w_gate", "ffn1_w_up"):
                d = sc.dma_sem("castp")
                Bwp[(n, cg)] = Buf()
                for r0 in range(0, D, 512):
                    sc.dma("pool", d, wbf[n][r0:r0 + 512, c0:c1], w32[n][r0:r0 + 512, c0:c1], writes=[Bwp[(n, cg)]])
        for kp in range(3):
            r0 = kp * 2048
            r1 = min(DFF, r0 + 2048)
            d = sc.dma_sem("castp")
            Bwp[("ffn1_w_down", kp)] = Buf()
            for rr in range(r0, r1, 512):
                sc.dma("pool", d, wbf["ffn1_w_down"][rr:min(r1, rr + 512), :], w32["ffn1_w_down"][rr:min(r1, rr + 512), :],
                       writes=[Bwp[("ffn1_w_down", kp)]])

        def wdep(n, k0, c0):
            if n in ("ffn1_w_gate", "ffn1_w_up"):
                return Bwp[(n, c0 // 2048)]
            if n == "ffn1_w_down":
                return Bwp[(n, k0 // 16)]
            return Bw[n]

        class WStream:
            def __init__(self, stack, nslots=4):
                self.n = nslots
                self.t = [sbuf(stack, "wt%d" % i, (128, 8192), BF16) for i in range(nslots)]
                self.B = [Buf("wt%d" % i) for i in range(nslots)]
                self.d = [sc.dma_sem("wt%d" % i) for i in range(nslots)]
                self.plan = []
                self.issued = 0
                self.used = 0

            def view(self, s, nk, ncol):
                return self.t[s][:, 0:nk * ncol].rearrange("p (k c) -> p k c", k=nk)

            def _issue(self, i):
                (n, k0, nk, c0, ncol) = self.plan[i]
                s = i % self.n
                sc.dma("sp", self.d[s], self.view(s, nk, ncol),
                       wv[n][:, k0:k0 + nk, c0:c0 + ncol], reads=[wdep(n, k0, c0)], writes=[self.B[s]])

            def next(self, tag):
                i = self.used
                assert self.plan[i] == tag, (i, self.plan[i], tag)
                lim = min(len(self.plan), i + self.n - 1)
                while self.issued < lim:
                    self._issue(self.issued)
                    self.issued += 1
                self.used += 1
                s = i % self.n
                return self.view(s, tag[2], tag[4]), self.B[s]

        def plan_ffn(pref):
            p = []
            for cb in range(11):
                ncol = 512 if cb < 10 else 384
                p.append((pref + "_w_gate", 0, 16, cb * 512, ncol))
                p.append((pref + "_w_up", 0, 16, cb * 512, ncol))
            for cg in range(4):
                for kp in range(3):
                    p.append((pref + "_w_down", kp * 16, 16 if kp < 2 else 11, cg * 512, 512))
            return p

        def rstd_from(pb, nfeat, rs, Brs):
            sc.op("act", lambda e: e.activation(out=rs[:], in_=ps[:, pb, :], func=AF.Ln,
                                                scale=1.0 / nfeat, bias=EPS),
                  reads=[Bps[pb]], writes=[Brs])
            sc.op("act", lambda e: e.activation(out=rs[:], in_=rs[:], func=AF.Exp, scale=-0.5), reads=[Brs], writes=[Brs])

        def ffn(ws, pref, xn, Bxn, res, Bres, hid, Bhid, sgt, Bsgt):
            for cb in range(11):
                ncol = 512 if cb < 10 else 384
                wg, Bg = ws.next((pref + "_w_gate", 0, 16, cb * 512, ncol))
                wu, Bu = ws.next((pref + "_w_up", 0, 16, cb * 512, ncol))
                for m in range(ncol // 128):
                    j = cb * 4 + m
                    pg = bank()
                    pu = bank()
                    for k in range(16):
                        sc.mm(lambda e: e.matmul(ps[:, pg, :], lhsT=wg[:, k, m * 128:(m + 1) * 128],
                                                 rhs=xn[:, k, :], start=(k == 0), stop=(k == 15)),
                              reads=[Bg, Bxn[k]], writes=[Bps[pg]], inc=(k == 15))
                    for k in range(16):
                        sc.mm(lambda e: e.matmul(ps[:, pu, :], lhsT=wu[:, k, m * 128:(m + 1) * 128],
                                                 rhs=xn[:, k, :], start=(k == 0), stop=(k == 15)),
                              reads=[Bu, Bxn[k]], writes=[Bps[pu]], inc=(k == 15))
                    tmp = sgt[j % len(sgt)]
                    Bt = Bsgt[j % len(sgt)]
                    sc.op("act", lambda e: e.activation(out=tmp[:], in_=ps[:, pg, :], func=AF.Silu),
                          reads=[Bps[pg]], writes=[Bt])
                    sc.op("dve", lambda e: e.tensor_tensor(out=hid[:, j, :], in0=ps[:, pu, :],
                                                           in1=tmp[:], op=ALU.mult),
                          reads=[Bps[pu], Bt], writes=[Bhid[j]])
            for cg in range(4):
                pbs = [bank() for _ in range(4)]
                for kp in range(3):
                    nk = 16 if kp < 2 else 11
                    wd, Bd = ws.next((pref + "_w_down", kp * 16, nk, cg * 512, 512))
                    for m in range(4):
                        for k in range(nk):
                            kk = kp * 16 + k
                            sc.mm(lambda e: e.matmul(ps[:, pbs[m], :],
                                                     lhsT=wd[:, k, m * 128:(m + 1) * 128],
                                                     rhs=hid[:, kk, :], start=(kk == 0), stop=(kk == 42)),
                                  reads=[Bd, Bhid[kk]], writes=[Bps[pbs[m]]], inc=(k == nk - 1))
                for m in range(4):
                    ob = cg * 4 + m
                    sc.op("dve", lambda e: e.scalar_tensor_tensor(
                        out=res[:, ob, :], in0=ps[:, pbs[m], :], scalar=0.5, in1=res[:, ob, :],
                        op0=ALU.mult, op1=ALU.add),
                        reads=[Bps[pbs[m]], Bres[ob]], writes=[Bres[ob]])

        def norm16(res, Bres, gname, xn, Bxn, sqr, Bsqr, rs, Brs, N=512):
            pb = bank()
            for k in range(16):
                q = sqr[k % len(sqr)]
                Bq = Bsqr[k % len(sqr)]
                if k % 2 == 0:
                    sc.op("act", lambda e: e.activation(out=q[:, 0:N], in_=res[:, k, 0:N], func=AF.Square),
                          reads=[Bres[k]], writes=[Bq])
                else:
                    sc.op("pool", lambda e: e.tensor_tensor(out=q[:, 0:N], in0=res[:, k, 0:N],
                                                            in1=res[:, k, 0:N], op=ALU.mult),
                          reads=[Bres[k]], writes=[Bq])
                sc.mm(lambda e: e.matmul(ps[:, pb, 0:N], lhsT=ones_bf[:], rhs=q[:, 0:N],
                                         start=(k == 0), stop=(k == 15)),
                      reads=[Bq, Bc], writes=[Bps[pb]], inc=True)
            sc.op("act", lambda e: e.activation(out=rs[:, 0:N], in_=ps[:, pb, 0:N], func=AF.Ln,
                                                scale=1.0 / D, bias=EPS),
                  reads=[Bps[pb]], writes=[Brs])
            sc.op("act", lambda e: e.activation(out=rs[:, 0:N], in_=rs[:, 0:N], func=AF.Exp, scale=-0.5),
                  reads=[Brs], writes=[Brs])
            g = cs[gname]
            for k in range(16):
                sc.op("dve", lambda e: e.scalar_tensor_tensor(
                    out=xn[:, k, 0:N], in0=res[:, k, 0:N], scalar=g[:, k:k + 1], in1=rs[:, 0:N],
                    op0=ALU.mult, op1=ALU.mult),
                    reads=[Bres[k], Brs, Bc], writes=[Bxn[k]])

        def load_T(src_rows, nrow_groups, res, Bres, stg, Bstg, dstg, col0, preloaded=False):
            for g in range(nrow_groups):
                if preloaded:
                    s = g
                else:
                    s = load_T.cnt % 2
                    load_T.cnt += 1
                    sc.dma("act", dstg[s], stg[s][:], src_rows(g), writes=[Bstg[s]])
                for kq in range(4):
                    pb = bank()
                    for c in range(4):
                        k = kq * 4 + c
                        sc.mm(lambda e: e.transpose(out=ps[:, pb, c * 128:(c + 1) * 128],
                                                    in_=stg[s][:, k * 128:(k + 1) * 128], identity=ident[:]),
                              reads=[Bstg[s], Bc], writes=[Bps[pb]], inc=(c == 3))
                    o_ap = res[:, kq * 4:(kq + 1) * 4, col0 + g * 128:col0 + (g + 1) * 128]
                    i_ap = ps[:, pb, :].rearrange("p (c n) -> p c n", c=4)
                    bl = [Bres[kq * 4 + c] for c in range(4)]
                    if kq % 2 == 0:
                        sc.op("act", lambda e: e.activation(out=o_ap, in_=i_ap, func=AF.Copy),
                              reads=[Bps[pb]], writes=bl)
                    else:
                        sc.op("dve", lambda e: e.tensor_copy(out=o_ap, in_=i_ap),
                              reads=[Bps[pb]], writes=bl)
        load_T.cnt = 0

        Bscr = {n: Buf("scr_" + n, dram=True) for n in
                ["hT", "gqT", "gkT", "gk", "gv", "gsr", "gaT", "QnT", "QrT", "KnT", "KrT", "V", "of", "ymT"]}
        with ExitStack() as A:
            ws = WStream(A)
            for t in range(NT):
                ws.plan += plan_ffn("ffn1")
            xT = sbuf(A, "xT", (128, 16, 512), F32)
            BxT = [Buf("xT%d" % k) for k in range(16)]
            xn = sbuf(A, "xn", (128, 16, 512), BF16)
            Bxn = [Buf("xn%d" % k) for k in range(16)]
            hid = sbuf(A, "hid", (128, 43, 512), BF16)
            Bhid = [Buf("hid%d" % k) for k in range(43)]
            stg = [sbuf(A, "stg%d" % i, (128, 2048), F32) for i in range(4)]
            Bstg = [Buf() for _ in range(4)]
            dstg = [sc.dma_sem("stg%d" % i) for i in range(4)]

            def x_prefetch(t):
                if t < NT:
                    for g in range(4):
                        sc.dma("act", dstg[g], stg[g][:], x_d[t * 512 + g * 128: t * 512 + (g + 1) * 128, :],
                               writes=[Bstg[g]])
            x_prefetch(0)
            sgt = [sbuf(A, "sgt%d" % i, (128, 512), F32) for i in range(3)]
            Bsgt = [Buf() for _ in range(3)]
            sqr = [sbuf(A, "sqr%d" % i, (128, 512), BF16) for i in range(4)]
            Bsqr = [Buf() for _ in range(4)]
            rs = sbuf(A, "rs", (128, 512), F32)
            Brs = Buf()
            dhs = sc.dma_sem("hstore")
            for t in range(NT):
                tc0 = t * 512
                tcols = slice(tc0, tc0 + 512)
                load_T(None, 4, xT, BxT, stg, Bstg, dstg, 0, preloaded=True)
                x_prefetch(t + 1)
                norm16(xT, BxT, "g_ffn1", xn, Bxn, sqr, Bsqr, rs, Brs)
                ffn(ws, "ffn1", xn, Bxn, xT, BxT, hid, Bhid, sgt, Bsgt)
                if dbg and t == 0:
                    d_xn = nc.dram_tensor("d_xn", [128, 16, 512], BF16, kind="ExternalOutput").ap()
                    d_hid = nc.dram_tensor("d_hid", [128, 43, 512], BF16, kind="ExternalOutput").ap()
                    sc.dma("act", dhs, d_xn, xn[:], reads=Bxn, writes=[Buf()])
                    sc.dma("act", dhs, d_hid, hid[:], reads=Bhid, writes=[Buf()])
                if t == 0:
                    cast(["w_in", "mla_w_uq", "mla_w_ukv"] + W_LATE, after=[BxT[15]])
                sc.dma("sp", dhs, hT_s.rearrange("(k p) s -> p k s", p=128)[:, :, tcols], xT[:],
                       reads=BxT, writes=[Bscr["hT"]])
            sc.barrier()
        if stop == "ffn1":
            print("instructions", sc.n_ins, "waits", sc.n_wait)
            return nc

        with ExitStack() as A:
            ws = WStream(A, nslots=3)
            for t in range(NT):
                ws.plan += [("w_in", 0, 16, c0, 512) for c0 in range(0, 3072, 512)]
                ws.plan += [("w_in", 0, 16, 3072, 32), ("w_in", 0, 16, 3104, 512), ("w_in", 0, 16, 3616, 320)]
            xT = sbuf(A, "xT2", (128, 16, 512), F32)
            BxT = [Buf("xT%d" % k) for k in range(16)]
            dxT = sc.dma_sem("xTload")
            xn = sbuf(A, "xn2", (128, 16, 512), BF16)
            Bxn = [Buf("xn%d" % k) for k in range(16)]
            sqr = [sbuf(A, "sqr2_%d" % i, (128, 512), BF16) for i in range(4)]
            Bsqr = [Buf() for _ in range(4)]
            rs = sbuf(A, "rs2", (128, 512), F32)
            Brs = Buf()
            zs = [sbuf(A, "zs%d" % i, (128, 4, 512), F32) for i in range(2)]
            Bzs = [Buf() for _ in range(2)]
            dzs = [sc.dma_sem("zs%d" % i) for i in range(2)]
            zcnt = [0]
            vst = [sbuf(A, "vst%d" % i, (128, 1024), BF16) for i in range(2)]
            Bvst = [Buf() for _ in range(2)]
            dvst = [sc.dma_sem("vst%d" % i) for i in range(2)]
            vsV = [sbuf(A, "vsV%d" % i, (128, 1024), BF16) for i in range(2)]
            BvsV = [Buf() for _ in range(2)]
            dvsV = [sc.dma_sem("vsV%d" % i) for i in range(2)]
            vcntV = [0]
            cqT = sbuf(A, "cqT", (128, 4, 512), F32)
            BcqT = [Buf() for _ in range(4)]
            ckvT = sbuf(A, "ckvT", (128, 2, 512), F32)
            BckvT = [Buf() for _ in range(2)]
            kpeT = sbuf(A, "kpeT", (64, 512), F32)
            BkpeT = Buf()
            sqkpe = sbuf(A, "sqkpe", (64, 512), BF16)
            Bsqkpe = Buf()
            cqn = sbuf(A, "cqn", (128, 4, 512), BF16)
            Bcqn = [Buf() for _ in range(4)]
            ckvn = sbuf(A, "ckvn", (128, 2, 512), BF16)
            Bckvn = [Buf() for _ in range(2)]
            qk_st = [[sbuf(A, "qkst%d_%d" % (a_, i), (128, 512), BF16) for i in range(2)] for a_ in range(4)]
            Bqk_st = [[Buf() for i in range(2)] for a_ in range(4)]
            dqk = [sc.dma_sem("qkst%d" % i) for i in range(2)]
            rq = [sbuf(A, "rq%d" % i, (128, 512), F32) for i in range(2)]
            Brq = [Buf() for _ in range(2)]
            rope_f = [sbuf(A, "ropef%d" % i, (64, 512), F32) for i in range(2)]
            Brope_f = [Buf() for _ in range(2)]
            rt1 = [sbuf(A, "rt1_%d" % i, (64, 512), F32) for i in range(2)]
            rt2 = [sbuf(A, "rt2_%d" % i, (64, 512), F32) for i in range(2)]
            Brt1 = [Buf() for _ in range(2)]
            Brt2 = [Buf() for _ in range(2)]
            cosT = sbuf(A, "cosT", (64, 512), F32)
            sinT = sbuf(A, "sinT", (64, 512), F32)
            Bcs = Buf()
            dcs = sc.dma_sem("cossin")
            ropec = [0]

            def store_z(fill, dst_ap, Bdst):
                i = zcnt[0] % 2
                zcnt[0] += 1
                fill(zs[i], Bzs[i])
                sc.dma("sp", dzs[i], dst_ap, zs[i][:], reads=[Bzs[i]], writes=[Bdst])

            wq = sbuf(A, "wq_res", (128, 4, 1536), BF16)
            wkv = sbuf(A, "wkv_res", (128, 2, 2048), BF16)
            Bwq = Buf()
            Bwkv = Buf()
            dwres = sc.dma_sem("wres")
            sc.dma("sp", dwres, wq[:], wv["mla_w_uq"][:, 0:4, :], reads=[Bw["mla_w_uq"]], writes=[Bwq])
            sc.dma("sp", dwres, wkv[:], wv["mla_w_ukv"][:, 0:2, :], reads=[Bw["mla_w_ukv"]], writes=[Bwkv])
            sqrB = [sbuf(A, "sqrB%d" % i, (128, 512), BF16) for i in range(3)]
            BsqrB = [Buf() for _ in range(3)]
            rsB = sbuf(A, "rsB", (128, 512), F32)
            BrsB = Buf()
            vcnt = [0]

            def wstat(w, Bwt, c0, M, pb, st=False):
                for k in range(16):
                    sc.mm(lambda e: e.matmul(ps[0:M, pb, :], lhsT=w[:, k, c0:c0 + M], rhs=xn[:, k, :],
                                             start=(k == 0), stop=(k == 15)),
                          reads=[Bwt, Bxn[k]], writes=[Bps[pb]], inc=(k == 15))
                    if st and k == 7:
                        step(1)

            def astat(w, Bwt, tg, pb, ncol=512, st=False):
                for k in range(16):
                    sc.mm(lambda e: e.matmul(ps[:, pb, 0:ncol], lhsT=xn[:, k, tg * 128:(tg + 1) * 128],
                                             rhs=w[:, k, 0:ncol], start=(k == 0), stop=(k == 15)),
                          reads=[Bwt, Bxn[k]], writes=[Bps[pb]], inc=(k == 15))
                    if st and k == 7:
                        step(1)

            def do_evac(i, out_ap, pb, Bout, M=128, func=AF.Copy):
                if i % 2 == 0 or func != AF.Copy:
                    sc.op("act", lambda e: e.activation(out=out_ap, in_=ps[0:M, pb, :], func=func),
                          reads=[Bps[pb]], writes=[Bout])
                else:
                    sc.op("dve", lambda e: e.tensor_copy(out=out_ap, in_=ps[0:M, pb, :]),
                          reads=[Bps[pb]], writes=[Bout])

            def normN(src, Bsrc, n, nfeat, gname, dst, Bdst, pb):
                for k in range(n):
                    q = sqrB[k % 3]
                    sc.op("act", lambda e: e.activation(out=q[:], in_=src[:, k, :], func=AF.Square),
                          reads=[Bsrc[k]], writes=[BsqrB[k % 3]])
                    sc.mm(lambda e: e.matmul(ps[:, pb, :], lhsT=ones_bf[:], rhs=q[:], start=(k == 0),
                                             stop=(k == n - 1)),
                          reads=[BsqrB[k % 3], Bc], writes=[Bps[pb]], inc=True)
                rstd_from(pb, nfeat, rsB, BrsB)
                g = cs[gname]
                for k in range(n):
                    sc.op("dve", lambda e: e.scalar_tensor_tensor(
                        out=dst[:, k, :], in0=src[:, k, :], scalar=g[:, k:k + 1], in1=rsB[:],
                        op0=ALU.mult, op1=ALU.mult), reads=[Bsrc[k], BrsB, Bc], writes=[Bdst[k]])

            def rope_a(src_ap, Bsrcs, gcol, rstd, Brstd):
                i = ropec[0] % 2
                ropec[0] += 1
                rf = rope_f[i]
                sc.op("dve", lambda e: e.scalar_tensor_tensor(
                    out=rf[:], in0=src_ap, scalar=cs[gcol][0:64, 1:2], in1=rstd[0:64, :],
                    op0=ALU.mult, op1=ALU.mult), reads=Bsrcs + [Brstd, Bc], writes=[Brope_f[i]])
                return i

            def rope_b(i, pw, dst_ap, Bdst):
                rf = rope_f[i]
                sc.mm(lambda e: e.matmul(ps[0:64, pw, :], lhsT=cs["pswap"][:, :], rhs=rf[:],
                                         start=True, stop=True),
                      reads=[Brope_f[i], Bc], writes=[Bps[pw]], inc=True)
                sc.op("pool", lambda e: e.tensor_tensor(out=rt1[i][:], in0=rf[:], in1=cosT[:], op=ALU.mult),
                      reads=[Brope_f[i], Bcs], writes=[Brt1[i]])
                sc.op("dve", lambda e: e.tensor_tensor(out=rt2[i][:], in0=ps[0:64, pw, :], in1=sinT[:],
                                                       op=ALU.mult),
                      reads=[Bps[pw], Bcs], writes=[Brt2[i]])
                sc.op("pool", lambda e: e.tensor_tensor(out=dst_ap, in0=rt1[i][:], in1=rt2[i][:], op=ALU.add),
                      reads=[Brt1[i], Brt2[i]], writes=[Bdst])

            def prep(t):
                tcols = slice(t * 512, (t + 1) * 512)
                normN(cqT, BcqT, 4, 512.0, "g_mq", cqn, Bcqn, 6)
                yield
                normN(ckvT, BckvT, 2, 256.0, "g_mkv", ckvn, Bckvn, 6)
                yield
                pq, pr, pk = 3, 4, 5
                for h in range(8):
                    hs = h % 2
                    for k in range(4):
                        sc.mm(lambda e: e.matmul(ps[:, pq, :], lhsT=wq[:, k, h * 192:h * 192 + 128],
                                                 rhs=cqn[:, k, :], start=(k == 0), stop=(k == 3)),
                              reads=[Bwq, Bcqn[k]], writes=[Bps[pq]], inc=(k == 3))
                    for k in range(4):
                        sc.mm(lambda e: e.matmul(ps[0:64, pr, :], lhsT=wq[:, k, h * 192 + 128:h * 192 + 192],
                                                 rhs=cqn[:, k, :], start=(k == 0), stop=(k == 3)),
                              reads=[Bwq, Bcqn[k]], writes=[Bps[pr]], inc=(k == 3))
                    for k in range(2):
                        sc.mm(lambda e: e.matmul(ps[:, pk, :], lhsT=wkv[:, k, h * 256:h * 256 + 128],
                                                 rhs=ckvn[:, k, :], start=(k == 0), stop=(k == 1)),
                              reads=[Bwkv, Bckvn[k]], writes=[Bps[pk]], inc=(k == 1))
                    q0, q1, q2 = sqrB[0], sqrB[1], sqrB[2]
                    sc.op("act", lambda e: e.activation(out=q0[:], in_=ps[:, pq, :], func=AF.Square),
                          reads=[Bps[pq]], writes=[BsqrB[0]])
                    sc.op("act", lambda e: e.activation(out=q1[0:64, :], in_=ps[0:64, pr, :], func=AF.Square),
                          reads=[Bps[pr]], writes=[BsqrB[1]])
                    sc.op("act", lambda e: e.activation(out=q2[:], in_=ps[:, pk, :], func=AF.Square),
                          reads=[Bps[pk]], writes=[BsqrB[2]])
                    yield
                    pss, psk = 6, 7
                    sc.mm(lambda e: e.matmul(ps[:, pss, :], lhsT=ones_bf[:], rhs=q0[:], start=True, stop=False),
                          reads=[BsqrB[0], Bc], writes=[Bps[pss]], inc=False)
                    sc.mm(lambda e: e.matmul(ps[:, pss, :], lhsT=ones_bf[0:64, :], rhs=q1[0:64, :],
                                             start=False, stop=True),
                          reads=[BsqrB[1], Bc], writes=[Bps[pss]], inc=True)
                    sc.mm(lambda e: e.matmul(ps[:, psk, :], lhsT=ones_bf[:], rhs=q2[:], start=True, stop=False),
                          reads=[BsqrB[2], Bc], writes=[Bps[psk]], inc=False)
                    sc.mm(lambda e: e.matmul(ps[:, psk, :], lhsT=ones_bf[0:64, :], rhs=sqkpe[:],
                                             start=False, stop=True),
                          reads=[Bsqkpe, Bc], writes=[Bps[psk]], inc=True)
                    rstd_from(pss, 192.0, rq[0], Brq[0])
                    rstd_from(psk, 192.0, rq[1], Brq[1])
                    yield
                    sc.op("dve", lambda e: e.scalar_tensor_tensor(
                        out=qk_st[0][hs][:], in0=ps[:, pq, :], scalar=cs["g_qq"][:, 0:1], in1=rq[0][:],
                        op0=ALU.mult, op1=ALU.mult), reads=[Bps[pq], Brq[0], Bc], writes=[Bqk_st[0][hs]])
                    iq = rope_a(ps[0:64, pr, :], [Bps[pr]], "g_qq", rq[0], Brq[0])
                    sc.op("dve", lambda e: e.scalar_tensor_tensor(
                        out=qk_st[2][hs][:], in0=ps[:, pk, :], scalar=cs["g_qk"][:, 0:1], in1=rq[1][:],
                        op0=ALU.mult, op1=ALU.mult), reads=[Bps[pk], Brq[1], Bc], writes=[Bqk_st[2][hs]])
                    ik = rope_a(kpeT[:, :], [BkpeT], "g_qk", rq[1], Brq[1])
                    sc.dma("sp", dqk[hs], QnT_s[h * 128:(h + 1) * 128, tcols], qk_st[0][hs][:],
                           reads=[Bqk_st[0][hs]], writes=[Bscr["QnT"]])
                    sc.dma("sp", dqk[hs], KnT_s[h * 128:(h + 1) * 128, tcols], qk_st[2][hs][:],
                           reads=[Bqk_st[2][hs]], writes=[Bscr["KnT"]])
                    yield
                    rope_b(iq, 6, qk_st[1][hs][0:64, :], Bqk_st[1][hs])
                    rope_b(ik, 7, qk_st[3][hs][0:64, :], Bqk_st[3][hs])
                    sc.dma("sp", dqk[hs], QrT_s[h * 64:(h + 1) * 64, tcols], qk_st[1][hs][0:64, :],
                           reads=[Bqk_st[1][hs]], writes=[Bscr["QrT"]])
                    sc.dma("sp", dqk[hs], KrT_s[h * 64:(h + 1) * 64, tcols], qk_st[3][hs][0:64, :],
                           reads=[Bqk_st[3][hs]], writes=[Bscr["KrT"]])
                    yield
                wkv_v = wkv[:].rearrange("p k (h two d) -> p k h two d", two=2, d=128)
                for tg in range(4):
                    vs_ = vcntV[0] % 2
                    vcntV[0] += 1
                    for half in range(2):
                        pb = 3 + half
                        for k in range(2):
                            sc.mm(lambda e: e.matmul(ps[:, pb, :].rearrange("p (h d) -> p h d", h=4),
                                                     lhsT=ckvn[:, k, tg * 128:(tg + 1) * 128],
                                                     rhs=wkv_v[:, k, half * 4:(half + 1) * 4, 1, :],
                                                     start=(k == 0), stop=(k == 1)),
                                  reads=[Bwkv, Bckvn[k]], writes=[Bps[pb]], inc=(k == 1))
                        do_evac(half, vsV[vs_][:, half * 512:(half + 1) * 512], pb, BvsV[vs_])
                    r0 = t * 512 + tg * 128
                    sc.dma("sp", dvsV[vs_], V_s[r0:r0 + 128, :], vsV[vs_][:], reads=[BvsV[vs_]], writes=[Bscr["V"]])
                    yield

            prev = [None]

            def step(n=2):
                if prev[0] is not None:
                    for _ in range(n):
                        next(prev[0], None)

            def drain():
                if prev[0] is not None:
                    for _ in prev[0]:
                        pass
                prev[0] = None

            bank_pool[0] = [0, 1, 2]

            def load_h(t):
                if t < NT:
                    sc.dma("act", dxT, xT[:], hT_s.rearrange("(k p) s -> p k s", p=128)[:, :, t * 512:(t + 1) * 512],
                           reads=[Bscr["hT"]], writes=BxT)
            load_h(0)
            for t in range(NT):
                tc0 = t * 512
                tcols = slice(tc0, tc0 + 512)
                norm16(xT, BxT, "g_mix", xn, Bxn, sqr, Bsqr, rs, Brs)
                load_h(t + 1)
                w, Bwt = ws.next(("w_in", 0, 16, 0, 512))

                def fill_gq(z, Bz):
                    for m in range(4):
                        pb = bank()
                        wstat(w, Bwt, m * 128, 128, pb, st=True)
                        do_evac(m, z[:, m, :], pb, Bz)
                        step(1)
                store_z(fill_gq, gqT_s.rearrange("(c p) s -> p c s", p=128)[:, :, tcols], Bscr["gqT"])
                w, Bwt = ws.next(("w_in", 0, 16, 512, 512))
                store_z(fill_gq, gkT_s.rearrange("(c p) s -> p c s", p=128)[:, :, tcols], Bscr["gkT"])

                def fill_tm(z, Bz, func=AF.Copy):
                    for tg in range(4):
                        pb = bank()
                        astat(w, Bwt, tg, pb, st=True)
                        do_evac(tg, z[:, tg, :], pb, Bz, func=func)
                        step(1)
                store_z(fill_tm, gk_s.rearrange("(g p) c -> p g c", p=128)[:, t * 4:(t + 1) * 4, :], Bscr["gk"])
                w0, Bw0 = ws.next(("w_in", 0, 16, 1024, 512))
                w1, Bw1 = ws.next(("w_in", 0, 16, 1536, 512))
                for tg in range(4):
                    vs_ = vcnt[0] % 2
                    vcnt[0] += 1
                    for half in range(2):
                        pb = bank()
                        astat((w0, w1)[half], (Bw0, Bw1)[half], tg, pb, st=True)
                        do_evac(tg + half, vst[vs_][:, half * 512:(half + 1) * 512], pb, Bvst[vs_])
                        step(1)
                    r0 = t * 512 + tg * 128
                    sc.dma("sp", dvst[vs_], gv_s[r0:r0 + 128, :], vst[vs_][:], reads=[Bvst[vs_]], writes=[Bscr["gv"]])
                for half in range(2):
                    w, Bwt = ws.next(("w_in", 0, 16, 2048 + half * 512, 512))
                    store_z(lambda z, Bz: fill_tm(z, Bz, AF.Silu),
                            gsr_s.rearrange("(g p) c -> p g c", p=128)[:, t * 4:(t + 1) * 4, half * 512:(half + 1) * 512],
                            Bscr["gsr"])
                w, Bwt = ws.next(("w_in", 0, 16, 3072, 32))
                i = zcnt[0] % 2
                zcnt[0] += 1
                pb = bank()
                wstat(w, Bwt, 0, 32, pb)
                do_evac(0, zs[i][0:32, 0, :], pb, Bzs[i], M=32)
                sc.dma("sp", dzs[i], gaT_s[:, tcols], zs[i][0:32, 0, :], reads=[Bzs[i]], writes=[Bscr["gaT"]])
                drain()
                w, Bwt = ws.next(("w_in", 0, 16, 3104, 512))
                for m in range(4):
                    pb = bank()
                    wstat(w, Bwt, m * 128, 128, pb)
                    do_evac(m, cqT[:, m, :], pb, BcqT[m])
                w, Bwt = ws.next(("w_in", 0, 16, 3616, 320))
                for m in range(2):
                    pb = bank()
                    wstat(w, Bwt, m * 128, 128, pb)
                    do_evac(m, ckvT[:, m, :], pb, BckvT[m])
                pb = bank()
                wstat(w, Bwt, 256, 64, pb)
                do_evac(0, kpeT[:, :], pb, BkpeT, M=64)
                sc.op("act", lambda e: e.activation(out=sqkpe[:], in_=kpeT[:], func=AF.Square),
                      reads=[BkpeT], writes=[Bsqkpe])
                sc.dma("act", dcs, cosT[:], cos_d[:, tcols], writes=[Bcs])
                sc.dma("act", dcs, sinT[:], sin_d[:, tcols], writes=[Bcs])
                prev[0] = prep(t)
            drain()
            bank_pool[0] = list(range(8))
            sc.barrier()

        if stop == "A":
            print("instructions", sc.n_ins, "waits", sc.n_wait)
            return nc

        with ExitStack() as Bk:
            NQB = S // 512
            dld = sc.dma_sem("p3const")
            Bk3 = Buf("p3const")

            def cload(name, ap, shape, dt=F32):
                t_ = sbuf(Bk, name, shape, dt)
                sc.dma("sp", dld, t_[:], ap, writes=[Bk3])
                return t_
            gain_bc = cload("gain_bc", gla_gain_d, (128, 1024))
            wa2c = [cload("wa2c%d" % i, wa2c_d[i], (32, 512)) for i in range(2)]
            ba = [cload("ba%d" % i, ba_d[i], (1, 512)) for i in range(2)]
            tri = {n: cload("tri" + n, tri_d[n], (128, 128)) for n in tri_d}
            mask = [cload("mask%d" % i, mask_d[i], (128, 512)) for i in range(2)]
            Sst = [[sbuf(Bk, "Sst%d%d" % (d_, h), (128, 256), F32) for h in range(4)] for d_ in range(2)]
            Sbf = [[sbuf(Bk, "Sbf%d%d" % (d_, h), (128, 256), BF16) for h in range(4)] for d_ in range(2)]
            BS = [[Buf() for h in range(4)] for d_ in range(2)]
            BSb = [[Buf() for h in range(4)] for d_ in range(2)]
            for d_ in range(2):
                for h in range(4):
                    sc.op("pool", lambda e: e.memset(Sst[d_][h][:], 0.0), writes=[BS[d_][h]])
                    sc.op("pool", lambda e: e.memset(Sbf[d_][h][:], 0.0), writes=[BSb[d_][h]])
            gin = {}
            Bgin = {}
            for nm, shp, dt in (("qT", (128, 512), F32), ("kT", (128, 512), F32), ("k", (128, 512), F32),
                                ("v", (128, 1024), BF16), ("ga", (32, 128), F32), ("of", (128, 1024), F32),
                                ("sr", (128, 1024), F32)):
                gin[nm] = [sbuf(Bk, "gin_%s%d" % (nm, i), shp, dt) for i in range(2)]
                Bgin[nm] = [Buf() for i in range(2)]
            dgl = [sc.dma_sem("gl%d" % i) for i in range(2)]
            tmp = {}
            Bt = {}
            for nm, shp, dt in (("la", (128, 512), F32), ("E1T", (128, 512), F32), ("E2T", (128, 512), F32),
                                ("E3", (128, 512), F32), ("qtT", (128, 512), BF16), ("ktT", (128, 512), BF16),
                                ("ks", (128, 512), BF16), ("ATm", (128, 512), BF16), ("o_sb", (128, 1024), F32),
                                ("gs", (128, 1024), F32), ("y_sb", (128, 1024), F32), ("yT", (128, 8, 128), BF16),
                                ("junk", (128, 256), F32), ("ss", (128, 4), F32)):
                tmp[nm] = sbuf(Bk, "t_" + nm, shp, dt)
                Bt[nm] = Buf("t_" + nm)
            dgs = sc.dma_sem("glstore")
            NU = 2 * NCH

            def gla_load(u):
                if u >= NU:
                    return
                dr = 0 if u < NCH else 1
                n = u if dr == 0 else (NU - 1 - u)
                sl = u % 2
                c0 = n * 128
                cc = slice(c0, c0 + 128)
                d = dgl[sl]
                sc.dma("sp", d, gin["qT"][sl][:].rearrange("p (h i) -> p h i", h=4),
                       gqT_s.rearrange("(h p) s -> p h s", p=128)[:, :, cc], reads=[Bscr["gqT"]], writes=[Bgin["qT"][sl]])
                sc.dma("sp", d, gin["kT"][sl][:].rearrange("p (h i) -> p h i", h=4),
                       gkT_s.rearrange("(h p) s -> p h s", p=128)[:, :, cc], reads=[Bscr["gkT"]], writes=[Bgin["kT"][sl]])
                sc.dma("sp", d, gin["k"][sl][:], gk_s[cc, :], reads=[Bscr["gk"]], writes=[Bgin["k"][sl]])
                sc.dma("sp", d, gin["v"][sl][:], gv_s[cc, :], reads=[Bscr["gv"]], writes=[Bgin["v"][sl]])
                sc.dma("sp", d, gin["ga"][sl][:], gaT_s[:, cc], reads=[Bscr["gaT"]], writes=[Bgin["ga"][sl]])

            def gla_load_of(u):
                if u >= NU or u < NCH:
                    return
                n = NU - 1 - u
                sl = u % 2
                cc = slice(n * 128, n * 128 + 128)
                d = dgl[sl]
                sc.dma("sp", d, gin["of"][sl][:], of_s[cc, :], reads=[Bscr["of"]], writes=[Bgin["of"][sl]])
                sc.dma("sp", d, gin["sr"][sl][:], gsr_s[cc, :], reads=[Bscr["gsr"]], writes=[Bgin["sr"][sl]])

            def gla_unit(u):
                dr = 0 if u < NCH else 1
                n = u if dr == 0 else (NU - 1 - u)
                sl = u % 2
                c0 = n * 128
                cc = slice(c0, c0 + 128)
                Ln = tri["LnF" if dr == 0 else "LnB"]
                Un = tri["UnF" if dr == 0 else "UnB"]
                la, E1T, E2T, E3 = tmp["la"], tmp["E1T"], tmp["E2T"], tmp["E3"]
                qtT, ktT, ks, ATm = tmp["qtT"], tmp["ktT"], tmp["ks"], tmp["ATm"]
                o_sb, gs, y_sb, yT = tmp["o_sb"], tmp["gs"], tmp["y_sb"], tmp["yT"]
                qT_in, kT_in, k_in, v_in, ga_in = (gin[x][sl] for x in ("qT", "kT", "k", "v", "ga"))
                sc.mm(lambda e: e.matmul(ps[:, 0, :], lhsT=ga_in[0:32, :], rhs=wa2c[dr][0:32, :], start=True, stop=False),
                      reads=[Bgin["ga"][sl], Bk3], writes=[Bps[0]], inc=False)
                sc.mm(lambda e: e.matmul(ps[:, 0, :], lhsT=ones_f[0:1, :], rhs=ba[dr][0:1, :], start=False, stop=True),
                      reads=[Bc, Bk3], writes=[Bps[0]], inc=True)
                sc.op("act", lambda e: e.activation(out=la[:], in_=ps[:, 0, :], func=AF.Exp, scale=-1.0),
                      reads=[Bps[0]], writes=[Bt["la"]])
                sc.op("act", lambda e: e.activation(out=la[:], in_=la[:], func=AF.Ln, bias=1.0),
                      reads=[Bt["la"]], writes=[Bt["la"]])
                yield
                for h in range(4):
                    sc.mm(lambda e: e.matmul(ps[:, 1, h * 128:(h + 1) * 128], lhsT=la[:, h * 128:(h + 1) * 128],
                                             rhs=Ln[:], start=True, stop=True),
                          reads=[Bt["la"], Bk3], writes=[Bps[1]], inc=(h == 3))
                sc.mm(lambda e: e.matmul(ps[:, 0, :], lhsT=Un[:], rhs=la[:], start=True, stop=True),
                      reads=[Bt["la"], Bk3], writes=[Bps[0]], inc=True)
                sc.op("act", lambda e: e.activation(out=E1T[:], in_=ps[:, 1, :], func=AF.Exp),
                      reads=[Bps[1]], writes=[Bt["E1T"]])
                sc.op("act", lambda e: e.activation(out=E2T[:], in_=ps[:, 1, :], func=AF.Exp, scale=-1.0),
                      reads=[Bps[1]], writes=[Bt["E2T"]])
                sc.op("act", lambda e: e.activation(out=E3[:], in_=ps[:, 0, :], func=AF.Exp),
                      reads=[Bps[0]], writes=[Bt["E3"]])
                sc.op("dve", lambda e: e.scalar_tensor_tensor(out=qtT[:], in0=qT_in[:], scalar=float(128.0 ** -0.5),
                                                              in1=E1T[:], op0=ALU.mult, op1=ALU.mult),
                      reads=[Bgin["qT"][sl], Bt["E1T"]], writes=[Bt["qtT"]])
                sc.op("pool", lambda e: e.tensor_tensor(out=ktT[:], in0=kT_in[:], in1=E2T[:], op=ALU.mult),
                      reads=[Bgin["kT"][sl], Bt["E2T"]], writes=[Bt["ktT"]])
                sc.op("pool", lambda e: e.tensor_tensor(out=ks[:], in0=k_in[:], in1=E3[:], op=ALU.mult),
                      reads=[Bgin["k"][sl], Bt["E3"]], writes=[Bt["ks"]])
                yield
                for h in range(4):
                    hs = slice(h * 128, (h + 1) * 128)
                    sc.mm(lambda e: e.matmul(ps[:, 1, hs], lhsT=ktT[:, hs], rhs=qtT[:, hs], start=True, stop=True),
                          reads=[Bt["ktT"], Bt["qtT"]], writes=[Bps[1]], inc=(h == 3))
                sc.op("dve", lambda e: e.tensor_tensor(out=ATm[:], in0=ps[:, 1, :], in1=mask[dr][:], op=ALU.mult),
                      reads=[Bps[1], Bk3], writes=[Bt["ATm"]])
                for h in range(4):
                    hs = slice(h * 128, (h + 1) * 128)
                    cs_ = slice((h % 2) * 256, (h % 2) * 256 + 256)
                    bk = h // 2
                    if h % 2 == 0 and bk == 1:
                        pass
                    sc.mm(lambda e: e.matmul(ps[:, bk, cs_], lhsT=ks[:, hs], rhs=v_in[:, h * 256:(h + 1) * 256],
                                             start=True, stop=True),
                          reads=[Bt["ks"], Bgin["v"][sl]], writes=[Bps[bk]], inc=(h % 2 == 1)) if bk == 0 else None
                yield
                for h in range(4):
                    hs = slice(h * 128, (h + 1) * 128)
                    cs_ = slice((h % 2) * 256, (h % 2) * 256 + 256)
                    bk = 2 + h // 2
                    sc.mm(lambda e: e.matmul(ps[:, bk, cs_], lhsT=ATm[:, hs], rhs=v_in[:, h * 256:(h + 1) * 256],
                                             start=True, stop=False),
                          reads=[Bt["ATm"], Bgin["v"][sl]], writes=[Bps[bk]], inc=False)
                    sc.mm(lambda e: e.matmul(ps[:, bk, cs_], lhsT=qtT[:, hs], rhs=Sbf[dr][h][:], start=False, stop=True),
                          reads=[Bt["qtT"], BSb[dr][h]], writes=[Bps[bk]], inc=(h % 2 == 1))
                for h in range(2, 4):
                    hs = slice(h * 128, (h + 1) * 128)
                    cs_ = slice((h % 2) * 256, (h % 2) * 256 + 256)
                    sc.mm(lambda e: e.matmul(ps[:, 1, cs_], lhsT=ks[:, hs], rhs=v_in[:, h * 256:(h + 1) * 256],
                                             start=True, stop=True),
                          reads=[Bt["ks"], Bgin["v"][sl]], writes=[Bps[1]], inc=(h == 3))
                lastc = 127 if dr == 0 else 0
                for h in range(4):
                    cs_ = slice((h % 2) * 256, (h % 2) * 256 + 256)
                    bk = h // 2
                    dec = E1T[:, h * 128 + lastc:h * 128 + lastc + 1]
                    sc.op("dve", lambda e: e.scalar_tensor_tensor(out=Sst[dr][h][:], in0=Sst[dr][h][:], scalar=dec,
                                                                  in1=ps[:, bk, cs_], op0=ALU.mult, op1=ALU.add),
                          reads=[BS[dr][h], Bt["E1T"], Bps[bk]], writes=[BS[dr][h]])
                    sc.op("act", lambda e: e.activation(out=Sbf[dr][h][:], in_=Sst[dr][h][:], func=AF.Copy),
                          reads=[BS[dr][h]], writes=[BSb[dr][h]])
                yield
                if dr == 0:
                    sc.op("act", lambda e: e.activation(out=o_sb[:, 0:512], in_=ps[:, 2, :], func=AF.Copy),
                          reads=[Bps[2]], writes=[Bt["o_sb"]])
                    sc.op("dve", lambda e: e.tensor_copy(out=o_sb[:, 512:1024], in_=ps[:, 3, :]),
                          reads=[Bps[3], Bt["o_sb"]], writes=[Bt["o_sb"]])
                    sc.dma("pool", dgs, of_s[cc, :], o_sb[:], reads=[Bt["o_sb"]], writes=[Bscr["of"]])
                    return
                of_in, sr_in = gin["of"][sl], gin["sr"][sl]
                sc.op("dve", lambda e: e.tensor_tensor(out=o_sb[:, 0:512], in0=ps[:, 2, :], in1=of_in[:, 0:512], op=ALU.add),
                      reads=[Bps[2], Bgin["of"][sl]], writes=[Bt["o_sb"]])
                sc.op("dve", lambda e: e.tensor_tensor(out=o_sb[:, 512:1024], in0=ps[:, 3, :], in1=of_in[:, 512:1024], op=ALU.add),
                      reads=[Bps[3], Bgin["of"][sl], Bt["o_sb"]], writes=[Bt["o_sb"]])
                sc.op("pool", lambda e: e.tensor_tensor(out=gs[:], in0=gain_bc[:], in1=sr_in[:], op=ALU.mult),
                      reads=[Bk3, Bgin["sr"][sl]], writes=[Bt["gs"]])
                for h in range(4):
                    sc.op("act", lambda e: e.activation(out=tmp["junk"][:], in_=o_sb[:, h * 256:(h + 1) * 256],
                                                        func=AF.Square, accum_out=tmp["ss"][:, h:h + 1]),
                          reads=[Bt["o_sb"]], writes=[Bt["junk"], Bt["ss"]])
                sc.op("act", lambda e: e.activation(out=tmp["ss"][:], in_=tmp["ss"][:], func=AF.Ln, scale=1.0 / 256, bias=EPS),
                      reads=[Bt["ss"]], writes=[Bt["ss"]])
                sc.op("act", lambda e: e.activation(out=tmp["ss"][:], in_=tmp["ss"][:], func=AF.Exp, scale=-0.5),
                      reads=[Bt["ss"]], writes=[Bt["ss"]])
                for h in range(4):
                    hv = slice(h * 256, (h + 1) * 256)
                    sc.op("dve", lambda e: e.scalar_tensor_tensor(out=y_sb[:, hv], in0=o_sb[:, hv], scalar=tmp["ss"][:, h:h + 1],
                                                                  in1=gs[:, hv], op0=ALU.mult, op1=ALU.mult),
                          reads=[Bt["o_sb"], Bt["ss"], Bt["gs"]], writes=[Bt["y_sb"]])
                yield
                for c in range(8):
                    bk = 2 + c // 4
                    sc.mm(lambda e: e.transpose(out=ps[:, bk, (c % 4) * 128:(c % 4 + 1) * 128],
                                                in_=y_sb[:, c * 128:(c + 1) * 128], identity=ident[:]),
                          reads=[Bt["y_sb"], Bc], writes=[Bps[bk]], inc=(c % 4 == 3))
                sc.op("act", lambda e: e.activation(out=yT[:, 0:4, :], in_=ps[:, 2, :].rearrange("p (c n) -> p c n", c=4),
                                                    func=AF.Copy), reads=[Bps[2]], writes=[Bt["yT"]])
                sc.op("dve", lambda e: e.tensor_copy(out=yT[:, 4:8, :], in_=ps[:, 3, :].rearrange("p (c n) -> p c n", c=4)),
                      reads=[Bps[3], Bt["yT"]], writes=[Bt["yT"]])
                sc.dma("pool", dgs, ymT_s.rearrange("(c p) s -> p c s", p=128)[:, 0:8, cc], yT[:],
                       reads=[Bt["yT"]], writes=[Bscr["ymT"]])

            Kn = [sbuf(Bk, "Kn%d" % i, (128, S), BF16) for i in range(2)]
            Kr = [sbuf(Bk, "Kr%d" % i, (128, S), BF16) for i in range(2)]
            Vh = [sbuf(Bk, "Vh%d" % i, (128, NCH, 128), BF16) for i in range(2)]
            BKV = [Buf() for i in range(2)]
            dkv = [sc.dma_sem("kv%d" % i) for i in range(2)]
            Qn = [sbuf(Bk, "Qn%d" % i, (128, 512), BF16) for i in range(2)]
            Qr = [sbuf(Bk, "Qr%d" % i, (128, 512), BF16) for i in range(2)]
            BQ = [Buf() for i in range(2)]
            for i in range(2):
                sc.op("pool", lambda e: e.memset(Kr[i][64:128, :], 0.0), writes=[BKV[i]])
                sc.op("pool", lambda e: e.memset(Qr[i][64:128, :], 0.0), writes=[BQ[i]])
            dq = [sc.dma_sem("q%d" % i) for i in range(2)]
            PT = [sbuf(Bk, "PT%d" % i, (128, 512), BF16) for i in range(4)]
            BPT = [Buf() for i in range(4)]
            rsum = sbuf(Bk, "rsum", (128, 512), F32)
            Brsum = Buf()
            pacc = sbuf(Bk, "pacc", (128, 512), F32)
            Bpacc = Buf()
            yo = [sbuf(Bk, "yo%d" % i, (128, 512), BF16) for i in range(2)]
            Byo = [Buf() for i in range(2)]
            dyo = [sc.dma_sem("yo%d" % i) for i in range(2)]
            NMU = 8 * NQB
            mscale = float(192.0 ** -0.5)

            def kv_load(h):
                if h >= 8:
                    return
                s_ = h % 2
                sc.dma("sp", dkv[s_], Kn[s_][:], KnT_s[h * 128:(h + 1) * 128, :], reads=[Bscr["KnT"]], writes=[BKV[s_]])
                sc.dma("sp", dkv[s_], Kr[s_][0:64, :], KrT_s[h * 64:(h + 1) * 64, :], reads=[Bscr["KrT"]], writes=[BKV[s_]])
                sc.dma("sp", dkv[s_], Vh[s_][:], V_s.rearrange("(g p) c -> p g c", p=128)[:, :, h * 128:(h + 1) * 128],
                       reads=[Bscr["V"]], writes=[BKV[s_]])

            def q_load(u):
                if u >= NMU:
                    return
                h, qb = u // NQB, u % NQB
                s_ = u % 2
                sc.dma("sp", dq[s_], Qn[s_][:], QnT_s[h * 128:(h + 1) * 128, qb * 512:(qb + 1) * 512],
                       reads=[Bscr["QnT"]], writes=[BQ[s_]])
                sc.dma("sp", dq[s_], Qr[s_][0:64, :], QrT_s[h * 64:(h + 1) * 64, qb * 512:(qb + 1) * 512],
                       reads=[Bscr["QrT"]], writes=[BQ[s_]])

            def mla_unit(u, gen):
                h, qb = u // NQB, u % NQB
                ks_ = h % 2
                qs = u % 2
                if qb == 0:
                    kv_load(h + 1)
                q_load(u + 1)
                step = max(1, NCH // 8)

                def score(kc):
                    bk = 4 + kc % 2
                    kcs = slice(kc * 128, (kc + 1) * 128)
                    sc.mm(lambda e: e.matmul(ps[:, bk, :], lhsT=Kn[ks_][:, kcs], rhs=Qn[qs][:], start=True, stop=False),
                          reads=[BKV[ks_], BQ[qs]], writes=[Bps[bk]], inc=False)
                    sc.mm(lambda e: e.matmul(ps[:, bk, :], lhsT=Kr[ks_][:, kcs], rhs=Qr[qs][:], start=False, stop=True),
                          reads=[BKV[ks_], BQ[qs]], writes=[Bps[bk]], inc=True)
                    p_ = kc % 4
                    sc.op("act", lambda e: e.activation(out=PT[p_][:], in_=ps[:, bk, :], func=AF.Exp, scale=mscale),
                          reads=[Bps[bk]], writes=[BPT[p_]])
                score(0)
                if NCH > 1:
                    score(1)
                for kc in range(NCH):
                    p_ = kc % 4
                    if kc + 2 < NCH:
                        score(kc + 2)
                    sc.mm(lambda e: e.matmul(ps[:, 6, :], lhsT=Vh[ks_][:, kc, :], rhs=PT[p_][:], start=(kc == 0),
                                             stop=(kc == NCH - 1)),
                          reads=[BKV[ks_], BPT[p_]], writes=[Bps[6]], inc=(kc == NCH - 1))
                    sc.mm(lambda e: e.matmul(ps[:, 7, :], lhsT=ones_bf[:], rhs=PT[p_][:], start=(kc == 0),
                                             stop=(kc == NCH - 1)),
                          reads=[Bc, BPT[p_]], writes=[Bps[7]], inc=True)
                    if gen is not None and kc % step == step - 1:
                        next(gen, None)
                sc.op("act", lambda e: e.activation(out=rsum[:], in_=ps[:, 7, :], func=AF.Ln), reads=[Bps[7]], writes=[Brsum])
                sc.op("act", lambda e: e.activation(out=rsum[:], in_=rsum[:], func=AF.Exp, scale=-1.0), reads=[Brsum], writes=[Brsum])
                ys = u % 2
                sc.op("dve", lambda e: e.tensor_tensor(out=yo[ys][:], in0=ps[:, 6, :], in1=rsum[:], op=ALU.mult),
                      reads=[Bps[6], Brsum], writes=[Byo[ys]])
                sc.dma("pool", dyo[ys], ymT_s[1024 + h * 128:1024 + (h + 1) * 128, qb * 512:(qb + 1) * 512], yo[ys][:],
                       reads=[Byo[ys]], writes=[Bscr["ymT"]])

            gla_load(0)
            kv_load(0)
            q_load(0)
            nun = max(NU, NMU)
            for u in range(nun):
                gen = None
                if u < NU:
                    gla_load(u + 1)
                    gen = gla_unit(u)
                if u < NMU:
                    mla_unit(u, gen)
                if gen is not None:
                    for _ in gen:
                        pass
                gla_load_of(u + 1)
            sc.barrier()
        if stop == "B":
            print("instructions", sc.n_ins, "waits", sc.n_wait)
            return nc

        h2T_s = dscr("s_h2T", (D, S), F32)
        Bh2 = Buf("scr_h2T", dram=True)
        xscale = float(512.0 ** -0.5)
        with ExitStack() as C:
            ws = WStream(C)
            ws.plan += [("xattn_wk", 0, 16, cb * 512, 512) for cb in range(4)]
            ws.plan += [("xattn_wv", 0, 16, cb * 512, 512) for cb in range(4)]
            for t in range(NT):
                ws.plan += [("w_out", 0, 16, cb * 512, 512) for cb in range(4)]
                ws.plan += [("xattn_wq", 0, 16, cb * 512, 512) for cb in range(4)]
                ws.plan += [("xattn_wo", 0, 16, cb * 512, 512) for cb in range(4)]
            kmn = sbuf(C, "kmn", (128, 16, 256), BF16)
            Bkmn = [Buf() for _ in range(4)]
            vm = sbuf(C, "vm", (128, 2, 2048), BF16)
            Bvm = Buf()
            sqr = [sbuf(C, "sqr4_%d" % i, (128, 512), BF16) for i in range(4)]
            Bsqr = [Buf() for _ in range(4)]
            rs = sbuf(C, "rs4", (128, 512), F32)
            Brs = Buf()
            with ExitStack() as M:
                memT = sbuf(M, "memT", (128, 16, 256), F32)
                BmemT = [Buf() for _ in range(16)]
                memn = sbuf(M, "memn", (128, 16, 256), BF16)
                Bmemn = [Buf() for _ in range(16)]
                kmT = sbuf(M, "kmT", (128, 16, 256), F32)
                BkmT = [Buf() for _ in range(16)]
                stg = [sbuf(M, "mstg%d" % i, (128, 2048), F32) for i in range(2)]
                Bstg = [Buf() for _ in range(2)]
                dstg = [sc.dma_sem("mstg%d" % i) for i in range(2)]
                load_T(lambda g: mem_d[g * 128:(g + 1) * 128, :], 2, memT, BmemT, stg, Bstg, dstg, 0)
                norm16(memT, BmemT, "g_mem", memn, Bmemn, sqr, Bsqr, rs, Brs, N=256)
                for cb in range(4):
                    w, Bwt = ws.next(("xattn_wk", 0, 16, cb * 512, 512))
                    for m in range(4):
                        ob = cb * 4 + m
                        pb = bank()
                        for k in range(16):
                            sc.mm(lambda e: e.matmul(ps[:, pb, 0:256], lhsT=w[:, k, m * 128:(m + 1) * 128],
                                                     rhs=memn[:, k, :], start=(k == 0), stop=(k == 15)),
                                  reads=[Bwt, Bmemn[k]], writes=[Bps[pb]], inc=(k == 15))
                        sc.op("act", lambda e: e.activation(out=kmT[:, ob, :], in_=ps[:, pb, 0:256], func=AF.Copy),
                              reads=[Bps[pb]], writes=[BkmT[ob]])
                    pb = bank()
                    for c in range(4):
                        q = sqr[c]
                        sc.op("act", lambda e: e.activation(out=q[:, 0:256], in_=kmT[:, cb * 4 + c, :], func=AF.Square),
                              reads=[BkmT[cb * 4 + c]], writes=[Bsqr[c]])
                        sc.mm(lambda e: e.matmul(ps[:, pb, 0:256], lhsT=ones_bf[:], rhs=q[:, 0:256], start=(c == 0),
                                                 stop=(c == 3)), reads=[Bsqr[c], Bc], writes=[Bps[pb]], inc=True)
                    sc.op("act", lambda e: e.activation(out=rs[:, 0:256], in_=ps[:, pb, 0:256], func=AF.Ln,
                                                        scale=1.0 / 512, bias=EPS), reads=[Bps[pb]], writes=[Brs])
                    sc.op("act", lambda e: e.activation(out=rs[:, 0:256], in_=rs[:, 0:256], func=AF.Exp, scale=-0.5),
                          reads=[Brs], writes=[Brs])
                    for c in range(4):
                        sc.op("dve", lambda e: e.scalar_tensor_tensor(
                            out=kmn[:, cb * 4 + c, :], in0=kmT[:, cb * 4 + c, :], scalar=cs["g_xk"][:, c:c + 1],
                            in1=rs[:, 0:256], op0=ALU.mult, op1=ALU.mult),
                            reads=[BkmT[cb * 4 + c], Brs, Bc], writes=[Bkmn[cb]])
                for cb in range(4):
                    w, Bwt = ws.next(("xattn_wv", 0, 16, cb * 512, 512))
                    for mg in range(2):
                        pb = bank()
                        for k in range(16):
                            sc.mm(lambda e: e.matmul(ps[:, pb, :], lhsT=memn[:, k, mg * 128:(mg + 1) * 128],
                                                     rhs=w[:, k, :], start=(k == 0), stop=(k == 15)),
                                  reads=[Bwt, Bmemn[k]], writes=[Bps[pb]], inc=(k == 15))
                        sc.op("act", lambda e: e.activation(out=vm[:, mg, cb * 512:(cb + 1) * 512], in_=ps[:, pb, :],
                                                            func=AF.Copy), reads=[Bps[pb]], writes=[Bvm])
                sc.barrier()
            xT = sbuf(C, "xT4", (128, 16, 512), F32)
            BxT = [Buf("xT%d" % k) for k in range(16)]
            dxT = sc.dma_sem("xT4load")
            ym = sbuf(C, "ym", (128, 16, 512), BF16)
            Bym = Buf()
            dym = sc.dma_sem("ymload")
            xn = sbuf(C, "xn4", (128, 16, 512), BF16)
            Bxn = [Buf("xn%d" % k) for k in range(16)]
            ox = sbuf(C, "ox", (128, 16, 512), BF16)
            Box = [Buf() for _ in range(16)]
            qhs = [sbuf(C, "qh%d" % i, (128, 4, 512), F32) for i in range(2)]
            Bqhs = [[Buf() for _ in range(4)] for i in range(2)]
            qn = sbuf(C, "qn", (128, 4, 512), BF16)
            Bqn = [Buf() for _ in range(4)]
            PTx = [sbuf(C, "PTx%d" % i, (128, 512), BF16) for i in range(2)]
            BPTx = [Buf() for _ in range(2)]
            rsx = sbuf(C, "rsx", (128, 512), F32)
            Brsx = Buf()
            dh2 = sc.dma_sem("h2store")
            def load_ym(t):
                if t < NT:
                    sc.dma("act", dym, ym[:], ymT_s.rearrange("(k p) s -> p k s", p=128)[:, :, t * 512:(t + 1) * 512],
                           reads=[Bscr["ymT"]], writes=[Bym])
            load_ym(0)
            for t in range(NT):
                tcols = slice(t * 512, (t + 1) * 512)
                sc.dma("act", dxT, xT[:], hT_s.rearrange("(k p) s -> p k s", p=128)[:, :, tcols],
                       reads=[Bscr["hT"]], writes=BxT)

                def proj_add(wname, src, Bsrc_of, store=False):
                    for cb in range(4):
                        w, Bwt = ws.next((wname, 0, 16, cb * 512, 512))
                        for m in range(4):
                            ob = cb * 4 + m
                            pb = bank()
                            for k in range(16):
                                sc.mm(lambda e: e.matmul(ps[:, pb, :], lhsT=w[:, k, m * 128:(m + 1) * 128],
                                                         rhs=src[:, k, :], start=(k == 0), stop=(k == 15)),
                                      reads=[Bwt, Bsrc_of(k)], writes=[Bps[pb]], inc=(k == 15))
                            sc.op("dve", lambda e: e.tensor_tensor(out=xT[:, ob, :], in0=ps[:, pb, :], in1=xT[:, ob, :],
                                                                   op=ALU.add),
                                  reads=[Bps[pb], BxT[ob]], writes=[BxT[ob]])
                        if store:
                            sc.dma("sp", dh2, h2T_s.rearrange("(k p) s -> p k s", p=128)[:, cb * 4:(cb + 1) * 4, tcols],
                                   xT[:, cb * 4:(cb + 1) * 4, :], reads=BxT[cb * 4:(cb + 1) * 4], writes=[Bh2])
                proj_add("w_out", ym, lambda k: Bym)
                load_ym(t + 1)
                if stop == "C1":
                    sc.dma("sp", dh2, h2T_s.rearrange("(k p) s -> p k s", p=128)[:, :, tcols], xT[:],
                           reads=BxT, writes=[Bh2])
                    continue
                norm16(xT, BxT, "g_xn", xn, Bxn, sqr, Bsqr, rs, Brs)
                def qproj(h):
                    w, Bwt = ws.next(("xattn_wq", 0, 16, h * 512, 512))
                    qh_ = qhs[h % 2]
                    Bqh_ = Bqhs[h % 2]
                    for m in range(4):
                        pb = bank()
                        for k in range(16):
                            sc.mm(lambda e: e.matmul(ps[:, pb, :], lhsT=w[:, k, m * 128:(m + 1) * 128], rhs=xn[:, k, :],
                                                     start=(k == 0), stop=(k == 15)),
                                  reads=[Bwt, Bxn[k]], writes=[Bps[pb]], inc=(k == 15))
                        if m % 2 == 0:
                            sc.op("act", lambda e: e.activation(out=qh_[:, m, :], in_=ps[:, pb, :], func=AF.Copy),
                                  reads=[Bps[pb]], writes=[Bqh_[m]])
                        else:
                            sc.op("dve", lambda e: e.tensor_copy(out=qh_[:, m, :], in_=ps[:, pb, :]),
                                  reads=[Bps[pb]], writes=[Bqh_[m]])
                qproj(0)
                for h in range(4):
                    if h + 1 < 4:
                        qproj(h + 1)
                    qh = qhs[h % 2]
                    Bqh = Bqhs[h % 2]
                    pb = bank()
                    for c in range(4):
                        q = sqr[c]
                        sc.op("act", lambda e: e.activation(out=q[:], in_=qh[:, c, :], func=AF.Square),
                              reads=[Bqh[c]], writes=[Bsqr[c]])
                        sc.mm(lambda e: e.matmul(ps[:, pb, :], lhsT=ones_bf[:], rhs=q[:], start=(c == 0), stop=(c == 3)),
                              reads=[Bsqr[c], Bc], writes=[Bps[pb]], inc=True)
                    rstd_from(pb, 512.0, rs, Brs)
                    for c in range(4):
                        sc.op("dve", lambda e: e.scalar_tensor_tensor(
                            out=qn[:, c, :], in0=qh[:, c, :], scalar=cs["g_xq"][:, c:c + 1], in1=rs[:],
                            op0=ALU.mult, op1=ALU.mult), reads=[Bqh[c], Brs, Bc], writes=[Bqn[c]])
                    for mc in range(2):
                        pb = bank()
                        for c in range(4):
                            sc.mm(lambda e: e.matmul(ps[:, pb, :], lhsT=kmn[:, h * 4 + c, mc * 128:(mc + 1) * 128],
                                                     rhs=qn[:, c, :], start=(c == 0), stop=(c == 3)),
                                  reads=[Bkmn[h], Bqn[c]], writes=[Bps[pb]], inc=(c == 3))
                        sc.op("act", lambda e: e.activation(out=PTx[mc][:], in_=ps[:, pb, :], func=AF.Exp, scale=xscale),
                              reads=[Bps[pb]], writes=[BPTx[mc]])
                    pb = bank()
                    for mc in range(2):
                        sc.mm(lambda e: e.matmul(ps[:, pb, :], lhsT=ones_bf[:], rhs=PTx[mc][:], start=(mc == 0), stop=(mc == 1)),
                              reads=[BPTx[mc], Bc], writes=[Bps[pb]], inc=(mc == 1))
                    sc.op("act", lambda e: e.activation(out=rsx[:], in_=ps[:, pb, :], func=AF.Ln), reads=[Bps[pb]], writes=[Brsx])
                    sc.op("act", lambda e: e.activation(out=rsx[:], in_=rsx[:], func=AF.Exp, scale=-1.0), reads=[Brsx], writes=[Brsx])
                    for c2 in range(4):
                        pb = bank()
                        for mc in range(2):
                            sc.mm(lambda e: e.matmul(ps[:, pb, :], lhsT=vm[:, mc, h * 512 + c2 * 128:h * 512 + (c2 + 1) * 128],
                                                     rhs=PTx[mc][:], start=(mc == 0), stop=(mc == 1)),
                                  reads=[Bvm, BPTx[mc]], writes=[Bps[pb]], inc=(mc == 1))
                        sc.op("dve", lambda e: e.tensor_tensor(out=ox[:, h * 4 + c2, :], in0=ps[:, pb, :], in1=rsx[:],
                                                               op=ALU.mult),
                              reads=[Bps[pb], Brsx], writes=[Box[h * 4 + c2]])
                proj_add("xattn_wo", ox, lambda k: Box[k], store=True)
            sc.barrier()
        if stop in ("C", "C1"):
            print("instructions", sc.n_ins, "waits", sc.n_wait)
            return nc

        with ExitStack() as E:
            ws = WStream(E)
            for t in range(NT):
                ws.plan += plan_ffn("ffn2")
            xT = sbuf(E, "xT5", (128, 16, 512), F32)
            BxT = [Buf("xT%d" % k) for k in range(16)]
            dxT = sc.dma_sem("xT5load")
            xn = sbuf(E, "xn5", (128, 16, 512), BF16)
            Bxn = [Buf("xn%d" % k) for k in range(16)]
            hid = sbuf(E, "hid5", (128, 43, 512), BF16)
            Bhid = [Buf("hid%d" % k) for k in range(43)]
            ost = [sbuf(E, "ost%d" % i, (128, 2048), F32) for i in range(2)]
            Bost = [Buf() for _ in range(2)]
            dost = [sc.dma_sem("ost%d" % i) for i in range(2)]
            sgt = [sbuf(E, "sgt5_%d" % i, (128, 512), F32) for i in range(3)]
            Bsgt = [Buf() for _ in range(3)]
            sqr = [sbuf(E, "sqr5_%d" % i, (128, 512), BF16) for i in range(4)]
            Bsqr = [Buf() for _ in range(4)]
            rs = sbuf(E, "rs5", (128, 512), F32)
            Brs = Buf()
            By = Buf("y", dram=True)
            ocnt = 0
            for t in range(NT):
                tcols = slice(t * 512, (t + 1) * 512)
                sc.dma("act", dxT, xT[:], h2T_s.rearrange("(k p) s -> p k s", p=128)[:, :, tcols],
                       reads=[Bh2], writes=BxT)
                norm16(xT, BxT, "g_ffn2", xn, Bxn, sqr, Bsqr, rs, Brs)
                ffn(ws, "ffn2", xn, Bxn, xT, BxT, hid, Bhid, sgt, Bsgt)
                for tg in range(4):
                    o_ = ocnt % 2
                    ocnt += 1
                    for kq in range(4):
                        pb = bank()
                        for c in range(4):
                            k = kq * 4 + c
                            sc.mm(lambda e: e.transpose(out=ps[:, pb, c * 128:(c + 1) * 128],
                                                        in_=xT[:, k, tg * 128:(tg + 1) * 128], identity=ident[:]),
                                  reads=[BxT[k], Bc], writes=[Bps[pb]], inc=(c == 3))
                        if kq % 2 == 0:
                            sc.op("act", lambda e: e.activation(out=ost[o_][:, kq * 512:(kq + 1) * 512], in_=ps[:, pb, :],
                                                                func=AF.Copy), reads=[Bps[pb]], writes=[Bost[o_]])
                        else:
                            sc.op("dve", lambda e: e.tensor_copy(out=ost[o_][:, kq * 512:(kq + 1) * 512], in_=ps[:, pb, :]),
                                  reads=[Bps[pb]], writes=[Bost[o_]])
                    r0 = t * 512 + tg * 128
                    sc.dma("sp", dost[o_], y_d[r0:r0 + 128, :], ost[o_][:], reads=[Bost[o_]], writes=[By])
            sc.barrier()
        print("instructions", sc.n_ins, "waits", sc.n_wait)
    return nc


def _consts(S):
    c = {}
    c["c_ident"] = np.eye(128, dtype=np.float32)
    ps = np.zeros((64, 64), np.float32)
    for m in range(32):
        ps[m + 32, m] = -1.0
        ps[m, m + 32] = 1.0
    c["c_pswap"] = ps
    inv = (np.float32(10000.0) ** (-np.arange(32, dtype=np.float32) / np.float32(32))).astype(np.float32)
    ang = (np.arange(S, dtype=np.float32)[None, :] * inv[:, None]).astype(np.float32)
    c["c_cos"] = np.concatenate([np.cos(ang), np.cos(ang)], 0).astype(np.float32)
    c["c_sin"] = np.concatenate([np.sin(ang), np.sin(ang)], 0).astype(np.float32)
    j = np.arange(128)[:, None]
    i = np.arange(128)[None, :]
    sc = np.float32(-1.0 / 16.0)
    c["c_LnF"] = (j <= i).astype(np.float32) * sc
    c["c_UnF"] = (j > i).astype(np.float32) * sc
    c["c_LnB"] = (j >= i).astype(np.float32) * sc
    c["c_UnB"] = (j < i).astype(np.float32) * sc
    c["c_maskF"] = np.tile((j <= i).astype(np.float32), (1, 4))
    c["c_maskB"] = np.tile((j >= i).astype(np.float32), (1, 4))
    return c


def _colmajor(v, n):
    return np.ascontiguousarray(np.asarray(v, np.float32).reshape(n, 128).T)


def _shared_inputs(inp, S):
    m = dict(_consts(S))
    for n in WSHAPES:
        m[n] = np.ascontiguousarray(np.asarray(inp[n], np.float32)[0])
    m["g_ffn1"] = _colmajor(inp["ffn1_norm"][0], 16)
    m["g_mix"] = _colmajor(inp["mix_norm"][0], 16)
    m["g_xn"] = _colmajor(inp["xattn_norm"][0], 16)
    m["g_mem"] = _colmajor(inp["mem_norm"][0], 16)
    m["g_ffn2"] = _colmajor(inp["ffn2_norm"][0], 16)
    m["g_mq"] = _colmajor(inp["mla_q_norm"][0], 4)
    m["g_mkv"] = _colmajor(inp["mla_kv_norm"][0], 2)
    for nm, src in (("g_qq", "mla_qk_q_norm"), ("g_qk", "mla_qk_k_norm")):
        g = np.zeros((128, 2), np.float32)
        v = np.asarray(inp[src], np.float32)[0]
        g[:, 0] = v[0:128]
        g[0:64, 1] = v[128:192]
        m[nm] = g
    m["g_xq"] = _colmajor(inp["xattn_q_norm"][0], 4)
    m["g_xk"] = _colmajor(inp["xattn_k_norm"][0], 4)
    m["gla_gain_bc"] = np.ascontiguousarray(
        np.broadcast_to(np.asarray(inp["gla_out_norm"], np.float32)[0][None, :], (128, 1024)))
    z = np.zeros((16, 512), np.float32)
    m["wa2c_f"] = np.concatenate([np.asarray(inp["gla_wa2_fwd"], np.float32)[0], z], 0)
    m["wa2c_b"] = np.concatenate([z, np.asarray(inp["gla_wa2_bwd"], np.float32)[0]], 0)
    m["ba_f"] = np.asarray(inp["gla_ba_fwd"], np.float32).reshape(1, 512)
    m["ba_b"] = np.asarray(inp["gla_ba_bwd"], np.float32).reshape(1, 512)
    return m


def kernel(**inputs):
    S = SEQ
    xs = [np.asarray(inputs["x_prompt"], np.float32)[b] for b in range(2)] + \
         [np.asarray(inputs["x_sample"], np.float32)[b] for b in range(4)]
    ms = [np.asarray(inputs["mem_prompt"], np.float32)[b] for b in range(2)] + \
         [np.asarray(inputs["mem_sample"], np.float32)[b] for b in range(4)]
    shared = _shared_inputs(inputs, S)
    nc = build(S)
    core_seq = {0: 0, 1: 1, 2: 2, 4: 3, 5: 4, 6: 5}
    zx = np.zeros((S, D), np.float32)
    zm = np.zeros((MEM, D), np.float32)
    in_maps = []
    for c in range(8):
        m = dict(shared)
        if c in core_seq:
            m["x"] = np.ascontiguousarray(xs[core_seq[c]])
            m["mem"] = np.ascontiguousarray(ms[core_seq[c]])
        else:
            m["x"] = zx
            m["mem"] = zm
        in_maps.append(m)
    res = run_bass_kernel_spmd(nc, in_maps, core_ids=list(range(8)))
    inv = {q: c for c, q in core_seq.items()}
    ys = [np.asarray(res.results[inv[q]]["y"], np.float32) for q in range(6)]
    return (np.stack(ys[0:2], 0), np.stack(ys[2:6], 0))
```

```python
import numpy as np
from contextlib import ExitStack
import concourse.bass as bass
import concourse.mybir as mybir
from concourse.bass_utils import run_bass_kernel_spmd

F32 = mybir.dt.float32
BF16 = mybir.dt.bfloat16
AF = mybir.ActivationFunctionType
ALU = mybir.AluOpType

D = 2048
DFF = 5504
DIN = 3936
EPS = 1e-6
SEQ = 4096
MEM = 256

WSHAPES = {
    "ffn1_w_gate": (D, DFF), "ffn1_w_up": (D, DFF), "ffn1_w_down": (DFF, D),
    "w_in": (D, DIN), "mla_w_uq": (512, 1536), "mla_w_ukv": (256, 2048),
    "w_out": (D, D), "xattn_wq": (D, D), "xattn_wk": (D, D), "xattn_wv": (D, D),
    "xattn_wo": (D, D),
    "ffn2_w_gate": (D, DFF), "ffn2_w_up": (D, DFF), "ffn2_w_down": (DFF, D),
}
W_EARLY = ["ffn1_w_gate", "ffn1_w_up", "ffn1_w_down", "w_in", "mla_w_uq", "mla_w_ukv"]
W_LATE = ["w_out", "xattn_wk", "xattn_wv", "xattn_wq", "xattn_wo",
          "ffn2_w_gate", "ffn2_w_up", "ffn2_w_down"]


class Buf:
    __slots__ = ("name", "w", "r", "dram")

    def __init__(self, name="", dram=False):
        self.name = name
        self.w = {}
        self.r = {}
        self.dram = dram


class DmaSem:
    __slots__ = ("sem", "val", "key")

    def __init__(self, sem, key):
        self.sem = sem
        self.val = 0
        self.key = key


class Sched:
    def __init__(self, nc, stack):
        self.nc = nc
        self.stack = stack
        self.eng = {"pe": nc.tensor, "act": nc.scalar, "dve": nc.vector,
                    "pool": nc.gpsimd, "sp": nc.sync}
        self.esem = {}
        self.ecnt = {}
        self.known = {}
        self.nsem = 0
        self.dsems = []
        self.dmap = {}
        for e in self.eng:
            self.esem[e] = self._sem("e_" + e)
            self.ecnt[e] = 0
            self.known[e] = {}
        self.n_wait = 0
        self.n_ins = 0

    def _sem(self, name):
        self.nsem += 1
        return self.stack.enter_context(self.nc.semaphore("%s_%d" % (name, self.nsem)))

    def dma_sem(self, name):
        s = self._sem("d_" + name)
        d = DmaSem(s, "d_%s_%d" % (name, self.nsem))
        self.dsems.append(d)
        self.dmap[d.key] = d
        return d

    def _wait_for(self, e, reads, writes, skip_waw=None):
        need = {}
        own = "e_" + e
        for b in reads:
            for k, sv in b.w.items():
                if k not in need or need[k][1] < sv[1]:
                    need[k] = sv
        for b in writes:
            for k, sv in b.w.items():
                if k == skip_waw or (b.dram and k in self.dmap):
                    continue
                if k not in need or need[k][1] < sv[1]:
                    need[k] = sv
            for k, sv in b.r.items():
                if k == own:
                    continue
                if k not in need or need[k][1] < sv[1]:
                    need[k] = sv
        kn = self.known[e]
        for k, (s, v) in need.items():
            if k == own and e == "pe":
                continue
            if kn.get(k, 0) >= v:
                continue
            if k in self.dmap:
                v = self.dmap[k].val
            self.eng[e].wait_ge(s, v)
            self.n_wait += 1
            kn[k] = v

    def op(self, e, fn, reads=(), writes=()):
        self._wait_for(e, reads, writes)
        ins = fn(self.eng[e])
        self.n_ins += 1
        self.ecnt[e] += 1
        ins.then_inc(self.esem[e], 1)
        key = "e_" + e
        ev = (self.esem[e], self.ecnt[e])
        for b in reads:
            b.r[key] = ev
        for b in writes:
            b.w = {key: ev}
            b.r = {}
        return ins

    def mm(self, fn, reads=(), writes=(), inc=True):
        self._wait_for("pe", reads, writes)
        ins = fn(self.eng["pe"])
        self.n_ins += 1
        key = "e_pe"
        if inc:
            self.ecnt["pe"] += 1
            ins.then_inc(self.esem["pe"], 1)
        ev = (self.esem["pe"], self.ecnt["pe"] + (0 if inc else 1))
        for b in reads:
            b.r[key] = ev
        for b in writes:
            b.w = {key: ev}
            b.r = {}
        return ins

    def dma(self, e, ds, out, in_, reads=(), writes=(), **kw):
        self._wait_for(e, reads, writes, skip_waw=ds.key)
        ins = self.eng[e].dma_start(out=out, in_=in_, **kw)
        self.n_ins += 1
        ds.val += 16
        ins.then_inc(ds.sem, 16)
        ev = (ds.sem, ds.val)
        for b in reads:
            b.r[ds.key] = ev
        for b in writes:
            if b.dram:
                b.w[ds.key] = ev
            else:
                b.w = {ds.key: ev}
                b.r = {}
        return ins

    def barrier(self):
        for e in self.eng:
            kn = self.known[e]
            for f in self.eng:
                if f == "sp":
                    continue
                k = "e_" + f
                v = self.ecnt[f]
                if v > 0 and kn.get(k, 0) < v and not (f == e):
                    self.eng[e].wait_ge(self.esem[f], v)
                    kn[k] = v
            for d in self.dsems:
                if d.val > 0 and kn.get(d.key, 0) < d.val:
                    self.eng[e].wait_ge(d.sem, d.val)
                    kn[d.key] = d.val


def build(S=SEQ, dbg=False, stop=None):
    NT = S // 512
    NCH = S // 128
    nc = bass.Bass("TRN2", target_bir_lowering=False)

    def din(name, shape, dt=F32):
        return nc.dram_tensor(name, list(shape), dt, kind="ExternalInput").ap()

    def dscr(name, shape, dt):
        return nc.dram_tensor(name, list(shape), dt,
                              kind="ExternalOutput" if dbg else "Internal").ap()

    x_d = din("x", (S, D))
    mem_d = din("mem", (MEM, D))
    y_d = nc.dram_tensor("y", [S, D], F32, kind="ExternalOutput").ap()
    w32 = {n: din(n, sh) for n, sh in WSHAPES.items()}
    wbf = {n: nc.dram_tensor("bf_" + n, list(sh), BF16, kind="Internal").ap()
           for n, sh in WSHAPES.items()}
    wv = {n: wbf[n].rearrange("(kc p) c -> p kc c", p=128) for n in WSHAPES}
    cst = {
        "ident": din("c_ident", (128, 128)),
        "g_ffn1": din("g_ffn1", (128, 16)), "g_mix": din("g_mix", (128, 16)),
        "g_xn": din("g_xn", (128, 16)), "g_mem": din("g_mem", (128, 16)),
        "g_ffn2": din("g_ffn2", (128, 16)),
        "g_mq": din("g_mq", (128, 4)), "g_mkv": din("g_mkv", (128, 2)),
        "g_qq": din("g_qq", (128, 2)), "g_qk": din("g_qk", (128, 2)),
        "g_xq": din("g_xq", (128, 4)), "g_xk": din("g_xk", (128, 4)),
        "pswap": din("c_pswap", (64, 64)),
    }
    cos_d = din("c_cos", (64, S))
    sin_d = din("c_sin", (64, S))
    gla_gain_d = din("gla_gain_bc", (128, 1024))
    wa2c_d = [din("wa2c_f", (32, 512)), din("wa2c_b", (32, 512))]
    ba_d = [din("ba_f", (1, 512)), din("ba_b", (1, 512))]
    tri_d = {n: din("c_" + n, (128, 128)) for n in ["LnF", "UnF", "LnB", "UnB"]}
    mask_d = [din("c_maskF", (128, 512)), din("c_maskB", (128, 512))]

    hT_s = dscr("s_hT", (D, S), F32)
    gqT_s = dscr("s_gqT", (512, S), F32)
    gkT_s = dscr("s_gkT", (512, S), F32)
    gk_s = dscr("s_gk", (S, 512), F32)
    gv_s = dscr("s_gv", (S, 1024), BF16)
    gsr_s = dscr("s_gsr", (S, 1024), F32)
    gaT_s = dscr("s_gaT", (32, S), F32)
    QnT_s = dscr("s_QnT", (1024, S), BF16)
    QrT_s = dscr("s_QrT", (512, S), BF16)
    KnT_s = dscr("s_KnT", (1024, S), BF16)
    KrT_s = dscr("s_KrT", (512, S), BF16)
    V_s = dscr("s_V", (S, 1024), BF16)
    of_s = dscr("s_of", (S, 1024), F32)
    ymT_s = dscr("s_ymT", (D, S), BF16)

    with ExitStack() as G:
        sc = Sched(nc, G)

        sb_cnt = [0]

        def sbuf(stack, name, shape, dt):
            sb_cnt[0] += 1
            return stack.enter_context(nc.sbuf_tensor("%s_%d" % (name, sb_cnt[0]), list(shape), dt))

        ps = G.enter_context(nc.psum_tensor("ps", [128, 8, 512], F32))
        Bps = [Buf("ps%d" % i) for i in range(8)]
        bank_i = [0]

        bank_pool = [list(range(8))]

        def bank():
            p = bank_pool[0]
            b = p[bank_i[0] % len(p)]
            bank_i[0] += 1
            return b

        cs = {}
        Bc = Buf("consts")
        dc = sc.dma_sem("const")
        for n, ap in cst.items():
            cs[n] = sbuf(G, "k_" + n, ap.shape, F32)
            sc.dma("sp", dc, cs[n][:], ap, writes=[Bc])
        ones_bf = sbuf(G, "ones_bf", (128, 128), BF16)
        ones_f = sbuf(G, "ones_f", (128, 128), F32)
        sc.op("dve", lambda e: e.memset(ones_bf[:], 1.0), writes=[Bc])
        sc.op("dve", lambda e: e.memset(ones_f[:], 1.0), writes=[Bc])
        ident = cs["ident"]

        Bw = {n: Buf("w_" + n, dram=True) for n in WSHAPES}
        dcast = {n: sc.dma_sem("cast_" + n) for n in WSHAPES}

        def cast(names, after=()):
            if after:
                sc._wait_for("pool", list(after), ())
            for n in names:
                R, C = WSHAPES[n]
                for r0 in range(0, R, 512):
                    r1 = min(R, r0 + 512)
                    sc.dma("pool", dcast[n], wbf[n][r0:r1, :], w32[n][r0:r1, :],
                           writes=[Bw[n]], max_dma_last_dim=8192)

        Bwp = {}
        for cg in range(3):
            c0 = cg * 2048
            c1 = min(DFF, c0 + 2048)
            for n in ("ffn1_w_gate", "ffn1_w_up"):
                d = sc.dma_sem("castp")
                Bwp[(n, cg)] = Buf()
                for r0 in range(0, D, 512):
                    sc.dma("pool", d, wbf[n][r0:r0 + 512, c0:c1], w32[n][r0:r0 + 512, c0:c1], writes=[Bwp[(n, cg)]])
        for kp in range(3):
            r0 = kp * 2048
            r1 = min(DFF, r0 + 2048)
            d = sc.dma_sem("castp")
            Bwp[("ffn1_w_down", kp)] = Buf()
            for rr in range(r0, r1, 512):
                sc.dma("pool", d, wbf["ffn1_w_down"][rr:min(r1, rr + 512), :], w32["ffn1_w_down"][rr:min(r1, rr + 512), :],
                       writes=[Bwp[("ffn1_w_down", kp)]])

        def wdep(n, k0, c0):
            if n in ("ffn1_w_gate", "ffn1_w_up"):
                return Bwp[(n, c0 // 2048)]
            if n == "ffn1_w_down":
                return Bwp[(n, k0 // 16)]
            return Bw[n]

        class WStream:
            def __init__(self, stack, nslots=4):
                self.n = nslots
                self.t = [sbuf(stack, "wt%d" % i, (128, 8192), BF16) for i in range(nslots)]
                self.B = [Buf("wt%d" % i) for i in range(nslots)]
                self.d = [sc.dma_sem("wt%d" % i) for i in range(nslots)]
                self.plan = []
                self.issued = 0
                self.used = 0

            def view(self, s, nk, ncol):
                return self.t[s][:, 0:nk * ncol].rearrange("p (k c) -> p k c", k=nk)

            def _issue(self, i):
                (n, k0, nk, c0, ncol) = self.plan[i]
                s = i % self.n
                sc.dma("sp", self.d[s], self.view(s, nk, ncol),
                       wv[n][:, k0:k0 + nk, c0:c0 + ncol], reads=[wdep(n, k0, c0)], writes=[self.B[s]])

            def next(self, tag):
                i = self.used
                assert self.plan[i] == tag, (i, self.plan[i], tag)
                lim = min(len(self.plan), i + self.n - 1)
                while self.issued < lim:
                    self._issue(self.issued)
                    self.issued += 1
                self.used += 1
                s = i % self.n
                return self.view(s, tag[2], tag[4]), self.B[s]

        def plan_ffn(pref):
            p = []
            for cb in range(11):
                ncol = 512 if cb < 10 else 384
                p.append((pref + "_w_gate", 0, 16, cb * 512, ncol))
                p.append((pref + "_w_up", 0, 16, cb * 512, ncol))
            for cg in range(4):
                for kp in range(3):
                    p.append((pref + "_w_down", kp * 16, 16 if kp < 2 else 11, cg * 512, 512))
            return p

        def rstd_from(pb, nfeat, rs, Brs):
            sc.op("act", lambda e: e.activation(out=rs[:], in_=ps[:, pb, :], func=AF.Ln,
                                                scale=1.0 / nfeat, bias=EPS),
                  reads=[Bps[pb]], writes=[Brs])
            sc.op("act", lambda e: e.activation(out=rs[:], in_=rs[:], func=AF.Exp, scale=-0.5), reads=[Brs], writes=[Brs])

        def ffn(ws, pref, xn, Bxn, res, Bres, hid, Bhid, sgt, Bsgt):
            for cb in range(11):
                ncol = 512 if cb < 10 else 384
                wg, Bg = ws.next((pref + "_w_gate", 0, 16, cb * 512, ncol))
                wu, Bu = ws.next((pref + "_w_up", 0, 16, cb * 512, ncol))
                for m in range(ncol // 128):
                    j = cb * 4 + m
                    pg = bank()
                    pu = bank()
                    for k in range(16):
                        sc.mm(lambda e: e.matmul(ps[:, pg, :], lhsT=wg[:, k, m * 128:(m + 1) * 128],
                                                 rhs=xn[:, k, :], start=(k == 0), stop=(k == 15)),
                              reads=[Bg, Bxn[k]], writes=[Bps[pg]], inc=(k == 15))
                    for k in range(16):
                        sc.mm(lambda e: e.matmul(ps[:, pu, :], lhsT=wu[:, k, m * 128:(m + 1) * 128],
                                                 rhs=xn[:, k, :], start=(k == 0), stop=(k == 15)),
                              reads=[Bu, Bxn[k]], writes=[Bps[pu]], inc=(k == 15))
                    tmp = sgt[j % len(sgt)]
                    Bt = Bsgt[j % len(sgt)]
                    sc.op("act", lambda e: e.activation(out=tmp[:], in_=ps[:, pg, :], func=AF.Silu),
                          reads=[Bps[pg]], writes=[Bt])
                    sc.op("dve", lambda e: e.tensor_tensor(out=hid[:, j, :], in0=ps[:, pu, :],
                                                           in1=tmp[:], op=ALU.mult),
                          reads=[Bps[pu], Bt], writes=[Bhid[j]])
            for cg in range(4):
                pbs = [bank() for _ in range(4)]
                for kp in range(3):
                    nk = 16 if kp < 2 else 11
                    wd, Bd = ws.next((pref + "_w_down", kp * 16, nk, cg * 512, 512))
                    for m in range(4):
                        for k in range(nk):
                            kk = kp * 16 + k
                            sc.mm(lambda e: e.matmul(ps[:, pbs[m], :],
                                                     lhsT=wd[:, k, m * 128:(m + 1) * 128],
                                                     rhs=hid[:, kk, :], start=(kk == 0), stop=(kk == 42)),
                                  reads=[Bd, Bhid[kk]], writes=[Bps[pbs[m]]], inc=(k == nk - 1))
                for m in range(4):
                    ob = cg * 4 + m
                    sc.op("dve", lambda e: e.scalar_tensor_tensor(
                        out=res[:, ob, :], in0=ps[:, pbs[m], :], scalar=0.5, in1=res[:, ob, :],
                        op0=ALU.mult, op1=ALU.add),
                        reads=[Bps[pbs[m]], Bres[ob]], writes=[Bres[ob]])

        def norm16(res, Bres, gname, xn, Bxn, sqr, Bsqr, rs, Brs, N=512):
            pb = bank()
            for k in range(16):
                q = sqr[k % len(sqr)]
                Bq = Bsqr[k % len(sqr)]
                en = "adadpadadpadadpad"[k]
                if en == "a":
                    sc.op("act", lambda e: e.activation(out=q[:, 0:N], in_=res[:, k, 0:N], func=AF.Square),
                          reads=[Bres[k]], writes=[Bq])
                else:
                    sc.op("dve" if en == "d" else "pool",
                          lambda e: e.tensor_tensor(out=q[:, 0:N], in0=res[:, k, 0:N],
                                                    in1=res[:, k, 0:N], op=ALU.mult),
                          reads=[Bres[k]], writes=[Bq])
                sc.mm(lambda e: e.matmul(ps[:, pb, 0:N], lhsT=ones_bf[:], rhs=q[:, 0:N],
                                         start=(k == 0), stop=(k == 15)),
                      reads=[Bq, Bc], writes=[Bps[pb]], inc=True)
            sc.op("act", lambda e: e.activation(out=rs[:, 0:N], in_=ps[:, pb, 0:N], func=AF.Ln,
                                                scale=1.0 / D, bias=EPS),
                  reads=[Bps[pb]], writes=[Brs])
            sc.op("act", lambda e: e.activation(out=rs[:, 0:N], in_=rs[:, 0:N], func=AF.Exp, scale=-0.5),
                  reads=[Brs], writes=[Brs])
            g = cs[gname]
            for k in range(16):
                sc.op("dve", lambda e: e.scalar_tensor_tensor(
                    out=xn[:, k, 0:N], in0=res[:, k, 0:N], scalar=g[:, k:k + 1], in1=rs[:, 0:N],
                    op0=ALU.mult, op1=ALU.mult),
                    reads=[Bres[k], Brs, Bc], writes=[Bxn[k]])

        def load_T(src_rows, nrow_groups, res, Bres, stg, Bstg, dstg, col0, preloaded=False):
            for g in range(nrow_groups):
                if preloaded:
                    s = g
                else:
                    s = load_T.cnt % 2
                    load_T.cnt += 1
                    sc.dma("act", dstg[s], stg[s][:], src_rows(g), writes=[Bstg[s]])
                for kq in range(4):
                    pb = bank()
                    for c in range(4):
                        k = kq * 4 + c
                        sc.mm(lambda e: e.transpose(out=ps[:, pb, c * 128:(c + 1) * 128],
                                                    in_=stg[s][:, k * 128:(k + 1) * 128], identity=ident[:]),
                              reads=[Bstg[s], Bc], writes=[Bps[pb]], inc=(c == 3))
                    o_ap = res[:, kq * 4:(kq + 1) * 4, col0 + g * 128:col0 + (g + 1) * 128]
                    i_ap = ps[:, pb, :].rearrange("p (c n) -> p c n", c=4)
                    bl = [Bres[kq * 4 + c] for c in range(4)]
                    if kq % 2 == 0:
                        sc.op("act", lambda e: e.activation(out=o_ap, in_=i_ap, func=AF.Copy),
                              reads=[Bps[pb]], writes=bl)
                    else:
                        sc.op("dve", lambda e: e.tensor_copy(out=o_ap, in_=i_ap),
                              reads=[Bps[pb]], writes=bl)
        load_T.cnt = 0

        Bscr = {n: Buf("scr_" + n, dram=True) for n in
                ["hT", "gqT", "gkT", "gk", "gv", "gsr", "gaT", "QnT", "QrT", "KnT", "KrT", "V", "of", "ymT"]}
        with ExitStack() as A:
            ws = WStream(A)
            for t in range(NT):
                ws.plan += plan_ffn("ffn1")
            xT = sbuf(A, "xT", (128, 16, 512), F32)
            BxT = [Buf("xT%d" % k) for k in range(16)]
            xn = sbuf(A, "xn", (128, 16, 512), BF16)
            Bxn = [Buf("xn%d" % k) for k in range(16)]
            hid = sbuf(A, "hid", (128, 43, 512), BF16)
            Bhid = [Buf("hid%d" % k) for k in range(43)]
            stg = [sbuf(A, "stg%d" % i, (128, 2048), F32) for i in range(4)]
            Bstg = [Buf() for _ in range(4)]
            dstg = [sc.dma_sem("stg%d" % i) for i in range(4)]

            def x_prefetch(t):
                if t < NT:
                    for g in range(4):
                        sc.dma("act", dstg[g], stg[g][:], x_d[t * 512 + g * 128: t * 512 + (g + 1) * 128, :],
                               writes=[Bstg[g]])
            x_prefetch(0)
            sgt = [sbuf(A, "sgt%d" % i, (128, 512), F32) for i in range(3)]
            Bsgt = [Buf() for _ in range(3)]
            sqr = [sbuf(A, "sqr%d" % i, (128, 512), BF16) for i in range(4)]
            Bsqr = [Buf() for _ in range(4)]
            rs = sbuf(A, "rs", (128, 512), F32)
            Brs = Buf()
            dhs = sc.dma_sem("hstore")
            for t in range(NT):
                tc0 = t * 512
                tcols = slice(tc0, tc0 + 512)
                load_T(None, 4, xT, BxT, stg, Bstg, dstg, 0, preloaded=True)
                x_prefetch(t + 1)
                norm16(xT, BxT, "g_ffn1", xn, Bxn, sqr, Bsqr, rs, Brs)
                ffn(ws, "ffn1", xn, Bxn, xT, BxT, hid, Bhid, sgt, Bsgt)
                if dbg and t == 0:
                    d_xn = nc.dram_tensor("d_xn", [128, 16, 512], BF16, kind="ExternalOutput").ap()
                    d_hid = nc.dram_tensor("d_hid", [128, 43, 512], BF16, kind="ExternalOutput").ap()
                    sc.dma("act", dhs, d_xn, xn[:], reads=Bxn, writes=[Buf()])
                    sc.dma("act", dhs, d_hid, hid[:], reads=Bhid, writes=[Buf()])
                if t == 0:
                    cast(["w_in", "mla_w_uq", "mla_w_ukv"] + W_LATE, after=[BxT[15]])
                sc.dma("sp", dhs, hT_s.rearrange("(k p) s -> p k s", p=128)[:, :, tcols], xT[:],
                       reads=BxT, writes=[Bscr["hT"]])
            sc.barrier()
        if stop == "ffn1":
            print("instructions", sc.n_ins, "waits", sc.n_wait)
            return nc

        with ExitStack() as A:
            ws = WStream(A, nslots=3)
            for t in range(NT):
                ws.plan += [("w_in", 0, 16, c0, 512) for c0 in range(0, 3072, 512)]
                ws.plan += [("w_in", 0, 16, 3072, 32), ("w_in", 0, 16, 3104, 512), ("w_in", 0, 16, 3616, 320)]
            xT = sbuf(A, "xT2", (128, 16, 512), F32)
            BxT = [Buf("xT%d" % k) for k in range(16)]
            dxT = sc.dma_sem("xTload")
            xn = sbuf(A, "xn2", (128, 16, 512), BF16)
            Bxn = [Buf("xn%d" % k) for k in range(16)]
            sqr = [sbuf(A, "sqr2_%d" % i, (128, 512), BF16) for i in range(4)]
            Bsqr = [Buf() for _ in range(4)]
            rs = sbuf(A, "rs2", (128, 512), F32)
            Brs = Buf()
            zs = [sbuf(A, "zs%d" % i, (128, 4, 512), F32) for i in range(2)]
            Bzs = [Buf() for _ in range(2)]
            dzs = [sc.dma_sem("zs%d" % i) for i in range(2)]
            zcnt = [0]
            vst = [sbuf(A, "vst%d" % i, (128, 1024), BF16) for i in range(2)]
            Bvst = [Buf() for _ in range(2)]
            dvst = [sc.dma_sem("vst%d" % i) for i in range(2)]
            vsV = [sbuf(A, "vsV%d" % i, (128, 1024), BF16) for i in range(2)]
            BvsV = [Buf() for _ in range(2)]
            dvsV = [sc.dma_sem("vsV%d" % i) for i in range(2)]
            vcntV = [0]
            cqT = sbuf(A, "cqT", (128, 4, 512), F32)
            BcqT = [Buf() for _ in range(4)]
            ckvT = sbuf(A, "ckvT", (128, 2, 512), F32)
            BckvT = [Buf() for _ in range(2)]
            kpeT = sbuf(A, "kpeT", (64, 512), F32)
            BkpeT = Buf()
            sqkpe = sbuf(A, "sqkpe", (64, 512), BF16)
            Bsqkpe = Buf()
            cqn = sbuf(A, "cqn", (128, 4, 512), BF16)
            Bcqn = [Buf() for _ in range(4)]
            ckvn = sbuf(A, "ckvn", (128, 2, 512), BF16)
            Bckvn = [Buf() for _ in range(2)]
            qk_st = [[sbuf(A, "qkst%d_%d" % (a_, i), (128, 512), BF16) for i in range(2)] for a_ in range(4)]
            Bqk_st = [[Buf() for i in range(2)] for a_ in range(4)]
            dqk = [sc.dma_sem("qkst%d" % i) for i in range(2)]
            rq = [sbuf(A, "rq%d" % i, (128, 512), F32) for i in range(2)]
            Brq = [Buf() for _ in range(2)]
            rope_f = [sbuf(A, "ropef%d" % i, (64, 512), F32) for i in range(2)]
            Brope_f = [Buf() for _ in range(2)]
            rt1 = [sbuf(A, "rt1_%d" % i, (64, 512), F32) for i in range(2)]
            rt2 = [sbuf(A, "rt2_%d" % i, (64, 512), F32) for i in range(2)]
            Brt1 = [Buf() for _ in range(2)]
            Brt2 = [Buf() for _ in range(2)]
            cosT = sbuf(A, "cosT", (64, 512), F32)
            sinT = sbuf(A, "sinT", (64, 512), F32)
            Bcs = Buf()
            dcs = sc.dma_sem("cossin")
            ropec = [0]

            def store_z(fill, dst_ap, Bdst):
                i = zcnt[0] % 2
                zcnt[0] += 1
                fill(zs[i], Bzs[i])
                sc.dma("sp", dzs[i], dst_ap, zs[i][:], reads=[Bzs[i]], writes=[Bdst])

            wq = sbuf(A, "wq_res", (128, 4, 1536), BF16)
            wkv = sbuf(A, "wkv_res", (128, 2, 2048), BF16)
            Bwq = Buf()
            Bwkv = Buf()
            dwres = sc.dma_sem("wres")
            sc.dma("sp", dwres, wq[:], wv["mla_w_uq"][:, 0:4, :], reads=[Bw["mla_w_uq"]], writes=[Bwq])
            sc.dma("sp", dwres, wkv[:], wv["mla_w_ukv"][:, 0:2, :], reads=[Bw["mla_w_ukv"]], writes=[Bwkv])
            sqrB = [sbuf(A, "sqrB%d" % i, (128, 512), BF16) for i in range(3)]
            BsqrB = [Buf() for _ in range(3)]
            rsB = sbuf(A, "rsB", (128, 512), F32)
            BrsB = Buf()
            vcnt = [0]

            def wstat(w, Bwt, c0, M, pb, st=False):
                for k in range(16):
                    sc.mm(lambda e: e.matmul(ps[0:M, pb, :], lhsT=w[:, k, c0:c0 + M], rhs=xn[:, k, :],
                                             start=(k == 0), stop=(k == 15)),
                          reads=[Bwt, Bxn[k]], writes=[Bps[pb]], inc=(k == 15))
                    if st and k == 7:
                        step(1)

            def astat(w, Bwt, tg, pb, ncol=512, st=False):
                for k in range(16):
                    sc.mm(lambda e: e.matmul(ps[:, pb, 0:ncol], lhsT=xn[:, k, tg * 128:(tg + 1) * 128],
                                             rhs=w[:, k, 0:ncol], start=(k == 0), stop=(k == 15)),
                          reads=[Bwt, Bxn[k]], writes=[Bps[pb]], inc=(k == 15))
                    if st and k == 7:
                        step(1)

            def do_evac(i, out_ap, pb, Bout, M=128, func=AF.Copy):
                if i % 2 == 0 or func != AF.Copy:
                    sc.op("act", lambda e: e.activation(out=out_ap, in_=ps[0:M, pb, :], func=func),
                          reads=[Bps[pb]], writes=[Bout])
                else:
                    sc.op("dve", lambda e: e.tensor_copy(out=out_ap, in_=ps[0:M, pb, :]),
                          reads=[Bps[pb]], writes=[Bout])

            def normN(src, Bsrc, n, nfeat, gname, dst, Bdst, pb):
                for k in range(n):
                    q = sqrB[k % 3]
                    sc.op("act", lambda e: e.activation(out=q[:], in_=src[:, k, :], func=AF.Square),
                          reads=[Bsrc[k]], writes=[BsqrB[k % 3]])
                    sc.mm(lambda e: e.matmul(ps[:, pb, :], lhsT=ones_bf[:], rhs=q[:], start=(k == 0),
                                             stop=(k == n - 1)),
                          reads=[BsqrB[k % 3], Bc], writes=[Bps[pb]], inc=True)
                rstd_from(pb, nfeat, rsB, BrsB)
                g = cs[gname]
                for k in range(n):
                    sc.op("dve", lambda e: e.scalar_tensor_tensor(
                        out=dst[:, k, :], in0=src[:, k, :], scalar=g[:, k:k + 1], in1=rsB[:],
                        op0=ALU.mult, op1=ALU.mult), reads=[Bsrc[k], BrsB, Bc], writes=[Bdst[k]])

            def rope_a(src_ap, Bsrcs, gcol, rstd, Brstd):
                i = ropec[0] % 2
                ropec[0] += 1
                rf = rope_f[i]
                sc.op("dve", lambda e: e.scalar_tensor_tensor(
                    out=rf[:], in0=src_ap, scalar=cs[gcol][0:64, 1:2], in1=rstd[0:64, :],
                    op0=ALU.mult, op1=ALU.mult), reads=Bsrcs + [Brstd, Bc], writes=[Brope_f[i]])
                return i

            def rope_b(i, pw, dst_ap, Bdst):
                rf = rope_f[i]
                sc.mm(lambda e: e.matmul(ps[0:64, pw, :], lhsT=cs["pswap"][:, :], rhs=rf[:],
                                         start=True, stop=True),
                      reads=[Brope_f[i], Bc], writes=[Bps[pw]], inc=True)
                sc.op("pool", lambda e: e.tensor_tensor(out=rt1[i][:], in0=rf[:], in1=cosT[:], op=ALU.mult),
                      reads=[Brope_f[i], Bcs], writes=[Brt1[i]])
                sc.op("dve", lambda e: e.tensor_tensor(out=rt2[i][:], in0=ps[0:64, pw, :], in1=sinT[:],
                                                       op=ALU.mult),
                      reads=[Bps[pw], Bcs], writes=[Brt2[i]])
                sc.op("pool", lambda e: e.tensor_tensor(out=dst_ap, in0=rt1[i][:], in1=rt2[i][:], op=ALU.add),
                      reads=[Brt1[i], Brt2[i]], writes=[Bdst])

            def prep(t):
                tcols = slice(t * 512, (t + 1) * 512)
                normN(cqT, BcqT, 4, 512.0, "g_mq", cqn, Bcqn, 6)
                yield
                normN(ckvT, BckvT, 2, 256.0, "g_mkv", ckvn, Bckvn, 6)
                yield
                pq, pr, pk = 3, 4, 5
                for h in range(8):
                    hs = h % 2
                    for k in range(4):
                        sc.mm(lambda e: e.matmul(ps[:, pq, :], lhsT=wq[:, k, h * 192:h * 192 + 128],
                                                 rhs=cqn[:, k, :], start=(k == 0), stop=(k == 3)),
                              reads=[Bwq, Bcqn[k]], writes=[Bps[pq]], inc=(k == 3))
                    for k in range(4):
                        sc.mm(lambda e: e.matmul(ps[0:64, pr, :], lhsT=wq[:, k, h * 192 + 128:h * 192 + 192],
                                                 rhs=cqn[:, k, :], start=(k == 0), stop=(k == 3)),
                              reads=[Bwq, Bcqn[k]], writes=[Bps[pr]], inc=(k == 3))
                    for k in range(2):
                        sc.mm(lambda e: e.matmul(ps[:, pk, :], lhsT=wkv[:, k, h * 256:h * 256 + 128],
                                                 rhs=ckvn[:, k, :], start=(k == 0), stop=(k == 1)),
                              reads=[Bwkv, Bckvn[k]], writes=[Bps[pk]], inc=(k == 1))
                    q0, q1, q2 = sqrB[0], sqrB[1], sqrB[2]
                    sc.op("act", lambda e: e.activation(out=q0[:], in_=ps[:, pq, :], func=AF.Square),
                          reads=[Bps[pq]], writes=[BsqrB[0]])
                    sc.op("act", lambda e: e.activation(out=q1[0:64, :], in_=ps[0:64, pr, :], func=AF.Square),
                          reads=[Bps[pr]], writes=[BsqrB[1]])
                    sc.op("act", lambda e: e.activation(out=q2[:], in_=ps[:, pk, :], func=AF.Square),
                          reads=[Bps[pk]], writes=[BsqrB[2]])
                    yield
                    pss, psk = 6, 7
                    sc.mm(lambda e: e.matmul(ps[:, pss, :], lhsT=ones_bf[:], rhs=q0[:], start=True, stop=False),
                          reads=[BsqrB[0], Bc], writes=[Bps[pss]], inc=False)
                    sc.mm(lambda e: e.matmul(ps[:, pss, :], lhsT=ones_bf[0:64, :], rhs=q1[0:64, :],
                                             start=False, stop=True),
                          reads=[BsqrB[1], Bc], writes=[Bps[pss]], inc=True)
                    sc.mm(lambda e: e.matmul(ps[:, psk, :], lhsT=ones_bf[:], rhs=q2[:], start=True, stop=False),
                          reads=[BsqrB[2], Bc], writes=[Bps[psk]], inc=False)
                    sc.mm(lambda e: e.matmul(ps[:, psk, :], lhsT=ones_bf[0:64, :], rhs=sqkpe[:],
                                             start=False, stop=True),
                          reads=[Bsqkpe, Bc], writes=[Bps[psk]], inc=True)
                    rstd_from(pss, 192.0, rq[0], Brq[0])
                    rstd_from(psk, 192.0, rq[1], Brq[1])
                    yield
                    sc.op("dve", lambda e: e.scalar_tensor_tensor(
                        out=qk_st[0][hs][:], in0=ps[:, pq, :], scalar=cs["g_qq"][:, 0:1], in1=rq[0][:],
                        op0=ALU.mult, op1=ALU.mult), reads=[Bps[pq], Brq[0], Bc], writes=[Bqk_st[0][hs]])
                    iq = rope_a(ps[0:64, pr, :], [Bps[pr]], "g_qq", rq[0], Brq[0])
                    sc.op("dve", lambda e: e.scalar_tensor_tensor(
                        out=qk_st[2][hs][:], in0=ps[:, pk, :], scalar=cs["g_qk"][:, 0:1], in1=rq[1][:],
                        op0=ALU.mult, op1=ALU.mult), reads=[Bps[pk], Brq[1], Bc], writes=[Bqk_st[2][hs]])
                    ik = rope_a(kpeT[:, :], [BkpeT], "g_qk", rq[1], Brq[1])
                    sc.dma("sp", dqk[hs], QnT_s[h * 128:(h + 1) * 128, tcols], qk_st[0][hs][:],
                           reads=[Bqk_st[0][hs]], writes=[Bscr["QnT"]])
                    sc.dma("sp", dqk[hs], KnT_s[h * 128:(h + 1) * 128, tcols], qk_st[2][hs][:],
                           reads=[Bqk_st[2][hs]], writes=[Bscr["KnT"]])
                    yield
                    rope_b(iq, 6, qk_st[1][hs][0:64, :], Bqk_st[1][hs])
                    rope_b(ik, 7, qk_st[3][hs][0:64, :], Bqk_st[3][hs])
                    sc.dma("sp", dqk[hs], QrT_s[h * 64:(h + 1) * 64, tcols], qk_st[1][hs][0:64, :],
                           reads=[Bqk_st[1][hs]], writes=[Bscr["QrT"]])
                    sc.dma("sp", dqk[hs], KrT_s[h * 64:(h + 1) * 64, tcols], qk_st[3][hs][0:64, :],
                           reads=[Bqk_st[3][hs]], writes=[Bscr["KrT"]])
                    yield
                wkv_v = wkv[:].rearrange("p k (h two d) -> p k h two d", two=2, d=128)
                for tg in range(4):
                    vs_ = vcntV[0] % 2
                    vcntV[0] += 1
                    for half in range(2):
                        pb = 3 + half
                        for k in range(2):
                            sc.mm(lambda e: e.matmul(ps[:, pb, :].rearrange("p (h d) -> p h d", h=4),
                                                     lhsT=ckvn[:, k, tg * 128:(tg + 1) * 128],
                                                     rhs=wkv_v[:, k, half * 4:(half + 1) * 4, 1, :],
                                                     start=(k == 0), stop=(k == 1)),
                                  reads=[Bwkv, Bckvn[k]], writes=[Bps[pb]], inc=(k == 1))
                        do_evac(half, vsV[vs_][:, half * 512:(half + 1) * 512], pb, BvsV[vs_])
                    r0 = t * 512 + tg * 128
                    sc.dma("sp", dvsV[vs_], V_s[r0:r0 + 128, :], vsV[vs_][:], reads=[BvsV[vs_]], writes=[Bscr["V"]])
                    yield

            prev = [None]

            def step(n=2):
                if prev[0] is not None:
                    for _ in range(n):
                        next(prev[0], None)

            def drain():
                if prev[0] is not None:
                    for _ in prev[0]:
                        pass
                prev[0] = None

            bank_pool[0] = [0, 1, 2]

            def load_h(t):
                if t < NT:
                    sc.dma("act", dxT, xT[:], hT_s.rearrange("(k p) s -> p k s", p=128)[:, :, t * 512:(t + 1) * 512],
                           reads=[Bscr["hT"]], writes=BxT)
            load_h(0)
            for t in range(NT):
                tc0 = t * 512
                tcols = slice(tc0, tc0 + 512)
                norm16(xT, BxT, "g_mix", xn, Bxn, sqr, Bsqr, rs, Brs)
                load_h(t + 1)
                w, Bwt = ws.next(("w_in", 0, 16, 0, 512))

                def fill_gq(z, Bz):
                    for m in range(4):
                        pb = bank()
                        wstat(w, Bwt, m * 128, 128, pb, st=True)
                        do_evac(m, z[:, m, :], pb, Bz)
                        step(1)
                store_z(fill_gq, gqT_s.rearrange("(c p) s -> p c s", p=128)[:, :, tcols], Bscr["gqT"])
                w, Bwt = ws.next(("w_in", 0, 16, 512, 512))
                store_z(fill_gq, gkT_s.rearrange("(c p) s -> p c s", p=128)[:, :, tcols], Bscr["gkT"])

                def fill_tm(z, Bz, func=AF.Copy):
                    for tg in range(4):
                        pb = bank()
                        astat(w, Bwt, tg, pb, st=True)
                        do_evac(tg, z[:, tg, :], pb, Bz, func=func)
                        step(1)
                store_z(fill_tm, gk_s.rearrange("(g p) c -> p g c", p=128)[:, t * 4:(t + 1) * 4, :], Bscr["gk"])
                w0, Bw0 = ws.next(("w_in", 0, 16, 1024, 512))
                w1, Bw1 = ws.next(("w_in", 0, 16, 1536, 512))
                for tg in range(4):
                    vs_ = vcnt[0] % 2
                    vcnt[0] += 1
                    for half in range(2):
                        pb = bank()
                        astat((w0, w1)[half], (Bw0, Bw1)[half], tg, pb, st=True)
                        do_evac(tg + half, vst[vs_][:, half * 512:(half + 1) * 512], pb, Bvst[vs_])
                        step(1)
                    r0 = t * 512 + tg * 128
                    sc.dma("sp", dvst[vs_], gv_s[r0:r0 + 128, :], vst[vs_][:], reads=[Bvst[vs_]], writes=[Bscr["gv"]])
                for half in range(2):
                    w, Bwt = ws.next(("w_in", 0, 16, 2048 + half * 512, 512))
                    store_z(lambda z, Bz: fill_tm(z, Bz, AF.Silu),
                            gsr_s.rearrange("(g p) c -> p g c", p=128)[:, t * 4:(t + 1) * 4, half * 512:(half + 1) * 512],
                            Bscr["gsr"])
                w, Bwt = ws.next(("w_in", 0, 16, 3072, 32))
                i = zcnt[0] % 2
                zcnt[0] += 1
                pb = bank()
                wstat(w, Bwt, 0, 32, pb)
                do_evac(0, zs[i][0:32, 0, :], pb, Bzs[i], M=32)
                sc.dma("sp", dzs[i], gaT_s[:, tcols], zs[i][0:32, 0, :], reads=[Bzs[i]], writes=[Bscr["gaT"]])
                drain()
                w, Bwt = ws.next(("w_in", 0, 16, 3104, 512))
                for m in range(4):
                    pb = bank()
                    wstat(w, Bwt, m * 128, 128, pb)
                    do_evac(m, cqT[:, m, :], pb, BcqT[m])
                w, Bwt = ws.next(("w_in", 0, 16, 3616, 320))
                for m in range(2):
                    pb = bank()
                    wstat(w, Bwt, m * 128, 128, pb)
                    do_evac(m, ckvT[:, m, :], pb, BckvT[m])
                pb = bank()
                wstat(w, Bwt, 256, 64, pb)
                do_evac(0, kpeT[:, :], pb, BkpeT, M=64)
                sc.op("act", lambda e: e.activation(out=sqkpe[:], in_=kpeT[:], func=AF.Square),
                      reads=[BkpeT], writes=[Bsqkpe])
                sc.dma("act", dcs, cosT[:], cos_d[:, tcols], writes=[Bcs])
                sc.dma("act", dcs, sinT[:], sin_d[:, tcols], writes=[Bcs])
                prev[0] = prep(t)
            drain()
            bank_pool[0] = list(range(8))
            sc.barrier()

        if stop == "A":
            print("instructions", sc.n_ins, "waits", sc.n_wait)
            return nc

        with ExitStack() as Bk:
            NQB = S // 512
            dld = sc.dma_sem("p3const")
            Bk3 = Buf("p3const")

            def cload(name, ap, shape, dt=F32):
                t_ = sbuf(Bk, name, shape, dt)
                sc.dma("sp", dld, t_[:], ap, writes=[Bk3])
                return t_
            gain_bc = cload("gain_bc", gla_gain_d, (128, 1024))
            wa2c = [cload("wa2c%d" % i, wa2c_d[i], (32, 512)) for i in range(2)]
            ba = [cload("ba%d" % i, ba_d[i], (1, 512)) for i in range(2)]
            tri = {n: cload("tri" + n, tri_d[n], (128, 128)) for n in tri_d}
            mask = [cload("mask%d" % i, mask_d[i], (128, 512)) for i in range(2)]
            Sst = [[sbuf(Bk, "Sst%d%d" % (d_, h), (128, 256), F32) for h in range(4)] for d_ in range(2)]
            Sbf = [[sbuf(Bk, "Sbf%d%d" % (d_, h), (128, 256), BF16) for h in range(4)] for d_ in range(2)]
            BS = [[Buf() for h in range(4)] for d_ in range(2)]
            BSb = [[Buf() for h in range(4)] for d_ in range(2)]
            for d_ in range(2):
                for h in range(4):
                    sc.op("pool", lambda e: e.memset(Sst[d_][h][:], 0.0), writes=[BS[d_][h]])
                    sc.op("pool", lambda e: e.memset(Sbf[d_][h][:], 0.0), writes=[BSb[d_][h]])
            gin = {}
            Bgin = {}
            for nm, shp, dt in (("qT", (128, 512), F32), ("kT", (128, 512), F32), ("k", (128, 512), F32),
                                ("v", (128, 1024), BF16), ("ga", (32, 128), F32), ("of", (128, 1024), F32),
                                ("sr", (128, 1024), F32)):
                gin[nm] = [sbuf(Bk, "gin_%s%d" % (nm, i), shp, dt) for i in range(2)]
                Bgin[nm] = [Buf() for i in range(2)]
            dgl = [sc.dma_sem("gl%d" % i) for i in range(2)]
            tmp = {}
            Bt = {}
            for nm, shp, dt in (("la", (128, 512), F32), ("E1T", (128, 512), F32), ("E2T", (128, 512), F32),
                                ("E3", (128, 512), F32), ("qtT", (128, 512), BF16), ("ktT", (128, 512), BF16),
                                ("ks", (128, 512), BF16), ("ATm", (128, 512), BF16), ("o_sb", (128, 1024), F32),
                                ("gs", (128, 1024), F32), ("y_sb", (128, 1024), F32), ("yT", (128, 8, 128), BF16),
                                ("junk", (128, 256), F32), ("ss", (128, 4), F32)):
                tmp[nm] = sbuf(Bk, "t_" + nm, shp, dt)
                Bt[nm] = Buf("t_" + nm)
            dgs = sc.dma_sem("glstore")
            NU = 2 * NCH

            def gla_load(u):
                if u >= NU:
                    return
                dr = 0 if u < NCH else 1
                n = u if dr == 0 else (NU - 1 - u)
                sl = u % 2
                c0 = n * 128
                cc = slice(c0, c0 + 128)
                d = dgl[sl]
                sc.dma("sp", d, gin["qT"][sl][:].rearrange("p (h i) -> p h i", h=4),
                       gqT_s.rearrange("(h p) s -> p h s", p=128)[:, :, cc], reads=[Bscr["gqT"]], writes=[Bgin["qT"][sl]])
                sc.dma("sp", d, gin["kT"][sl][:].rearrange("p (h i) -> p h i", h=4),
                       gkT_s.rearrange("(h p) s -> p h s", p=128)[:, :, cc], reads=[Bscr["gkT"]], writes=[Bgin["kT"][sl]])
                sc.dma("sp", d, gin["k"][sl][:], gk_s[cc, :], reads=[Bscr["gk"]], writes=[Bgin["k"][sl]])
                sc.dma("sp", d, gin["v"][sl][:], gv_s[cc, :], reads=[Bscr["gv"]], writes=[Bgin["v"][sl]])
                sc.dma("sp", d, gin["ga"][sl][:], gaT_s[:, cc], reads=[Bscr["gaT"]], writes=[Bgin["ga"][sl]])

            def gla_load_of(u):
                if u >= NU or u < NCH:
                    return
                n = NU - 1 - u
                sl = u % 2
                cc = slice(n * 128, n * 128 + 128)
                d = dgl[sl]
                sc.dma("sp", d, gin["of"][sl][:], of_s[cc, :], reads=[Bscr["of"]], writes=[Bgin["of"][sl]])
                sc.dma("sp", d, gin["sr"][sl][:], gsr_s[cc, :], reads=[Bscr["gsr"]], writes=[Bgin["sr"][sl]])

            def gla_unit(u):
                dr = 0 if u < NCH else 1
                n = u if dr == 0 else (NU - 1 - u)
                sl = u % 2
                c0 = n * 128
                cc = slice(c0, c0 + 128)
                Ln = tri["LnF" if dr == 0 else "LnB"]
                Un = tri["UnF" if dr == 0 else "UnB"]
                la, E1T, E2T, E3 = tmp["la"], tmp["E1T"], tmp["E2T"], tmp["E3"]
                qtT, ktT, ks, ATm = tmp["qtT"], tmp["ktT"], tmp["ks"], tmp["ATm"]
                o_sb, gs, y_sb, yT = tmp["o_sb"], tmp["gs"], tmp["y_sb"], tmp["yT"]
                qT_in, kT_in, k_in, v_in, ga_in = (gin[x][sl] for x in ("qT", "kT", "k", "v", "ga"))
                sc.mm(lambda e: e.matmul(ps[:, 0, :], lhsT=ga_in[0:32, :], rhs=wa2c[dr][0:32, :], start=True, stop=False),
                      reads=[Bgin["ga"][sl], Bk3], writes=[Bps[0]], inc=False)
                sc.mm(lambda e: e.matmul(ps[:, 0, :], lhsT=ones_f[0:1, :], rhs=ba[dr][0:1, :], start=False, stop=True),
                      reads=[Bc, Bk3], writes=[Bps[0]], inc=True)
                sc.op("act", lambda e: e.activation(out=la[:], in_=ps[:, 0, :], func=AF.Exp, scale=-1.0),
                      reads=[Bps[0]], writes=[Bt["la"]])
                sc.op("act", lambda e: e.activation(out=la[:], in_=la[:], func=AF.Ln, bias=1.0),
                      reads=[Bt["la"]], writes=[Bt["la"]])
                yield
                for h in range(4):
                    sc.mm(lambda e: e.matmul(ps[:, 1, h * 128:(h + 1) * 128], lhsT=la[:, h * 128:(h + 1) * 128],
                                             rhs=Ln[:], start=True, stop=True),
                          reads=[Bt["la"], Bk3], writes=[Bps[1]], inc=(h == 3))
                sc.mm(lambda e: e.matmul(ps[:, 0, :], lhsT=Un[:], rhs=la[:], start=True, stop=True),
                      reads=[Bt["la"], Bk3], writes=[Bps[0]], inc=True)
                sc.op("act", lambda e: e.activation(out=E1T[:], in_=ps[:, 1, :], func=AF.Exp),
                      reads=[Bps[1]], writes=[Bt["E1T"]])
                sc.op("act", lambda e: e.activation(out=E2T[:], in_=ps[:, 1, :], func=AF.Exp, scale=-1.0),
                      reads=[Bps[1]], writes=[Bt["E2T"]])
                sc.op("act", lambda e: e.activation(out=E3[:], in_=ps[:, 0, :], func=AF.Exp),
                      reads=[Bps[0]], writes=[Bt["E3"]])
                sc.op("dve", lambda e: e.scalar_tensor_tensor(out=qtT[:], in0=qT_in[:], scalar=float(128.0 ** -0.5),
                                                              in1=E1T[:], op0=ALU.mult, op1=ALU.mult),
                      reads=[Bgin["qT"][sl], Bt["E1T"]], writes=[Bt["qtT"]])
                sc.op("pool", lambda e: e.tensor_tensor(out=ktT[:], in0=kT_in[:], in1=E2T[:], op=ALU.mult),
                      reads=[Bgin["kT"][sl], Bt["E2T"]], writes=[Bt["ktT"]])
                sc.op("pool", lambda e: e.tensor_tensor(out=ks[:], in0=k_in[:], in1=E3[:], op=ALU.mult),
                      reads=[Bgin["k"][sl], Bt["E3"]], writes=[Bt["ks"]])
                yield
                for h in range(4):
                    hs = slice(h * 128, (h + 1) * 128)
                    sc.mm(lambda e: e.matmul(ps[:, 1, hs], lhsT=ktT[:, hs], rhs=qtT[:, hs], start=True, stop=True),
                          reads=[Bt["ktT"], Bt["qtT"]], writes=[Bps[1]], inc=(h == 3))
                sc.op("dve", lambda e: e.tensor_tensor(out=ATm[:], in0=ps[:, 1, :], in1=mask[dr][:], op=ALU.mult),
                      reads=[Bps[1], Bk3], writes=[Bt["ATm"]])
                for h in range(4):
                    hs = slice(h * 128, (h + 1) * 128)
                    cs_ = slice((h % 2) * 256, (h % 2) * 256 + 256)
                    bk = h // 2
                    if h % 2 == 0 and bk == 1:
                        pass
                    sc.mm(lambda e: e.matmul(ps[:, bk, cs_], lhsT=ks[:, hs], rhs=v_in[:, h * 256:(h + 1) * 256],
                                             start=True, stop=True),
                          reads=[Bt["ks"], Bgin["v"][sl]], writes=[Bps[bk]], inc=(h % 2 == 1)) if bk == 0 else None
                yield
                for h in range(4):
                    hs = slice(h * 128, (h + 1) * 128)
                    cs_ = slice((h % 2) * 256, (h % 2) * 256 + 256)
                    bk = 2 + h // 2
                    sc.mm(lambda e: e.matmul(ps[:, bk, cs_], lhsT=ATm[:, hs], rhs=v_in[:, h * 256:(h + 1) * 256],
                                             start=True, stop=False),
                          reads=[Bt["ATm"], Bgin["v"][sl]], writes=[Bps[bk]], inc=False)
                    sc.mm(lambda e: e.matmul(ps[:, bk, cs_], lhsT=qtT[:, hs], rhs=Sbf[dr][h][:], start=False, stop=True),
                          reads=[Bt["qtT"], BSb[dr][h]], writes=[Bps[bk]], inc=(h % 2 == 1))
                for h in range(2, 4):
                    hs = slice(h * 128, (h + 1) * 128)
                    cs_ = slice((h % 2) * 256, (h % 2) * 256 + 256)
                    sc.mm(lambda e: e.matmul(ps[:, 1, cs_], lhsT=ks[:, hs], rhs=v_in[:, h * 256:(h + 1) * 256],
                                             start=True, stop=True),
                          reads=[Bt["ks"], Bgin["v"][sl]], writes=[Bps[1]], inc=(h == 3))
                lastc = 127 if dr == 0 else 0
                for h in range(4):
                    cs_ = slice((h % 2) * 256, (h % 2) * 256 + 256)
                    bk = h // 2
                    dec = E1T[:, h * 128 + lastc:h * 128 + lastc + 1]
                    sc.op("dve", lambda e: e.scalar_tensor_tensor(out=Sst[dr][h][:], in0=Sst[dr][h][:], scalar=dec,
                                                                  in1=ps[:, bk, cs_], op0=ALU.mult, op1=ALU.add),
                          reads=[BS[dr][h], Bt["E1T"], Bps[bk]], writes=[BS[dr][h]])
                    sc.op("act", lambda e: e.activation(out=Sbf[dr][h][:], in_=Sst[dr][h][:], func=AF.Copy),
                          reads=[BS[dr][h]], writes=[BSb[dr][h]])
                yield
                if dr == 0:
                    sc.op("act", lambda e: e.activation(out=o_sb[:, 0:512], in_=ps[:, 2, :], func=AF.Copy),
                          reads=[Bps[2]], writes=[Bt["o_sb"]])
                    sc.op("dve", lambda e: e.tensor_copy(out=o_sb[:, 512:1024], in_=ps[:, 3, :]),
                          reads=[Bps[3], Bt["o_sb"]], writes=[Bt["o_sb"]])
                    sc.dma("pool", dgs, of_s[cc, :], o_sb[:], reads=[Bt["o_sb"]], writes=[Bscr["of"]])
                    return
                of_in, sr_in = gin["of"][sl], gin["sr"][sl]
                sc.op("dve", lambda e: e.tensor_tensor(out=o_sb[:, 0:512], in0=ps[:, 2, :], in1=of_in[:, 0:512], op=ALU.add),
                      reads=[Bps[2], Bgin["of"][sl]], writes=[Bt["o_sb"]])
                sc.op("dve", lambda e: e.tensor_tensor(out=o_sb[:, 512:1024], in0=ps[:, 3, :], in1=of_in[:, 512:1024], op=ALU.add),
                      reads=[Bps[3], Bgin["of"][sl], Bt["o_sb"]], writes=[Bt["o_sb"]])
                sc.op("pool", lambda e: e.tensor_tensor(out=gs[:], in0=gain_bc[:], in1=sr_in[:], op=ALU.mult),
                      reads=[Bk3, Bgin["sr"][sl]], writes=[Bt["gs"]])
                for h in range(4):
                    sc.op("act", lambda e: e.activation(out=tmp["junk"][:], in_=o_sb[:, h * 256:(h + 1) * 256],
                                                        func=AF.Square, accum_out=tmp["ss"][:, h:h + 1]),
                          reads=[Bt["o_sb"]], writes=[Bt["junk"], Bt["ss"]])
                sc.op("act", lambda e: e.activation(out=tmp["ss"][:], in_=tmp["ss"][:], func=AF.Ln, scale=1.0 / 256, bias=EPS),
                      reads=[Bt["ss"]], writes=[Bt["ss"]])
                sc.op("act", lambda e: e.activation(out=tmp["ss"][:], in_=tmp["ss"][:], func=AF.Exp, scale=-0.5),
                      reads=[Bt["ss"]], writes=[Bt["ss"]])
                for h in range(4):
                    hv = slice(h * 256, (h + 1) * 256)
                    sc.op("dve", lambda e: e.scalar_tensor_tensor(out=y_sb[:, hv], in0=o_sb[:, hv], scalar=tmp["ss"][:, h:h + 1],
                                                                  in1=gs[:, hv], op0=ALU.mult, op1=ALU.mult),
                          reads=[Bt["o_sb"], Bt["ss"], Bt["gs"]], writes=[Bt["y_sb"]])
                yield
                for c in range(8):
                    bk = 2 + c // 4
                    sc.mm(lambda e: e.transpose(out=ps[:, bk, (c % 4) * 128:(c % 4 + 1) * 128],
                                                in_=y_sb[:, c * 128:(c + 1) * 128], identity=ident[:]),
                          reads=[Bt["y_sb"], Bc], writes=[Bps[bk]], inc=(c % 4 == 3))
                sc.op("act", lambda e: e.activation(out=yT[:, 0:4, :], in_=ps[:, 2, :].rearrange("p (c n) -> p c n", c=4),
                                                    func=AF.Copy), reads=[Bps[2]], writes=[Bt["yT"]])
                sc.op("dve", lambda e: e.tensor_copy(out=yT[:, 4:8, :], in_=ps[:, 3, :].rearrange("p (c n) -> p c n", c=4)),
                      reads=[Bps[3], Bt["yT"]], writes=[Bt["yT"]])
                sc.dma("pool", dgs, ymT_s.rearrange("(c p) s -> p c s", p=128)[:, 0:8, cc], yT[:],
                       reads=[Bt["yT"]], writes=[Bscr["ymT"]])

            Kn = [sbuf(Bk, "Kn%d" % i, (128, S), BF16) for i in range(2)]
            Kr = [sbuf(Bk, "Kr%d" % i, (128, S), BF16) for i in range(2)]
            Vh = [sbuf(Bk, "Vh%d" % i, (128, NCH, 128), BF16) for i in range(2)]
            BKV = [Buf() for i in range(2)]
            dkv = [sc.dma_sem("kv%d" % i) for i in range(2)]
            Qn = [sbuf(Bk, "Qn%d" % i, (128, 512), BF16) for i in range(2)]
            Qr = [sbuf(Bk, "Qr%d" % i, (128, 512), BF16) for i in range(2)]
            BQ = [Buf() for i in range(2)]
            for i in range(2):
                sc.op("pool", lambda e: e.memset(Kr[i][64:128, :], 0.0), writes=[BKV[i]])
                sc.op("pool", lambda e: e.memset(Qr[i][64:128, :], 0.0), writes=[BQ[i]])
            dq = [sc.dma_sem("q%d" % i) for i in range(2)]
            PT = [sbuf(Bk, "PT%d" % i, (128, 512), BF16) for i in range(4)]
            BPT = [Buf() for i in range(4)]
            rsum = sbuf(Bk, "rsum", (128, 512), F32)
            Brsum = Buf()
            pacc = sbuf(Bk, "pacc", (128, 512), F32)
            Bpacc = Buf()
            yo = [sbuf(Bk, "yo%d" % i, (128, 512), BF16) for i in range(2)]
            Byo = [Buf() for i in range(2)]
            dyo = [sc.dma_sem("yo%d" % i) for i in range(2)]
            NMU = 8 * NQB
            mscale = float(192.0 ** -0.5)

            def kv_load(h):
                if h >= 8:
                    return
                s_ = h % 2
                sc.dma("sp", dkv[s_], Kn[s_][:], KnT_s[h * 128:(h + 1) * 128, :], reads=[Bscr["KnT"]], writes=[BKV[s_]])
                sc.dma("sp", dkv[s_], Kr[s_][0:64, :], KrT_s[h * 64:(h + 1) * 64, :], reads=[Bscr["KrT"]], writes=[BKV[s_]])
                sc.dma("sp", dkv[s_], Vh[s_][:], V_s.rearrange("(g p) c -> p g c", p=128)[:, :, h * 128:(h + 1) * 128],
                       reads=[Bscr["V"]], writes=[BKV[s_]])

            def q_load(u):
                if u >= NMU:
                    return
                h, qb = u // NQB, u % NQB
                s_ = u % 2
                sc.dma("sp", dq[s_], Qn[s_][:], QnT_s[h * 128:(h + 1) * 128, qb * 512:(qb + 1) * 512],
                       reads=[Bscr["QnT"]], writes=[BQ[s_]])
                sc.dma("sp", dq[s_], Qr[s_][0:64, :], QrT_s[h * 64:(h + 1) * 64, qb * 512:(qb + 1) * 512],
                       reads=[Bscr["QrT"]], writes=[BQ[s_]])

            def mla_unit(u, gen):
                h, qb = u // NQB, u % NQB
                ks_ = h % 2
                qs = u % 2
                if qb == 0:
                    kv_load(h + 1)
                q_load(u + 1)
                step = max(1, NCH // 8)

                def score(kc):
                    bk = 4 + kc % 2
                    kcs = slice(kc * 128, (kc + 1) * 128)
                    sc.mm(lambda e: e.matmul(ps[:, bk, :], lhsT=Kn[ks_][:, kcs], rhs=Qn[qs][:], start=True, stop=False),
                          reads=[BKV[ks_], BQ[qs]], writes=[Bps[bk]], inc=False)
                    sc.mm(lambda e: e.matmul(ps[:, bk, :], lhsT=Kr[ks_][:, kcs], rhs=Qr[qs][:], start=False, stop=True),
                          reads=[BKV[ks_], BQ[qs]], writes=[Bps[bk]], inc=True)
                    p_ = kc % 4
                    sc.op("act", lambda e: e.activation(out=PT[p_][:], in_=ps[:, bk, :], func=AF.Exp, scale=mscale),
                          reads=[Bps[bk]], writes=[BPT[p_]])
                score(0)
                if NCH > 1:
                    score(1)
                for kc in range(NCH):
                    p_ = kc % 4
                    if kc + 2 < NCH:
                        score(kc + 2)
                    sc.mm(lambda e: e.matmul(ps[:, 6, :], lhsT=Vh[ks_][:, kc, :], rhs=PT[p_][:], start=(kc == 0),
                                             stop=(kc == NCH - 1)),
                          reads=[BKV[ks_], BPT[p_]], writes=[Bps[6]], inc=(kc == NCH - 1))
                    sc.mm(lambda e: e.matmul(ps[:, 7, :], lhsT=ones_bf[:], rhs=PT[p_][:], start=(kc == 0),
                                             stop=(kc == NCH - 1)),
                          reads=[Bc, BPT[p_]], writes=[Bps[7]], inc=True)
                    if gen is not None and kc % step == step - 1:
                        next(gen, None)
                sc.op("act", lambda e: e.activation(out=rsum[:], in_=ps[:, 7, :], func=AF.Ln), reads=[Bps[7]], writes=[Brsum])
                sc.op("act", lambda e: e.activation(out=rsum[:], in_=rsum[:], func=AF.Exp, scale=-1.0), reads=[Brsum], writes=[Brsum])
                ys = u % 2
                sc.op("dve", lambda e: e.tensor_tensor(out=yo[ys][:], in0=ps[:, 6, :], in1=rsum[:], op=ALU.mult),
                      reads=[Bps[6], Brsum], writes=[Byo[ys]])
                sc.dma("pool", dyo[ys], ymT_s[1024 + h * 128:1024 + (h + 1) * 128, qb * 512:(qb + 1) * 512], yo[ys][:],
                       reads=[Byo[ys]], writes=[Bscr["ymT"]])

            gla_load(0)
            kv_load(0)
            q_load(0)
            nun = max(NU, NMU)
            for u in range(nun):
                gen = None
                if u < NU:
                    gla_load(u + 1)
                    gen = gla_unit(u)
                if u < NMU:
                    mla_unit(u, gen)
                if gen is not None:
                    for _ in gen:
                        pass
                gla_load_of(u + 1)
            sc.barrier()
        if stop == "B":
            print("instructions", sc.n_ins, "waits", sc.n_wait)
            return nc

        h2T_s = dscr("s_h2T", (D, S), F32)
        Bh2 = Buf("scr_h2T", dram=True)
        xscale = float(512.0 ** -0.5)
        with ExitStack() as C:
            ws = WStream(C)
            ws.plan += [("xattn_wk", 0, 16, cb * 512, 512) for cb in range(4)]
            ws.plan += [("xattn_wv", 0, 16, cb * 512, 512) for cb in range(4)]
            for t in range(NT):
                ws.plan += [("w_out", 0, 16, cb * 512, 512) for cb in range(4)]
                ws.plan += [("xattn_wq", 0, 16, cb * 512, 512) for cb in range(4)]
                ws.plan += [("xattn_wo", 0, 16, cb * 512, 512) for cb in range(4)]
            kmn = sbuf(C, "kmn", (128, 16, 256), BF16)
            Bkmn = [Buf() for _ in range(4)]
            vm = sbuf(C, "vm", (128, 2, 2048), BF16)
            Bvm = Buf()
            sqr = [sbuf(C, "sqr4_%d" % i, (128, 512), BF16) for i in range(4)]
            Bsqr = [Buf() for _ in range(4)]
            rs = sbuf(C, "rs4", (128, 512), F32)
            Brs = Buf()
            with ExitStack() as M:
                memT = sbuf(M, "memT", (128, 16, 256), F32)
                BmemT = [Buf() for _ in range(16)]
                memn = sbuf(M, "memn", (128, 16, 256), BF16)
                Bmemn = [Buf() for _ in range(16)]
                kmT = sbuf(M, "kmT", (128, 16, 256), F32)
                BkmT = [Buf() for _ in range(16)]
                stg = [sbuf(M, "mstg%d" % i, (128, 2048), F32) for i in range(2)]
                Bstg = [Buf() for _ in range(2)]
                dstg = [sc.dma_sem("mstg%d" % i) for i in range(2)]
                load_T(lambda g: mem_d[g * 128:(g + 1) * 128, :], 2, memT, BmemT, stg, Bstg, dstg, 0)
                norm16(memT, BmemT, "g_mem", memn, Bmemn, sqr, Bsqr, rs, Brs, N=256)
                for cb in range(4):
                    w, Bwt = ws.next(("xattn_wk", 0, 16, cb * 512, 512))
                    for m in range(4):
                        ob = cb * 4 + m
                        pb = bank()
                        for k in range(16):
                            sc.mm(lambda e: e.matmul(ps[:, pb, 0:256], lhsT=w[:, k, m * 128:(m + 1) * 128],
                                                     rhs=memn[:, k, :], start=(k == 0), stop=(k == 15)),
                                  reads=[Bwt, Bmemn[k]], writes=[Bps[pb]], inc=(k == 15))
                        sc.op("act", lambda e: e.activation(out=kmT[:, ob, :], in_=ps[:, pb, 0:256], func=AF.Copy),
                              reads=[Bps[pb]], writes=[BkmT[ob]])
                    pb = bank()
                    for c in range(4):
                        q = sqr[c]
                        sc.op("act", lambda e: e.activation(out=q[:, 0:256], in_=kmT[:, cb * 4 + c, :], func=AF.Square),
                              reads=[BkmT[cb * 4 + c]], writes=[Bsqr[c]])
                        sc.mm(lambda e: e.matmul(ps[:, pb, 0:256], lhsT=ones_bf[:], rhs=q[:, 0:256], start=(c == 0),
                                                 stop=(c == 3)), reads=[Bsqr[c], Bc], writes=[Bps[pb]], inc=True)
                    sc.op("act", lambda e: e.activation(out=rs[:, 0:256], in_=ps[:, pb, 0:256], func=AF.Ln,
                                                        scale=1.0 / 512, bias=EPS), reads=[Bps[pb]], writes=[Brs])
                    sc.op("act", lambda e: e.activation(out=rs[:, 0:256], in_=rs[:, 0:256], func=AF.Exp, scale=-0.5),
                          reads=[Brs], writes=[Brs])
                    for c in range(4):
                        sc.op("dve", lambda e: e.scalar_tensor_tensor(
                            out=kmn[:, cb * 4 + c, :], in0=kmT[:, cb * 4 + c, :], scalar=cs["g_xk"][:, c:c + 1],
                            in1=rs[:, 0:256], op0=ALU.mult, op1=ALU.mult),
                            reads=[BkmT[cb * 4 + c], Brs, Bc], writes=[Bkmn[cb]])
                for cb in range(4):
                    w, Bwt = ws.next(("xattn_wv", 0, 16, cb * 512, 512))
                    for mg in range(2):
                        pb = bank()
                        for k in range(16):
                            sc.mm(lambda e: e.matmul(ps[:, pb, :], lhsT=memn[:, k, mg * 128:(mg + 1) * 128],
                                                     rhs=w[:, k, :], start=(k == 0), stop=(k == 15)),
                                  reads=[Bwt, Bmemn[k]], writes=[Bps[pb]], inc=(k == 15))
                        sc.op("act", lambda e: e.activation(out=vm[:, mg, cb * 512:(cb + 1) * 512], in_=ps[:, pb, :],
                                                            func=AF.Copy), reads=[Bps[pb]], writes=[Bvm])
                sc.barrier()
            xT = sbuf(C, "xT4", (128, 16, 512), F32)
            BxT = [Buf("xT%d" % k) for k in range(16)]
            dxT = sc.dma_sem("xT4load")
            ym = sbuf(C, "ym", (128, 16, 512), BF16)
            Bym = Buf()
            dym = sc.dma_sem("ymload")
            xn = sbuf(C, "xn4", (128, 16, 512), BF16)
            Bxn = [Buf("xn%d" % k) for k in range(16)]
            ox = sbuf(C, "ox", (128, 16, 512), BF16)
            Box = [Buf() for _ in range(16)]
            qhs = [sbuf(C, "qh%d" % i, (128, 4, 512), F32) for i in range(2)]
            Bqhs = [[Buf() for _ in range(4)] for i in range(2)]
            qn = sbuf(C, "qn", (128, 4, 512), BF16)
            Bqn = [Buf() for _ in range(4)]
            PTx = [sbuf(C, "PTx%d" % i, (128, 512), BF16) for i in range(2)]
            BPTx = [Buf() for _ in range(2)]
            rsx = sbuf(C, "rsx", (128, 512), F32)
            Brsx = Buf()
            dh2 = sc.dma_sem("h2store")
            def load_ym(t):
                if t < NT:
                    sc.dma("act", dym, ym[:], ymT_s.rearrange("(k p) s -> p k s", p=128)[:, :, t * 512:(t + 1) * 512],
                           reads=[Bscr["ymT"]], writes=[Bym])
            load_ym(0)
            for t in range(NT):
                tcols = slice(t * 512, (t + 1) * 512)
                sc.dma("act", dxT, xT[:], hT_s.rearrange("(k p) s -> p k s", p=128)[:, :, tcols],
                       reads=[Bscr["hT"]], writes=BxT)

                def proj_add(wname, src, Bsrc_of, store=False):
                    for cb in range(4):
                        w, Bwt = ws.next((wname, 0, 16, cb * 512, 512))
                        for m in range(4):
                            ob = cb * 4 + m
                            pb = bank()
                            for k in range(16):
                                sc.mm(lambda e: e.matmul(ps[:, pb, :], lhsT=w[:, k, m * 128:(m + 1) * 128],
                                                         rhs=src[:, k, :], start=(k == 0), stop=(k == 15)),
                                      reads=[Bwt, Bsrc_of(k)], writes=[Bps[pb]], inc=(k == 15))
                            sc.op("dve", lambda e: e.tensor_tensor(out=xT[:, ob, :], in0=ps[:, pb, :], in1=xT[:, ob, :],
                                                                   op=ALU.add),
                                  reads=[Bps[pb], BxT[ob]], writes=[BxT[ob]])
                        if store:
                            sc.dma("sp", dh2, h2T_s.rearrange("(k p) s -> p k s", p=128)[:, cb * 4:(cb + 1) * 4, tcols],
                                   xT[:, cb * 4:(cb + 1) * 4, :], reads=BxT[cb * 4:(cb + 1) * 4], writes=[Bh2])
                proj_add("w_out", ym, lambda k: Bym)
                load_ym(t + 1)
                if stop == "C1":
                    sc.dma("sp", dh2, h2T_s.rearrange("(k p) s -> p k s", p=128)[:, :, tcols], xT[:],
                           reads=BxT, writes=[Bh2])
                    continue
                norm16(xT, BxT, "g_xn", xn, Bxn, sqr, Bsqr, rs, Brs)
                def qproj(h):
                    w, Bwt = ws.next(("xattn_wq", 0, 16, h * 512, 512))
                    qh_ = qhs[h % 2]
                    Bqh_ = Bqhs[h % 2]
                    for m in range(4):
                        pb = bank()
                        for k in range(16):
                            sc.mm(lambda e: e.matmul(ps[:, pb, :], lhsT=w[:, k, m * 128:(m + 1) * 128], rhs=xn[:, k, :],
                                                     start=(k == 0), stop=(k == 15)),
                                  reads=[Bwt, Bxn[k]], writes=[Bps[pb]], inc=(k == 15))
                        if m % 2 == 0:
                            sc.op("act", lambda e: e.activation(out=qh_[:, m, :], in_=ps[:, pb, :], func=AF.Copy),
                                  reads=[Bps[pb]], writes=[Bqh_[m]])
                        else:
                            sc.op("dve", lambda e: e.tensor_copy(out=qh_[:, m, :], in_=ps[:, pb, :]),
                                  reads=[Bps[pb]], writes=[Bqh_[m]])
                qproj(0)
                for h in range(4):
                    if h + 1 < 4:
                        qproj(h + 1)
                    qh = qhs[h % 2]
                    Bqh = Bqhs[h % 2]
                    pb = bank()
                    for c in range(4):
                        q = sqr[c]
                        sc.op("act", lambda e: e.activation(out=q[:], in_=qh[:, c, :], func=AF.Square),
                              reads=[Bqh[c]], writes=[Bsqr[c]])
                        sc.mm(lambda e: e.matmul(ps[:, pb, :], lhsT=ones_bf[:], rhs=q[:], start=(c == 0), stop=(c == 3)),
                              reads=[Bsqr[c], Bc], writes=[Bps[pb]], inc=True)
                    rstd_from(pb, 512.0, rs, Brs)
                    for c in range(4):
                        sc.op("dve", lambda e: e.scalar_tensor_tensor(
                            out=qn[:, c, :], in0=qh[:, c, :], scalar=cs["g_xq"][:, c:c + 1], in1=rs[:],
                            op0=ALU.mult, op1=ALU.mult), reads=[Bqh[c], Brs, Bc], writes=[Bqn[c]])
                    for mc in range(2):
                        pb = bank()
                        for c in range(4):
                            sc.mm(lambda e: e.matmul(ps[:, pb, :], lhsT=kmn[:, h * 4 + c, mc * 128:(mc + 1) * 128],
                                                     rhs=qn[:, c, :], start=(c == 0), stop=(c == 3)),
                                  reads=[Bkmn[h], Bqn[c]], writes=[Bps[pb]], inc=(c == 3))
                        sc.op("act", lambda e: e.activation(out=PTx[mc][:], in_=ps[:, pb, :], func=AF.Exp, scale=xscale),
                              reads=[Bps[pb]], writes=[BPTx[mc]])
                    pb = bank()
                    for mc in range(2):
                        sc.mm(lambda e: e.matmul(ps[:, pb, :], lhsT=ones_bf[:], rhs=PTx[mc][:], start=(mc == 0), stop=(mc == 1)),
                              reads=[BPTx[mc], Bc], writes=[Bps[pb]], inc=(mc == 1))
                    sc.op("act", lambda e: e.activation(out=rsx[:], in_=ps[:, pb, :], func=AF.Ln), reads=[Bps[pb]], writes=[Brsx])
                    sc.op("act", lambda e: e.activation(out=rsx[:], in_=rsx[:], func=AF.Exp, scale=-1.0), reads=[Brsx], writes=[Brsx])
                    for c2 in range(4):
                        pb = bank()
                        for mc in range(2):
                            sc.mm(lambda e: e.matmul(ps[:, pb, :], lhsT=vm[:, mc, h * 512 + c2 * 128:h * 512 + (c2 + 1) * 128],
                                                     rhs=PTx[mc][:], start=(mc == 0), stop=(mc == 1)),
                                  reads=[Bvm, BPTx[mc]], writes=[Bps[pb]], inc=(mc == 1))
                        sc.op("dve", lambda e: e.tensor_tensor(out=ox[:, h * 4 + c2, :], in0=ps[:, pb, :], in1=rsx[:],
                                                               op=ALU.mult),
                              reads=[Bps[pb], Brsx], writes=[Box[h * 4 + c2]])
                proj_add("xattn_wo", ox, lambda k: Box[k], store=True)
            sc.barrier()
        if stop in ("C", "C1"):
            print("instructions", sc.n_ins, "waits", sc.n_wait)
            return nc

        with ExitStack() as E:
            ws = WStream(E)
            for t in range(NT):
                ws.plan += plan_ffn("ffn2")
            xT = sbuf(E, "xT5", (128, 16, 512), F32)
            BxT = [Buf("xT%d" % k) for k in range(16)]
            dxT = sc.dma_sem("xT5load")
            xn = sbuf(E, "xn5", (128, 16, 512), BF16)
            Bxn = [Buf("xn%d" % k) for k in range(16)]
            hid = sbuf(E, "hid5", (128, 43, 512), BF16)
            Bhid = [Buf("hid%d" % k) for k in range(43)]
            ost = [sbuf(E, "ost%d" % i, (128, 2048), F32) for i in range(2)]
            Bost = [Buf() for _ in range(2)]
            dost = [sc.dma_sem("ost%d" % i) for i in range(2)]
            sgt = [sbuf(E, "sgt5_%d" % i, (128, 512), F32) for i in range(3)]
            Bsgt = [Buf() for _ in range(3)]
            sqr = [sbuf(E, "sqr5_%d" % i, (128, 512), BF16) for i in range(4)]
            Bsqr = [Buf() for _ in range(4)]
            rs = sbuf(E, "rs5", (128, 512), F32)
            Brs = Buf()
            By = Buf("y", dram=True)
            ocnt = 0
            for t in range(NT):
                tcols = slice(t * 512, (t + 1) * 512)
                sc.dma("act", dxT, xT[:], h2T_s.rearrange("(k p) s -> p k s", p=128)[:, :, tcols],
                       reads=[Bh2], writes=BxT)
                norm16(xT, BxT, "g_ffn2", xn, Bxn, sqr, Bsqr, rs, Brs)
                ffn(ws, "ffn2", xn, Bxn, xT, BxT, hid, Bhid, sgt, Bsgt)
                for tg in range(4):
                    o_ = ocnt % 2
                    ocnt += 1
                    for kq in range(4):
                        pb = bank()
                        for c in range(4):
                            k = kq * 4 + c
                            sc.mm(lambda e: e.transpose(out=ps[:, pb, c * 128:(c + 1) * 128],
                                                        in_=xT[:, k, tg * 128:(tg + 1) * 128], identity=ident[:]),
                                  reads=[BxT[k], Bc], writes=[Bps[pb]], inc=(c == 3))
                        if kq % 2 == 0:
                            sc.op("act", lambda e: e.activation(out=ost[o_][:, kq * 512:(kq + 1) * 512], in_=ps[:, pb, :],
                                                                func=AF.Copy), reads=[Bps[pb]], writes=[Bost[o_]])
                        else:
                            sc.op("dve", lambda e: e.tensor_copy(out=ost[o_][:, kq * 512:(kq + 1) * 512], in_=ps[:, pb, :]),
                                  reads=[Bps[pb]], writes=[Bost[o_]])
                    r0 = t * 512 + tg * 128
                    sc.dma("sp", dost[o_], y_d[r0:r0 + 128, :], ost[o_][:], reads=[Bost[o_]], writes=[By])
            sc.barrier()
        print("instructions", sc.n_ins, "waits", sc.n_wait)
    return nc


def _consts(S):
    c = {}
    c["c_ident"] = np.eye(128, dtype=np.float32)
    ps = np.zeros((64, 64), np.float32)
    for m in range(32):
        ps[m + 32, m] = -1.0
        ps[m, m + 32] = 1.0
    c["c_pswap"] = ps
    inv = (np.float32(10000.0) ** (-np.arange(32, dtype=np.float32) / np.float32(32))).astype(np.float32)
    ang = (np.arange(S, dtype=np.float32)[None, :] * inv[:, None]).astype(np.float32)
    c["c_cos"] = np.concatenate([np.cos(ang), np.cos(ang)], 0).astype(np.float32)
    c["c_sin"] = np.concatenate([np.sin(ang), np.sin(ang)], 0).astype(np.float32)
    j = np.arange(128)[:, None]
    i = np.arange(128)[None, :]
    sc = np.float32(-1.0 / 16.0)
    c["c_LnF"] = (j <= i).astype(np.float32) * sc
    c["c_UnF"] = (j > i).astype(np.float32) * sc
    c["c_LnB"] = (j >= i).astype(np.float32) * sc
    c["c_UnB"] = (j < i).astype(np.float32) * sc
    c["c_maskF"] = np.tile((j <= i).astype(np.float32), (1, 4))
    c["c_maskB"] = np.tile((j >= i).astype(np.float32), (1, 4))
    return c


def _colmajor(v, n):
    return np.ascontiguousarray(np.asarray(v, np.float32).reshape(n, 128).T)


def _shared_inputs(inp, S):
    m = dict(_consts(S))
    for n in WSHAPES:
        m[n] = np.ascontiguousarray(np.asarray(inp[n], np.float32)[0])
    m["g_ffn1"] = _colmajor(inp["ffn1_norm"][0], 16)
    m["g_mix"] = _colmajor(inp["mix_norm"][0], 16)
    m["g_xn"] = _colmajor(inp["xattn_norm"][0], 16)
    m["g_mem"] = _colmajor(inp["mem_norm"][0], 16)
    m["g_ffn2"] = _colmajor(inp["ffn2_norm"][0], 16)
    m["g_mq"] = _colmajor(inp["mla_q_norm"][0], 4)
    m["g_mkv"] = _colmajor(inp["mla_kv_norm"][0], 2)
    for nm, src in (("g_qq", "mla_qk_q_norm"), ("g_qk", "mla_qk_k_norm")):
        g = np.zeros((128, 2), np.float32)
        v = np.asarray(inp[src], np.float32)[0]
        g[:, 0] = v[0:128]
        g[0:64, 1] = v[128:192]
        m[nm] = g
    m["g_xq"] = _colmajor(inp["xattn_q_norm"][0], 4)
    m["g_xk"] = _colmajor(inp["xattn_k_norm"][0], 4)
    m["gla_gain_bc"] = np.ascontiguousarray(
        np.broadcast_to(np.asarray(inp["gla_out_norm"], np.float32)[0][None, :], (128, 1024)))
    z = np.zeros((16, 512), np.float32)
    m["wa2c_f"] = np.concatenate([np.asarray(inp["gla_wa2_fwd"], np.float32)[0], z], 0)
    m["wa2c_b"] = np.concatenate([z, np.asarray(inp["gla_wa2_bwd"], np.float32)[0]], 0)
    m["ba_f"] = np.asarray(inp["gla_ba_fwd"], np.float32).reshape(1, 512)
    m["ba_b"] = np.asarray(inp["gla_ba_bwd"], np.float32).reshape(1, 512)
    return m


def kernel(**inputs):
    S = SEQ
    xs = [np.asarray(inputs["x_prompt"], np.float32)[b] for b in range(2)] + \
         [np.asarray(inputs["x_sample"], np.float32)[b] for b in range(4)]
    ms = [np.asarray(inputs["mem_prompt"], np.float32)[b] for b in range(2)] + \
         [np.asarray(inputs["mem_sample"], np.float32)[b] for b in range(4)]
    shared = _shared_inputs(inputs, S)
    nc = build(S)
    core_seq = {0: 0, 1: 1, 2: 2, 4: 3, 5: 4, 6: 5}
    zx = np.zeros((S, D), np.float32)
    zm = np.zeros((MEM, D), np.float32)
    in_maps = []
    for c in range(8):
        m = dict(shared)
        if c in core_seq:
            m["x"] = np.ascontiguousarray(xs[core_seq[c]])
            m["mem"] = np.ascontiguousarray(ms[core_seq[c]])
        else:
            m["x"] = zx
            m["mem"] = zm
        in_maps.append(m)
    res = run_bass_kernel_spmd(nc, in_maps, core_ids=list(range(8)))
    inv = {q: c for c, q in core_seq.items()}
    ys = [np.asarray(res.results[inv[q]]["y"], np.float32) for q in range(6)]
    return (np.stack(ys[0:2], 0), np.stack(ys[2:6], 0))
```

```python
import numpy as np
from contextlib import ExitStack
import concourse.bass as bass
import concourse.mybir as mybir
from concourse.bass_utils import run_bass_kernel_spmd

F32 = mybir.dt.float32
BF16 = mybir.dt.bfloat16
AF = mybir.ActivationFunctionType
ALU = mybir.AluOpType

D = 2048
DFF = 5504
DIN = 3936
EPS = 1e-6
SEQ = 4096
MEM = 256

WSHAPES = {
    "ffn1_w_gate": (D, DFF), "ffn1_w_up": (D, DFF), "ffn1_w_down": (DFF, D),
    "w_in": (D, DIN), "mla_w_uq": (512, 1536), "mla_w_ukv": (256, 2048),
    "w_out": (D, D), "xattn_wq": (D, D), "xattn_wk": (D, D), "xattn_wv": (D, D),
    "xattn_wo": (D, D),
    "ffn2_w_gate": (D, DFF), "ffn2_w_up": (D, DFF), "ffn2_w_down": (DFF, D),
}
W_EARLY = ["ffn1_w_gate", "ffn1_w_up", "ffn1_w_down", "w_in", "mla_w_uq", "mla_w_ukv"]
W_LATE = ["w_out", "xattn_wk", "xattn_wv", "xattn_wq", "xattn_wo",
          "ffn2_w_gate", "ffn2_w_up", "ffn2_w_down"]


class Buf:
    __slots__ = ("name", "w", "r", "dram")

    def __init__(self, name="", dram=False):
        self.name = name
        self.w = {}
        self.r = {}
        self.dram = dram


class DmaSem:
    __slots__ = ("sem", "val", "key")

    def __init__(self, sem, key):
        self.sem = sem
        self.val = 0
        self.key = key


class Sched:
    def __init__(self, nc, stack):
        self.nc = nc
        self.stack = stack
        self.eng = {"pe": nc.tensor, "act": nc.scalar, "dve": nc.vector,
                    "pool": nc.gpsimd, "sp": nc.sync}
        self.esem = {}
        self.ecnt = {}
        self.known = {}
        self.nsem = 0
        self.dsems = []
        self.dmap = {}
        for e in self.eng:
            self.esem[e] = self._sem("e_" + e)
            self.ecnt[e] = 0
            self.known[e] = {}
        self.n_wait = 0
        self.n_ins = 0

    def _sem(self, name):
        self.nsem += 1
        return self.stack.enter_context(self.nc.semaphore("%s_%d" % (name, self.nsem)))

    def dma_sem(self, name):
        s = self._sem("d_" + name)
        d = DmaSem(s, "d_%s_%d" % (name, self.nsem))
        self.dsems.append(d)
        self.dmap[d.key] = d
        return d

    def _wait_for(self, e, reads, writes, skip_waw=None):
        need = {}
        own = "e_" + e
        for b in reads:
            for k, sv in b.w.items():
                if k not in need or need[k][1] < sv[1]:
                    need[k] = sv
        for b in writes:
            for k, sv in b.w.items():
                if k == skip_waw or (b.dram and k in self.dmap):
                    continue
                if k not in need or need[k][1] < sv[1]:
                    need[k] = sv
            for k, sv in b.r.items():
                if k == own:
                    continue
                if k not in need or need[k][1] < sv[1]:
                    need[k] = sv
        kn = self.known[e]
        for k, (s, v) in need.items():
            if k == own and e == "pe":
                continue
            if kn.get(k, 0) >= v:
                continue
            if k in self.dmap:
                v = self.dmap[k].val
            self.eng[e].wait_ge(s, v)
            self.n_wait += 1
            kn[k] = v

    def op(self, e, fn, reads=(), writes=()):
        self._wait_for(e, reads, writes)
        ins = fn(self.eng[e])
        self.n_ins += 1
        self.ecnt[e] += 1
        ins.then_inc(self.esem[e], 1)
        key = "e_" + e
        ev = (self.esem[e], self.ecnt[e])
        for b in reads:
            b.r[key] = ev
        for b in writes:
            b.w = {key: ev}
            b.r = {}
        return ins

    def mm(self, fn, reads=(), writes=(), inc=True):
        self._wait_for("pe", reads, writes)
        ins = fn(self.eng["pe"])
        self.n_ins += 1
        key = "e_pe"
        if inc:
            self.ecnt["pe"] += 1
            ins.then_inc(self.esem["pe"], 1)
        ev = (self.esem["pe"], self.ecnt["pe"] + (0 if inc else 1))
        for b in reads:
            b.r[key] = ev
        for b in writes:
            b.w = {key: ev}
            b.r = {}
        return ins

    def dma(self, e, ds, out, in_, reads=(), writes=(), **kw):
        self._wait_for(e, reads, writes, skip_waw=ds.key)
        ins = self.eng[e].dma_start(out=out, in_=in_, **kw)
        self.n_ins += 1
        ds.val += 16
        ins.then_inc(ds.sem, 16)
        ev = (ds.sem, ds.val)
        for b in reads:
            b.r[ds.key] = ev
        for b in writes:
            if b.dram:
                b.w[ds.key] = ev
            else:
                b.w = {ds.key: ev}
                b.r = {}
        return ins

    def barrier(self):
        for e in self.eng:
            kn = self.known[e]
            for f in self.eng:
                if f == "sp":
                    continue
                k = "e_" + f
                v = self.ecnt[f]
                if v > 0 and kn.get(k, 0) < v and not (f == e):
                    self.eng[e].wait_ge(self.esem[f], v)
                    kn[k] = v
            for d in self.dsems:
                if d.val > 0 and kn.get(d.key, 0) < d.val:
                    self.eng[e].wait_ge(d.sem, d.val)
                    kn[d.key] = d.val


def build(S=SEQ, dbg=False, stop=None):
    NT = S // 512
    NCH = S // 128
    nc = bass.Bass("TRN2", target_bir_lowering=False)

    def din(name, shape, dt=F32):
        return nc.dram_tensor(name, list(shape), dt, kind="ExternalInput").ap()

    def dscr(name, shape, dt):
        return nc.dram_tensor(name, list(shape), dt,
                              kind="ExternalOutput" if dbg else "Internal").ap()

    x_d = din("x", (S, D))
    mem_d = din("mem", (MEM, D))
    y_d = nc.dram_tensor("y", [S, D], F32, kind="ExternalOutput").ap()
    w32 = {n: din(n, sh) for n, sh in WSHAPES.items()}
    wbf = {n: nc.dram_tensor("bf_" + n, list(sh), BF16, kind="Internal").ap()
           for n, sh in WSHAPES.items()}
    wv = {n: wbf[n].rearrange("(kc p) c -> p kc c", p=128) for n in WSHAPES}
    cst = {
        "ident": din("c_ident", (128, 128)),
        "g_ffn1": din("g_ffn1", (128, 16)), "g_mix": din("g_mix", (128, 16)),
        "g_xn": din("g_xn", (128, 16)), "g_mem": din("g_mem", (128, 16)),
        "g_ffn2": din("g_ffn2", (128, 16)),
        "g_mq": din("g_mq", (128, 4)), "g_mkv": din("g_mkv", (128, 2)),
        "g_qq": din("g_qq", (128, 2)), "g_qk": din("g_qk", (128, 2)),
        "g_xq": din("g_xq", (128, 4)), "g_xk": din("g_xk", (128, 4)),
        "pswap": din("c_pswap", (64, 64)),
    }
    cos_d = din("c_cos", (64, S))
    sin_d = din("c_sin", (64, S))
    gla_gain_d = din("gla_gain_bc", (128, 1024))
    wa2c_d = [din("wa2c_f", (32, 512)), din("wa2c_b", (32, 512))]
    ba_d = [din("ba_f", (1, 512)), din("ba_b", (1, 512))]
    tri_d = {n: din("c_" + n, (128, 128)) for n in ["LnF", "UnF", "LnB", "UnB"]}
    mask_d = [din("c_maskF", (128, 512)), din("c_maskB", (128, 512))]

    hT_s = dscr("s_hT", (D, S), F32)
    gqT_s = dscr("s_gqT", (512, S), F32)
    gkT_s = dscr("s_gkT", (512, S), F32)
    gk_s = dscr("s_gk", (S, 512), F32)
    gv_s = dscr("s_gv", (S, 1024), BF16)
    gsr_s = dscr("s_gsr", (S, 1024), F32)
    gaT_s = dscr("s_gaT", (32, S), F32)
    QnT_s = dscr("s_QnT", (1024, S), BF16)
    QrT_s = dscr("s_QrT", (512, S), BF16)
    KnT_s = dscr("s_KnT", (1024, S), BF16)
    KrT_s = dscr("s_KrT", (512, S), BF16)
    V_s = dscr("s_V", (S, 1024), BF16)
    of_s = dscr("s_of", (S, 1024), F32)
    ymT_s = dscr("s_ymT", (D, S), BF16)

    with ExitStack() as G:
        sc = Sched(nc, G)

        sb_cnt = [0]

        def sbuf(stack, name, shape, dt):
            sb_cnt[0] += 1
            return stack.enter_context(nc.sbuf_tensor("%s_%d" % (name, sb_cnt[0]), list(shape), dt))

        ps = G.enter_context(nc.psum_tensor("ps", [128, 8, 512], F32))
        Bps = [Buf("ps%d" % i) for i in range(8)]
        bank_i = [0]

        bank_pool = [list(range(8))]

        def bank():
            p = bank_pool[0]
            b = p[bank_i[0] % len(p)]
            bank_i[0] += 1
            return b

        cs = {}
        Bc = Buf("consts")
        dc = sc.dma_sem("const")
        for n, ap in cst.items():
            cs[n] = sbuf(G, "k_" + n, ap.shape, F32)
            sc.dma("sp", dc, cs[n][:], ap, writes=[Bc])
        ones_bf = sbuf(G, "ones_bf", (128, 128), BF16)
        ones_f = sbuf(G, "ones_f", (128, 128), F32)
        sc.op("dve", lambda e: e.memset(ones_bf[:], 1.0), writes=[Bc])
        sc.op("dve", lambda e: e.memset(ones_f[:], 1.0), writes=[Bc])
        ident = cs["ident"]

        Bw = {n: Buf("w_" + n, dram=True) for n in WSHAPES}
        dcast = {n: sc.dma_sem("cast_" + n) for n in WSHAPES}

        def cast(names, after=()):
            if after:
                sc._wait_for("pool", list(after), ())
            for n in names:
                R, C = WSHAPES[n]
                for r0 in range(0, R, 512):
                    r1 = min(R, r0 + 512)
                    sc.dma("pool", dcast[n], wbf[n][r0:r1, :], w32[n][r0:r1, :],
                           writes=[Bw[n]], max_dma_last_dim=8192)

        Bwp = {}
        for cg in range(3):
            c0 = cg * 2048
            c1 = min(DFF, c0 + 2048)
            for n in ("ffn1_w_gate", "ffn1_w_up"):
                d = sc.dma_sem("castp")
                Bwp[(n, cg)] = Buf()
                for r0 in range(0, D, 512):
                    sc.dma("pool", d, wbf[n][r0:r0 + 512, c0:c1], w32[n][r0:r0 + 512, c0:c1], writes=[Bwp[(n, cg)]])
        for kp in range(3):
            r0 = kp * 2048
            r1 = min(DFF, r0 + 2048)
            d = sc.dma_sem("castp")
            Bwp[("ffn1_w_down", kp)] = Buf()
            for rr in range(r0, r1, 512):
                sc.dma("pool", d, wbf["ffn1_w_down"][rr:min(r1, rr + 512), :], w32["ffn1_w_down"][rr:min(r1, rr + 512), :],
                       writes=[Bwp[("ffn1_w_down", kp)]])

        def wdep(n, k0, c0):
            if n in ("ffn1_w_gate", "ffn1_w_up"):
                return Bwp[(n, c0 // 2048)]
            if n == "ffn1_w_down":
                return Bwp[(n, k0 // 16)]
            return Bw[n]

        class WStream:
            def __init__(self, stack, nslots=4):
                self.n = nslots
                self.t = [sbuf(stack, "wt%d" % i, (128, 8192), BF16) for i in range(nslots)]
                self.B = [Buf("wt%d" % i) for i in range(nslots)]
                self.d = [sc.dma_sem("wt%d" % i) for i in range(nslots)]
                self.plan = []
                self.issued = 0
                self.used = 0

            def view(self, s, nk, ncol):
                return self.t[s][:, 0:nk * ncol].rearrange("p (k c) -> p k c", k=nk)

            def _issue(self, i):
                (n, k0, nk, c0, ncol) = self.plan[i]
                s = i % self.n
                sc.dma("sp", self.d[s], self.view(s, nk, ncol),
                       wv[n][:, k0:k0 + nk, c0:c0 + ncol], reads=[wdep(n, k0, c0)], writes=[self.B[s]])

            def next(self, tag):
                i = self.used
                assert self.plan[i] == tag, (i, self.plan[i], tag)
                lim = min(len(self.plan), i + self.n - 1)
                while self.issued < lim:
                    self._issue(self.issued)
                    self.issued += 1
                self.used += 1
                s = i % self.n
                return self.view(s, tag[2], tag[4]), self.B[s]

        def plan_ffn(pref):
            p = []
            for cb in range(11):
                ncol = 512 if cb < 10 else 384
                p.append((pref + "_w_gate", 0, 16, cb * 512, ncol))
                p.append((pref + "_w_up", 0, 16, cb * 512, ncol))
            for cg in range(4):
                for kp in range(3):
                    p.append((pref + "_w_down", kp * 16, 16 if kp < 2 else 11, cg * 512, 512))
            return p

        def rstd_from(pb, nfeat, rs, Brs):
            sc.op("act", lambda e: e.activation(out=rs[:], in_=ps[:, pb, :], func=AF.Ln,
                                                scale=1.0 / nfeat, bias=EPS),
                  reads=[Bps[pb]], writes=[Brs])
            sc.op("act", lambda e: e.activation(out=rs[:], in_=rs[:], func=AF.Exp, scale=-0.5), reads=[Brs], writes=[Brs])

        def ffn(ws, pref, xn, Bxn, res, Bres, hid, Bhid, sgt, Bsgt):
            for cb in range(11):
                ncol = 512 if cb < 10 else 384
                wg, Bg = ws.next((pref + "_w_gate", 0, 16, cb * 512, ncol))
                wu, Bu = ws.next((pref + "_w_up", 0, 16, cb * 512, ncol))
                for m in range(ncol // 128):
                    j = cb * 4 + m
                    pg = bank()
                    pu = bank()
                    for k in range(16):
                        sc.mm(lambda e: e.matmul(ps[:, pg, :], lhsT=wg[:, k, m * 128:(m + 1) * 128],
                                                 rhs=xn[:, k, :], start=(k == 0), stop=(k == 15)),
                              reads=[Bg, Bxn[k]], writes=[Bps[pg]], inc=(k == 15))
                    for k in range(16):
                        sc.mm(lambda e: e.matmul(ps[:, pu, :], lhsT=wu[:, k, m * 128:(m + 1) * 128],
                                                 rhs=xn[:, k, :], start=(k == 0), stop=(k == 15)),
                              reads=[Bu, Bxn[k]], writes=[Bps[pu]], inc=(k == 15))
                    tmp = sgt[j % len(sgt)]
                    Bt = Bsgt[j % len(sgt)]
                    sc.op("act", lambda e: e.activation(out=tmp[:], in_=ps[:, pg, :], func=AF.Silu),
                          reads=[Bps[pg]], writes=[Bt])
                    sc.op("dve", lambda e: e.tensor_tensor(out=hid[:, j, :], in0=ps[:, pu, :],
                                                           in1=tmp[:], op=ALU.mult),
                          reads=[Bps[pu], Bt], writes=[Bhid[j]])
            for cg in range(4):
                pbs = [bank() for _ in range(4)]
                for kp in range(3):
                    nk = 16 if kp < 2 else 11
                    wd, Bd = ws.next((pref + "_w_down", kp * 16, nk, cg * 512, 512))
                    for m in range(4):
                        for k in range(nk):
                            kk = kp * 16 + k
                            sc.mm(lambda e: e.matmul(ps[:, pbs[m], :],
                                                     lhsT=wd[:, k, m * 128:(m + 1) * 128],
                                                     rhs=hid[:, kk, :], start=(kk == 0), stop=(kk == 42)),
                                  reads=[Bd, Bhid[kk]], writes=[Bps[pbs[m]]], inc=(k == nk - 1))
                for m in range(4):
                    ob = cg * 4 + m
                    sc.op("dve", lambda e: e.scalar_tensor_tensor(
                        out=res[:, ob, :], in0=ps[:, pbs[m], :], scalar=0.5, in1=res[:, ob, :],
                        op0=ALU.mult, op1=ALU.add),
                        reads=[Bps[pbs[m]], Bres[ob]], writes=[Bres[ob]])

        def norm16(res, Bres, gname, xn, Bxn, sqr, Bsqr, rs, Brs, N=512):
            pb = bank()
            for k in range(16):
                q = sqr[k % len(sqr)]
                Bq = Bsqr[k % len(sqr)]
                en = "adadpadadpadadpad"[k]
                if en == "a":
                    sc.op("act", lambda e: e.activation(out=q[:, 0:N], in_=res[:, k, 0:N], func=AF.Square),
                          reads=[Bres[k]], writes=[Bq])
                else:
                    sc.op("dve" if en == "d" else "pool",
                          lambda e: e.tensor_tensor(out=q[:, 0:N], in0=res[:, k, 0:N],
                                                    in1=res[:, k, 0:N], op=ALU.mult),
                          reads=[Bres[k]], writes=[Bq])
                sc.mm(lambda e: e.matmul(ps[:, pb, 0:N], lhsT=ones_bf[:], rhs=q[:, 0:N],
                                         start=(k == 0), stop=(k == 15)),
                      reads=[Bq, Bc], writes=[Bps[pb]], inc=True)
            sc.op("act", lambda e: e.activation(out=rs[:, 0:N], in_=ps[:, pb, 0:N], func=AF.Ln,
                                                scale=1.0 / D, bias=EPS),
                  reads=[Bps[pb]], writes=[Brs])
            sc.op("act", lambda e: e.activation(out=rs[:, 0:N], in_=rs[:, 0:N], func=AF.Exp, scale=-0.5),
                  reads=[Brs], writes=[Brs])
            g = cs[gname]
            for k in range(16):
                sc.op("dve", lambda e: e.scalar_tensor_tensor(
                    out=xn[:, k, 0:N], in0=res[:, k, 0:N], scalar=g[:, k:k + 1], in1=rs[:, 0:N],
                    op0=ALU.mult, op1=ALU.mult),
                    reads=[Bres[k], Brs, Bc], writes=[Bxn[k]])

        def load_T(src_rows, nrow_groups, res, Bres, stg, Bstg, dstg, col0, preloaded=False):
            for g in range(nrow_groups):
                if preloaded:
                    s = g
                else:
                    s = load_T.cnt % 2
                    load_T.cnt += 1
                    sc.dma("act", dstg[s], stg[s][:], src_rows(g), writes=[Bstg[s]])
                for kq in range(4):
                    pb = bank()
                    for c in range(4):
                        k = kq * 4 + c
                        sc.mm(lambda e: e.transpose(out=ps[:, pb, c * 128:(c + 1) * 128],
                                                    in_=stg[s][:, k * 128:(k + 1) * 128], identity=ident[:]),
                              reads=[Bstg[s], Bc], writes=[Bps[pb]], inc=(c == 3))
                    o_ap = res[:, kq * 4:(kq + 1) * 4, col0 + g * 128:col0 + (g + 1) * 128]
                    i_ap = ps[:, pb, :].rearrange("p (c n) -> p c n", c=4)
                    bl = [Bres[kq * 4 + c] for c in range(4)]
                    if kq % 2 == 0:
                        sc.op("act", lambda e: e.activation(out=o_ap, in_=i_ap, func=AF.Copy),
                              reads=[Bps[pb]], writes=bl)
                    else:
                        sc.op("dve", lambda e: e.tensor_copy(out=o_ap, in_=i_ap),
                              reads=[Bps[pb]], writes=bl)
        load_T.cnt = 0

        Bscr = {n: Buf("scr_" + n, dram=True) for n in
                ["hT", "gqT", "gkT", "gk", "gv", "gsr", "gaT", "QnT", "QrT", "KnT", "KrT", "V", "of", "ymT"]}
        with ExitStack() as A:
            ws = WStream(A)
            for t in range(NT):
                ws.plan += plan_ffn("ffn1")
            xT = sbuf(A, "xT", (128, 16, 512), F32)
            BxT = [Buf("xT%d" % k) for k in range(16)]
            xn = sbuf(A, "xn", (128, 16, 512), BF16)
            Bxn = [Buf("xn%d" % k) for k in range(16)]
            hid = sbuf(A, "hid", (128, 43, 512), BF16)
            Bhid = [Buf("hid%d" % k) for k in range(43)]
            stg = [sbuf(A, "stg%d" % i, (128, 2048), F32) for i in range(4)]
            Bstg = [Buf() for _ in range(4)]
            dstg = [sc.dma_sem("stg%d" % i) for i in range(4)]

            def x_prefetch(t):
                if t < NT:
                    for g in range(4):
                        sc.dma("act", dstg[g], stg[g][:], x_d[t * 512 + g * 128: t * 512 + (g + 1) * 128, :],
                               writes=[Bstg[g]])
            x_prefetch(0)
            sgt = [sbuf(A, "sgt%d" % i, (128, 512), F32) for i in range(3)]
            Bsgt = [Buf() for _ in range(3)]
            sqr = [sbuf(A, "sqr%d" % i, (128, 512), BF16) for i in range(4)]
            Bsqr = [Buf() for _ in range(4)]
            rs = sbuf(A, "rs", (128, 512), F32)
            Brs = Buf()
            dhs = sc.dma_sem("hstore")
            for t in range(NT):
                tc0 = t * 512
                tcols = slice(tc0, tc0 + 512)
                load_T(None, 4, xT, BxT, stg, Bstg, dstg, 0, preloaded=True)
                x_prefetch(t + 1)
                norm16(xT, BxT, "g_ffn1", xn, Bxn, sqr, Bsqr, rs, Brs)
                ffn(ws, "ffn1", xn, Bxn, xT, BxT, hid, Bhid, sgt, Bsgt)
                if dbg and t == 0:
                    d_xn = nc.dram_tensor("d_xn", [128, 16, 512], BF16, kind="ExternalOutput").ap()
                    d_hid = nc.dram_tensor("d_hid", [128, 43, 512], BF16, kind="ExternalOutput").ap()
                    sc.dma("act", dhs, d_xn, xn[:], reads=Bxn, writes=[Buf()])
                    sc.dma("act", dhs, d_hid, hid[:], reads=Bhid, writes=[Buf()])
                if t == 0:
                    cast(["w_in", "mla_w_uq", "mla_w_ukv"] + W_LATE, after=[BxT[15]])
                sc.dma("sp", dhs, hT_s.rearrange("(k p) s -> p k s", p=128)[:, :, tcols], xT[:],
                       reads=BxT, writes=[Bscr["hT"]])
            sc.barrier()
        if stop == "ffn1":
            print("instructions", sc.n_ins, "waits", sc.n_wait)
            return nc

        with ExitStack() as A:
            ws = WStream(A, nslots=3)
            for t in range(NT):
                ws.plan += [("w_in", 0, 16, c0, 512) for c0 in range(0, 3072, 512)]
                ws.plan += [("w_in", 0, 16, 3072, 32), ("w_in", 0, 16, 3104, 512), ("w_in", 0, 16, 3616, 320)]
            xT = sbuf(A, "xT2", (128, 16, 512), F32)
            BxT = [Buf("xT%d" % k) for k in range(16)]
            dxT = sc.dma_sem("xTload")
            xn = sbuf(A, "xn2", (128, 16, 512), BF16)
            Bxn = [Buf("xn%d" % k) for k in range(16)]
            sqr = [sbuf(A, "sqr2_%d" % i, (128, 512), BF16) for i in range(4)]
            Bsqr = [Buf() for _ in range(4)]
            rs = sbuf(A, "rs2", (128, 512), F32)
            Brs = Buf()
            zs = [sbuf(A, "zs%d" % i, (128, 4, 512), F32) for i in range(2)]
            Bzs = [Buf() for _ in range(2)]
            dzs = [sc.dma_sem("zs%d" % i) for i in range(2)]
            zcnt = [0]
            vst = [sbuf(A, "vst%d" % i, (128, 1024), BF16) for i in range(2)]
            Bvst = [Buf() for _ in range(2)]
            dvst = [sc.dma_sem("vst%d" % i) for i in range(2)]
            vsV = [sbuf(A, "vsV%d" % i, (128, 1024), BF16) for i in range(2)]
            BvsV = [Buf() for _ in range(2)]
            dvsV = [sc.dma_sem("vsV%d" % i) for i in range(2)]
            vcntV = [0]
            cqT = sbuf(A, "cqT", (128, 4, 512), F32)
            BcqT = [Buf() for _ in range(4)]
            ckvT = sbuf(A, "ckvT", (128, 2, 512), F32)
            BckvT = [Buf() for _ in range(2)]
            kpeT = sbuf(A, "kpeT", (64, 512), F32)
            BkpeT = Buf()
            sqkpe = sbuf(A, "sqkpe", (64, 512), BF16)
            Bsqkpe = Buf()
            cqn = sbuf(A, "cqn", (128, 4, 512), BF16)
            Bcqn = [Buf() for _ in range(4)]
            ckvn = sbuf(A, "ckvn", (128, 2, 512), BF16)
            Bckvn = [Buf() for _ in range(2)]
            qk_st = [[sbuf(A, "qkst%d_%d" % (a_, i), (128, 512), BF16) for i in range(2)] for a_ in range(4)]
            Bqk_st = [[Buf() for i in range(2)] for a_ in range(4)]
            dqk = [sc.dma_sem("qkst%d" % i) for i in range(2)]
            rq = [sbuf(A, "rq%d" % i, (128, 512), F32) for i in range(2)]
            Brq = [Buf() for _ in range(2)]
            rope_f = [sbuf(A, "ropef%d" % i, (64, 512), F32) for i in range(2)]
            Brope_f = [Buf() for _ in range(2)]
            rt1 = [sbuf(A, "rt1_%d" % i, (64, 512), F32) for i in range(2)]
            rt2 = [sbuf(A, "rt2_%d" % i, (64, 512), F32) for i in range(2)]
            Brt1 = [Buf() for _ in range(2)]
            Brt2 = [Buf() for _ in range(2)]
            cosT = sbuf(A, "cosT", (64, 512), F32)
            sinT = sbuf(A, "sinT", (64, 512), F32)
            Bcs = Buf()
            dcs = sc.dma_sem("cossin")
            ropec = [0]

            def store_z(fill, dst_ap, Bdst):
                i = zcnt[0] % 2
                zcnt[0] += 1
                fill(zs[i], Bzs[i])
                sc.dma("sp", dzs[i], dst_ap, zs[i][:], reads=[Bzs[i]], writes=[Bdst])

            wq = sbuf(A, "wq_res", (128, 4, 1536), BF16)
            wkv = sbuf(A, "wkv_res", (128, 2, 2048), BF16)
            Bwq = Buf()
            Bwkv = Buf()
            dwres = sc.dma_sem("wres")
            sc.dma("sp", dwres, wq[:], wv["mla_w_uq"][:, 0:4, :], reads=[Bw["mla_w_uq"]], writes=[Bwq])
            sc.dma("sp", dwres, wkv[:], wv["mla_w_ukv"][:, 0:2, :], reads=[Bw["mla_w_ukv"]], writes=[Bwkv])
            sqrB = [sbuf(A, "sqrB%d" % i, (128, 512), BF16) for i in range(3)]
            BsqrB = [Buf() for _ in range(3)]
            rsB = sbuf(A, "rsB", (128, 512), F32)
            BrsB = Buf()
            vcnt = [0]

            def wstat(w, Bwt, c0, M, pb, st=False):
                for k in range(16):
                    sc.mm(lambda e: e.matmul(ps[0:M, pb, :], lhsT=w[:, k, c0:c0 + M], rhs=xn[:, k, :],
                                             start=(k == 0), stop=(k == 15)),
                          reads=[Bwt, Bxn[k]], writes=[Bps[pb]], inc=(k == 15))
                    if st and k == 7:
                        step(1)

            def astat(w, Bwt, tg, pb, ncol=512, st=False):
                for k in range(16):
                    sc.mm(lambda e: e.matmul(ps[:, pb, 0:ncol], lhsT=xn[:, k, tg * 128:(tg + 1) * 128],
                                             rhs=w[:, k, 0:ncol], start=(k == 0), stop=(k == 15)),
                          reads=[Bwt, Bxn[k]], writes=[Bps[pb]], inc=(k == 15))
                    if st and k == 7:
                        step(1)

            def do_evac(i, out_ap, pb, Bout, M=128, func=AF.Copy):
                if i % 2 == 0 or func != AF.Copy:
                    sc.op("act", lambda e: e.activation(out=out_ap, in_=ps[0:M, pb, :], func=func),
                          reads=[Bps[pb]], writes=[Bout])
                else:
                    sc.op("dve", lambda e: e.tensor_copy(out=out_ap, in_=ps[0:M, pb, :]),
                          reads=[Bps[pb]], writes=[Bout])

            def normN(src, Bsrc, n, nfeat, gname, dst, Bdst, pb):
                for k in range(n):
                    q = sqrB[k % 3]
                    sc.op("act", lambda e: e.activation(out=q[:], in_=src[:, k, :], func=AF.Square),
                          reads=[Bsrc[k]], writes=[BsqrB[k % 3]])
                    sc.mm(lambda e: e.matmul(ps[:, pb, :], lhsT=ones_bf[:], rhs=q[:], start=(k == 0),
                                             stop=(k == n - 1)),
                          reads=[BsqrB[k % 3], Bc], writes=[Bps[pb]], inc=True)
                rstd_from(pb, nfeat, rsB, BrsB)
                g = cs[gname]
                for k in range(n):
                    sc.op("dve", lambda e: e.scalar_tensor_tensor(
                        out=dst[:, k, :], in0=src[:, k, :], scalar=g[:, k:k + 1], in1=rsB[:],
                        op0=ALU.mult, op1=ALU.mult), reads=[Bsrc[k], BrsB, Bc], writes=[Bdst[k]])

            def rope_a(src_ap, Bsrcs, gcol, rstd, Brstd):
                i = ropec[0] % 2
                ropec[0] += 1
                rf = rope_f[i]
                sc.op("dve", lambda e: e.scalar_tensor_tensor(
                    out=rf[:], in0=src_ap, scalar=cs[gcol][0:64, 1:2], in1=rstd[0:64, :],
                    op0=ALU.mult, op1=ALU.mult), reads=Bsrcs + [Brstd, Bc], writes=[Brope_f[i]])
                return i

            def rope_b(i, pw, dst_ap, Bdst):
                rf = rope_f[i]
                sc.mm(lambda e: e.matmul(ps[0:64, pw, :], lhsT=cs["pswap"][:, :], rhs=rf[:],
                                         start=True, stop=True),
                      reads=[Brope_f[i], Bc], writes=[Bps[pw]], inc=True)
                sc.op("pool", lambda e: e.tensor_tensor(out=rt1[i][:], in0=rf[:], in1=cosT[:], op=ALU.mult),
                      reads=[Brope_f[i], Bcs], writes=[Brt1[i]])
                sc.op("dve", lambda e: e.tensor_tensor(out=rt2[i][:], in0=ps[0:64, pw, :], in1=sinT[:],
                                                       op=ALU.mult),
                      reads=[Bps[pw], Bcs], writes=[Brt2[i]])
                sc.op("pool", lambda e: e.tensor_tensor(out=dst_ap, in0=rt1[i][:], in1=rt2[i][:], op=ALU.add),
                      reads=[Brt1[i], Brt2[i]], writes=[Bdst])

            def prep(t):
                tcols = slice(t * 512, (t + 1) * 512)
                normN(cqT, BcqT, 4, 512.0, "g_mq", cqn, Bcqn, 6)
                yield
                normN(ckvT, BckvT, 2, 256.0, "g_mkv", ckvn, Bckvn, 6)
                yield
                pq, pr, pk = 3, 4, 5
                for h in range(8):
                    hs = h % 2
                    for k in range(4):
                        sc.mm(lambda e: e.matmul(ps[:, pq, :], lhsT=wq[:, k, h * 192:h * 192 + 128],
                                                 rhs=cqn[:, k, :], start=(k == 0), stop=(k == 3)),
                              reads=[Bwq, Bcqn[k]], writes=[Bps[pq]], inc=(k == 3))
                    for k in range(4):
                        sc.mm(lambda e: e.matmul(ps[0:64, pr, :], lhsT=wq[:, k, h * 192 + 128:h * 192 + 192],
                                                 rhs=cqn[:, k, :], start=(k == 0), stop=(k == 3)),
                              reads=[Bwq, Bcqn[k]], writes=[Bps[pr]], inc=(k == 3))
                    for k in range(2):
                        sc.mm(lambda e: e.matmul(ps[:, pk, :], lhsT=wkv[:, k, h * 256:h * 256 + 128],
                                                 rhs=ckvn[:, k, :], start=(k == 0), stop=(k == 1)),
                              reads=[Bwkv, Bckvn[k]], writes=[Bps[pk]], inc=(k == 1))
                    q0, q1, q2 = sqrB[0], sqrB[1], sqrB[2]
                    sc.op("act", lambda e: e.activation(out=q0[:], in_=ps[:, pq, :], func=AF.Square),
                          reads=[Bps[pq]], writes=[BsqrB[0]])
                    sc.op("act", lambda e: e.activation(out=q1[0:64, :], in_=ps[0:64, pr, :], func=AF.Square),
                          reads=[Bps[pr]], writes=[BsqrB[1]])
                    sc.op("act", lambda e: e.activation(out=q2[:], in_=ps[:, pk, :], func=AF.Square),
                          reads=[Bps[pk]], writes=[BsqrB[2]])
                    yield
                    pss, psk = 6, 7
                    sc.mm(lambda e: e.matmul(ps[:, pss, :], lhsT=ones_bf[:], rhs=q0[:], start=True, stop=False),
                          reads=[BsqrB[0], Bc], writes=[Bps[pss]], inc=False)
                    sc.mm(lambda e: e.matmul(ps[:, pss, :], lhsT=ones_bf[0:64, :], rhs=q1[0:64, :],
                                             start=False, stop=True),
                          reads=[BsqrB[1], Bc], writes=[Bps[pss]], inc=True)
                    sc.mm(lambda e: e.matmul(ps[:, psk, :], lhsT=ones_bf[:], rhs=q2[:], start=True, stop=False),
                          reads=[BsqrB[2], Bc], writes=[Bps[psk]], inc=False)
                    sc.mm(lambda e: e.matmul(ps[:, psk, :], lhsT=ones_bf[0:64, :], rhs=sqkpe[:],
                                             start=False, stop=True),
                          reads=[Bsqkpe, Bc], writes=[Bps[psk]], inc=True)
                    rstd_from(pss, 192.0, rq[0], Brq[0])
                    rstd_from(psk, 192.0, rq[1], Brq[1])
                    yield
                    sc.op("dve", lambda e: e.scalar_tensor_tensor(
                        out=qk_st[0][hs][:], in0=ps[:, pq, :], scalar=cs["g_qq"][:, 0:1], in1=rq[0][:],
                        op0=ALU.mult, op1=ALU.mult), reads=[Bps[pq], Brq[0], Bc], writes=[Bqk_st[0][hs]])
                    iq = rope_a(ps[0:64, pr, :], [Bps[pr]], "g_qq", rq[0], Brq[0])
                    sc.op("dve", lambda e: e.scalar_tensor_tensor(
                        out=qk_st[2][hs][:], in0=ps[:, pk, :], scalar=cs["g_qk"][:, 0:1], in1=rq[1][:],
                        op0=ALU.mult, op1=ALU.mult), reads=[Bps[pk], Brq[1], Bc], writes=[Bqk_st[2][hs]])
                    ik = rope_a(kpeT[:, :], [BkpeT], "g_qk", rq[1], Brq[1])
                    sc.dma("sp", dqk[hs], QnT_s[h * 128:(h + 1) * 128, tcols], qk_st[0][hs][:],
                           reads=[Bqk_st[0][hs]], writes=[Bscr["QnT"]])
                    sc.dma("sp", dqk[hs], KnT_s[h * 128:(h + 1) * 128, tcols], qk_st[2][hs][:],
                           reads=[Bqk_st[2][hs]], writes=[Bscr["KnT"]])
                    yield
                    rope_b(iq, 6, qk_st[1][hs][0:64, :], Bqk_st[1][hs])
                    rope_b(ik, 7, qk_st[3][hs][0:64, :], Bqk_st[3][hs])
                    sc.dma("sp", dqk[hs], QrT_s[h * 64:(h + 1) * 64, tcols], qk_st[1][hs][0:64, :],
                           reads=[Bqk_st[1][hs]], writes=[Bscr["QrT"]])
                    sc.dma("sp", dqk[hs], KrT_s[h * 64:(h + 1) * 64, tcols], qk_st[3][hs][0:64, :],
                           reads=[Bqk_st[3][hs]], writes=[Bscr["KrT"]])
                    yield
                wkv_v = wkv[:].rearrange("p k (h two d) -> p k h two d", two=2, d=128)
                for tg in range(4):
                    vs_ = vcntV[0] % 2
                    vcntV[0] += 1
                    for half in range(2):
                        pb = 3 + half
                        for k in range(2):
                            sc.mm(lambda e: e.matmul(ps[:, pb, :].rearrange("p (h d) -> p h d", h=4),
                                                     lhsT=ckvn[:, k, tg * 128:(tg + 1) * 128],
                                                     rhs=wkv_v[:, k, half * 4:(half + 1) * 4, 1, :],
                                                     start=(k == 0), stop=(k == 1)),
                                  reads=[Bwkv, Bckvn[k]], writes=[Bps[pb]], inc=(k == 1))
                        do_evac(half, vsV[vs_][:, half * 512:(half + 1) * 512], pb, BvsV[vs_])
                    r0 = t * 512 + tg * 128
                    sc.dma("sp", dvsV[vs_], V_s[r0:r0 + 128, :], vsV[vs_][:], reads=[BvsV[vs_]], writes=[Bscr["V"]])
                    yield

            prev = [None]

            def step(n=2):
                if prev[0] is not None:
                    for _ in range(n):
                        next(prev[0], None)

            def drain():
                if prev[0] is not None:
                    for _ in prev[0]:
                        pass
                prev[0] = None

            bank_pool[0] = [0, 1, 2]

            def load_h(t):
                if t < NT:
                    sc.dma("act", dxT, xT[:], hT_s.rearrange("(k p) s -> p k s", p=128)[:, :, t * 512:(t + 1) * 512],
                           reads=[Bscr["hT"]], writes=BxT)
            load_h(0)
            for t in range(NT):
                tc0 = t * 512
                tcols = slice(tc0, tc0 + 512)
                norm16(xT, BxT, "g_mix", xn, Bxn, sqr, Bsqr, rs, Brs)
                load_h(t + 1)
                w, Bwt = ws.next(("w_in", 0, 16, 0, 512))

                def fill_gq(z, Bz):
                    for m in range(4):
                        pb = bank()
                        wstat(w, Bwt, m * 128, 128, pb, st=True)
                        do_evac(m, z[:, m, :], pb, Bz)
                        step(1)
                store_z(fill_gq, gqT_s.rearrange("(c p) s -> p c s", p=128)[:, :, tcols], Bscr["gqT"])
                w, Bwt = ws.next(("w_in", 0, 16, 512, 512))
                store_z(fill_gq, gkT_s.rearrange("(c p) s -> p c s", p=128)[:, :, tcols], Bscr["gkT"])

                def fill_tm(z, Bz, func=AF.Copy):
                    for tg in range(4):
                        pb = bank()
                        astat(w, Bwt, tg, pb, st=True)
                        do_evac(tg, z[:, tg, :], pb, Bz, func=func)
                        step(1)
                store_z(fill_tm, gk_s.rearrange("(g p) c -> p g c", p=128)[:, t * 4:(t + 1) * 4, :], Bscr["gk"])
                w0, Bw0 = ws.next(("w_in", 0, 16, 1024, 512))
                w1, Bw1 = ws.next(("w_in", 0, 16, 1536, 512))
                for tg in range(4):
                    vs_ = vcnt[0] % 2
                    vcnt[0] += 1
                    for half in range(2):
                        pb = bank()
                        astat((w0, w1)[half], (Bw0, Bw1)[half], tg, pb, st=True)
                        do_evac(tg + half, vst[vs_][:, half * 512:(half + 1) * 512], pb, Bvst[vs_])
                        step(1)
                    r0 = t * 512 + tg * 128
                    sc.dma("sp", dvst[vs_], gv_s[r0:r0 + 128, :], vst[vs_][:], reads=[Bvst[vs_]], writes=[Bscr["gv"]])
                for half in range(2):
                    w, Bwt = ws.next(("w_in", 0, 16, 2048 + half * 512, 512))
                    store_z(lambda z, Bz: fill_tm(z, Bz, AF.Silu),
                            gsr_s.rearrange("(g p) c -> p g c", p=128)[:, t * 4:(t + 1) * 4, half * 512:(half + 1) * 512],
                            Bscr["gsr"])
                w, Bwt = ws.next(("w_in", 0, 16, 3072, 32))
                i = zcnt[0] % 2
                zcnt[0] += 1
                pb = bank()
                wstat(w, Bwt, 0, 32, pb)
                do_evac(0, zs[i][0:32, 0, :], pb, Bzs[i], M=32)
                sc.dma("sp", dzs[i], gaT_s[:, tcols], zs[i][0:32, 0, :], reads=[Bzs[i]], writes=[Bscr["gaT"]])
                drain()
                w, Bwt = ws.next(("w_in", 0, 16, 3104, 512))
                for m in range(4):
                    pb = bank()
                    wstat(w, Bwt, m * 128, 128, pb)
                    do_evac(m, cqT[:, m, :], pb, BcqT[m])
                w, Bwt = ws.next(("w_in", 0, 16, 3616, 320))
                for m in range(2):
                    pb = bank()
                    wstat(w, Bwt, m * 128, 128, pb)
                    do_evac(m, ckvT[:, m, :], pb, BckvT[m])
                pb = bank()
                wstat(w, Bwt, 256, 64, pb)
                do_evac(0, kpeT[:, :], pb, BkpeT, M=64)
                sc.op("act", lambda e: e.activation(out=sqkpe[:], in_=kpeT[:], func=AF.Square),
                      reads=[BkpeT], writes=[Bsqkpe])
                sc.dma("act", dcs, cosT[:], cos_d[:, tcols], writes=[Bcs])
                sc.dma("act", dcs, sinT[:], sin_d[:, tcols], writes=[Bcs])
                prev[0] = prep(t)
            drain()
            bank_pool[0] = list(range(8))
            sc.barrier()

        if stop == "A":
            print("instructions", sc.n_ins, "waits", sc.n_wait)
            return nc

        with ExitStack() as Bk:
            NQB = S // 512
            dld = sc.dma_sem("p3const")
            Bk3 = Buf("p3const")

            def cload(name, ap, shape, dt=F32):
                t_ = sbuf(Bk, name, shape, dt)
                sc.dma("sp", dld, t_[:], ap, writes=[Bk3])
                return t_
            gain_bc = cload("gain_bc", gla_gain_d, (128, 1024))
            wa2c = [cload("wa2c%d" % i, wa2c_d[i], (32, 512)) for i in range(2)]
            ba = [cload("ba%d" % i, ba_d[i], (1, 512)) for i in range(2)]
            tri = {n: cload("tri" + n, tri_d[n], (128, 128)) for n in tri_d}
            mask = [cload("mask%d" % i, mask_d[i], (128, 512)) for i in range(2)]
            Sst = [[sbuf(Bk, "Sst%d%d" % (d_, h), (128, 256), F32) for h in range(4)] for d_ in range(2)]
            Sbf = [[sbuf(Bk, "Sbf%d%d" % (d_, h), (128, 256), BF16) for h in range(4)] for d_ in range(2)]
            BS = [[Buf() for h in range(4)] for d_ in range(2)]
            BSb = [[Buf() for h in range(4)] for d_ in range(2)]
            for d_ in range(2):
                for h in range(4):
                    sc.op("pool", lambda e: e.memset(Sst[d_][h][:], 0.0), writes=[BS[d_][h]])
                    sc.op("pool", lambda e: e.memset(Sbf[d_][h][:], 0.0), writes=[BSb[d_][h]])
            gin = {}
            Bgin = {}
            for nm, shp, dt in (("qT", (128, 512), F32), ("kT", (128, 512), F32), ("k", (128, 512), F32),
                                ("v", (128, 1024), BF16), ("ga", (32, 128), F32), ("of", (128, 1024), F32),
                                ("sr", (128, 1024), F32)):
                gin[nm] = [sbuf(Bk, "gin_%s%d" % (nm, i), shp, dt) for i in range(2)]
                Bgin[nm] = [Buf() for i in range(2)]
            dgl = [sc.dma_sem("gl%d" % i) for i in range(2)]
            tmp = {}
            Bt = {}
            for nm, shp, dt in (("la", (128, 512), F32), ("E1T", (128, 512), F32), ("E2T", (128, 512), F32),
                                ("E3", (128, 512), F32), ("qtT", (128, 512), BF16), ("ktT", (128, 512), BF16),
                                ("ks", (128, 512), BF16), ("ATm", (128, 512), BF16), ("o_sb", (128, 1024), F32),
                                ("gs", (128, 1024), F32), ("y_sb", (128, 1024), F32), ("yT", (128, 8, 128), BF16),
                                ("junk", (128, 256), F32), ("ss", (128, 4), F32)):
                tmp[nm] = sbuf(Bk, "t_" + nm, shp, dt)
                Bt[nm] = Buf("t_" + nm)
            dgs = sc.dma_sem("glstore")
            NU = 2 * NCH

            def gla_load(u):
                if u >= NU:
                    return
                dr = 0 if u < NCH else 1
                n = u if dr == 0 else (NU - 1 - u)
                sl = u % 2
                c0 = n * 128
                cc = slice(c0, c0 + 128)
                d = dgl[sl]
                sc.dma("sp", d, gin["qT"][sl][:].rearrange("p (h i) -> p h i", h=4),
                       gqT_s.rearrange("(h p) s -> p h s", p=128)[:, :, cc], reads=[Bscr["gqT"]], writes=[Bgin["qT"][sl]])
                sc.dma("sp", d, gin["kT"][sl][:].rearrange("p (h i) -> p h i", h=4),
                       gkT_s.rearrange("(h p) s -> p h s", p=128)[:, :, cc], reads=[Bscr["gkT"]], writes=[Bgin["kT"][sl]])
                sc.dma("sp", d, gin["k"][sl][:], gk_s[cc, :], reads=[Bscr["gk"]], writes=[Bgin["k"][sl]])
                sc.dma("sp", d, gin["v"][sl][:], gv_s[cc, :], reads=[Bscr["gv"]], writes=[Bgin["v"][sl]])
                sc.dma("sp", d, gin["ga"][sl][:], gaT_s[:, cc], reads=[Bscr["gaT"]], writes=[Bgin["ga"][sl]])

            def gla_load_of(u):
                if u >= NU or u < NCH:
                    return
                n = NU - 1 - u
                sl = u % 2
                cc = slice(n * 128, n * 128 + 128)
                d = dgl[sl]
                sc.dma("sp", d, gin["of"][sl][:], of_s[cc, :], reads=[Bscr["of"]], writes=[Bgin["of"][sl]])
                sc.dma("sp", d, gin["sr"][sl][:], gsr_s[cc, :], reads=[Bscr["gsr"]], writes=[Bgin["sr"][sl]])

            def gla_unit(u):
                dr = 0 if u < NCH else 1
                n = u if dr == 0 else (NU - 1 - u)
                sl = u % 2
                c0 = n * 128
                cc = slice(c0, c0 + 128)
                Ln = tri["LnF" if dr == 0 else "LnB"]
                Un = tri["UnF" if dr == 0 else "UnB"]
                la, E1T, E2T, E3 = tmp["la"], tmp["E1T"], tmp["E2T"], tmp["E3"]
                qtT, ktT, ks, ATm = tmp["qtT"], tmp["ktT"], tmp["ks"], tmp["ATm"]
                o_sb, gs, y_sb, yT = tmp["o_sb"], tmp["gs"], tmp["y_sb"], tmp["yT"]
                qT_in, kT_in, k_in, v_in, ga_in = (gin[x][sl] for x in ("qT", "kT", "k", "v", "ga"))
                sc.mm(lambda e: e.matmul(ps[:, 0, :], lhsT=ga_in[0:32, :], rhs=wa2c[dr][0:32, :], start=True, stop=False),
                      reads=[Bgin["ga"][sl], Bk3], writes=[Bps[0]], inc=False)
                sc.mm(lambda e: e.matmul(ps[:, 0, :], lhsT=ones_f[0:1, :], rhs=ba[dr][0:1, :], start=False, stop=True),
                      reads=[Bc, Bk3], writes=[Bps[0]], inc=True)
                sc.op("act", lambda e: e.activation(out=la[:], in_=ps[:, 0, :], func=AF.Exp, scale=-1.0),
                      reads=[Bps[0]], writes=[Bt["la"]])
                sc.op("act", lambda e: e.activation(out=la[:], in_=la[:], func=AF.Ln, bias=1.0),
                      reads=[Bt["la"]], writes=[Bt["la"]])
                yield
                for h in range(4):
                    sc.mm(lambda e: e.matmul(ps[:, 1, h * 128:(h + 1) * 128], lhsT=la[:, h * 128:(h + 1) * 128],
                                             rhs=Ln[:], start=True, stop=True),
                          reads=[Bt["la"], Bk3], writes=[Bps[1]], inc=(h == 3))
                sc.mm(lambda e: e.matmul(ps[:, 0, :], lhsT=Un[:], rhs=la[:], start=True, stop=True),
                      reads=[Bt["la"], Bk3], writes=[Bps[0]], inc=True)
                sc.op("act", lambda e: e.activation(out=E1T[:], in_=ps[:, 1, :], func=AF.Exp),
                      reads=[Bps[1]], writes=[Bt["E1T"]])
                sc.op("act", lambda e: e.activation(out=E2T[:], in_=ps[:, 1, :], func=AF.Exp, scale=-1.0),
                      reads=[Bps[1]], writes=[Bt["E2T"]])
                sc.op("act", lambda e: e.activation(out=E3[:], in_=ps[:, 0, :], func=AF.Exp),
                      reads=[Bps[0]], writes=[Bt["E3"]])
                sc.op("dve", lambda e: e.scalar_tensor_tensor(out=qtT[:], in0=qT_in[:], scalar=float(128.0 ** -0.5),
                                                              in1=E1T[:], op0=ALU.mult, op1=ALU.mult),
                      reads=[Bgin["qT"][sl], Bt["E1T"]], writes=[Bt["qtT"]])
                sc.op("pool", lambda e: e.tensor_tensor(out=ktT[:], in0=kT_in[:], in1=E2T[:], op=ALU.mult),
                      reads=[Bgin["kT"][sl], Bt["E2T"]], writes=[Bt["ktT"]])
                sc.op("pool", lambda e: e.tensor_tensor(out=ks[:], in0=k_in[:], in1=E3[:], op=ALU.mult),
                      reads=[Bgin["k"][sl], Bt["E3"]], writes=[Bt["ks"]])
                yield
                for h in range(4):
                    hs = slice(h * 128, (h + 1) * 128)
                    sc.mm(lambda e: e.matmul(ps[:, 1, hs], lhsT=ktT[:, hs], rhs=qtT[:, hs], start=True, stop=True),
                          reads=[Bt["ktT"], Bt["qtT"]], writes=[Bps[1]], inc=(h == 3))
                sc.op("dve", lambda e: e.tensor_tensor(out=ATm[:], in0=ps[:, 1, :], in1=mask[dr][:], op=ALU.mult),
                      reads=[Bps[1], Bk3], writes=[Bt["ATm"]])
                for h in range(4):
                    hs = slice(h * 128, (h + 1) * 128)
                    cs_ = slice((h % 2) * 256, (h % 2) * 256 + 256)
                    bk = h // 2
                    if h % 2 == 0 and bk == 1:
                        pass
                    sc.mm(lambda e: e.matmul(ps[:, bk, cs_], lhsT=ks[:, hs], rhs=v_in[:, h * 256:(h + 1) * 256],
                                             start=True, stop=True),
                          reads=[Bt["ks"], Bgin["v"][sl]], writes=[Bps[bk]], inc=(h % 2 == 1)) if bk == 0 else None
                yield
                for h in range(4):
                    hs = slice(h * 128, (h + 1) * 128)
                    cs_ = slice((h % 2) * 256, (h % 2) * 256 + 256)
                    bk = 2 + h // 2
                    sc.mm(lambda e: e.matmul(ps[:, bk, cs_], lhsT=ATm[:, hs], rhs=v_in[:, h * 256:(h + 1) * 256],
                                             start=True, stop=False),
                          reads=[Bt["ATm"], Bgin["v"][sl]], writes=[Bps[bk]], inc=False)
                    sc.mm(lambda e: e.matmul(ps[:, bk, cs_], lhsT=qtT[:, hs], rhs=Sbf[dr][h][:], start=False, stop=True),
                          reads=[Bt["qtT"], BSb[dr][h]], writes=[Bps[bk]], inc=(h % 2 == 1))
                for h in range(2, 4):
                    hs = slice(h * 128, (h + 1) * 128)
                    cs_ = slice((h % 2) * 256, (h % 2) * 256 + 256)
                    sc.mm(lambda e: e.matmul(ps[:, 1, cs_], lhsT=ks[:, hs], rhs=v_in[:, h * 256:(h + 1) * 256],
                                             start=True, stop=True),
                          reads=[Bt["ks"], Bgin["v"][sl]], writes=[Bps[1]], inc=(h == 3))
                lastc = 127 if dr == 0 else 0
                for h in range(4):
                    cs_ = slice((h % 2) * 256, (h % 2) * 256 + 256)
                    bk = h // 2
                    dec = E1T[:, h * 128 + lastc:h * 128 + lastc + 1]
                    sc.op("dve", lambda e: e.scalar_tensor_tensor(out=Sst[dr][h][:], in0=Sst[dr][h][:], scalar=dec,
                                                                  in1=ps[:, bk, cs_], op0=ALU.mult, op1=ALU.add),
                          reads=[BS[dr][h], Bt["E1T"], Bps[bk]], writes=[BS[dr][h]])
                    sc.op("pool", lambda e: e.tensor_copy(out=Sbf[dr][h][:], in_=Sst[dr][h][:]),
                          reads=[BS[dr][h]], writes=[BSb[dr][h]])
                yield
                if dr == 0:
                    sc.op("dve", lambda e: e.tensor_copy(out=o_sb[:, 0:512], in_=ps[:, 2, :]),
                          reads=[Bps[2]], writes=[Bt["o_sb"]])
                    sc.op("dve", lambda e: e.tensor_copy(out=o_sb[:, 512:1024], in_=ps[:, 3, :]),
                          reads=[Bps[3], Bt["o_sb"]], writes=[Bt["o_sb"]])
                    sc.dma("pool", dgs, of_s[cc, :], o_sb[:], reads=[Bt["o_sb"]], writes=[Bscr["of"]])
                    return
                of_in, sr_in = gin["of"][sl], gin["sr"][sl]
                sc.op("dve", lambda e: e.tensor_tensor(out=o_sb[:, 0:512], in0=ps[:, 2, :], in1=of_in[:, 0:512], op=ALU.add),
                      reads=[Bps[2], Bgin["of"][sl]], writes=[Bt["o_sb"]])
                sc.op("dve", lambda e: e.tensor_tensor(out=o_sb[:, 512:1024], in0=ps[:, 3, :], in1=of_in[:, 512:1024], op=ALU.add),
                      reads=[Bps[3], Bgin["of"][sl], Bt["o_sb"]], writes=[Bt["o_sb"]])
                sc.op("pool", lambda e: e.tensor_tensor(out=gs[:], in0=gain_bc[:], in1=sr_in[:], op=ALU.mult),
                      reads=[Bk3, Bgin["sr"][sl]], writes=[Bt["gs"]])
                for h in range(4):
                    sc.op("act", lambda e: e.activation(out=tmp["junk"][:], in_=o_sb[:, h * 256:(h + 1) * 256],
                                                        func=AF.Square, accum_out=tmp["ss"][:, h:h + 1]),
                          reads=[Bt["o_sb"]], writes=[Bt["junk"], Bt["ss"]])
                sc.op("act", lambda e: e.activation(out=tmp["ss"][:], in_=tmp["ss"][:], func=AF.Ln, scale=1.0 / 256, bias=EPS),
                      reads=[Bt["ss"]], writes=[Bt["ss"]])
                sc.op("act", lambda e: e.activation(out=tmp["ss"][:], in_=tmp["ss"][:], func=AF.Exp, scale=-0.5),
                      reads=[Bt["ss"]], writes=[Bt["ss"]])
                for h in range(4):
                    hv = slice(h * 256, (h + 1) * 256)
                    sc.op("dve", lambda e: e.scalar_tensor_tensor(out=y_sb[:, hv], in0=o_sb[:, hv], scalar=tmp["ss"][:, h:h + 1],
                                                                  in1=gs[:, hv], op0=ALU.mult, op1=ALU.mult),
                          reads=[Bt["o_sb"], Bt["ss"], Bt["gs"]], writes=[Bt["y_sb"]])
                yield
                for c in range(8):
                    bk = 2 + c // 4
                    sc.mm(lambda e: e.transpose(out=ps[:, bk, (c % 4) * 128:(c % 4 + 1) * 128],
                                                in_=y_sb[:, c * 128:(c + 1) * 128], identity=ident[:]),
                          reads=[Bt["y_sb"], Bc], writes=[Bps[bk]], inc=(c % 4 == 3))
                sc.op("act", lambda e: e.activation(out=yT[:, 0:4, :], in_=ps[:, 2, :].rearrange("p (c n) -> p c n", c=4),
                                                    func=AF.Copy), reads=[Bps[2]], writes=[Bt["yT"]])
                sc.op("dve", lambda e: e.tensor_copy(out=yT[:, 4:8, :], in_=ps[:, 3, :].rearrange("p (c n) -> p c n", c=4)),
                      reads=[Bps[3], Bt["yT"]], writes=[Bt["yT"]])
                sc.dma("pool", dgs, ymT_s.rearrange("(c p) s -> p c s", p=128)[:, 0:8, cc], yT[:],
                       reads=[Bt["yT"]], writes=[Bscr["ymT"]])

            Kn = [sbuf(Bk, "Kn%d" % i, (128, S), BF16) for i in range(2)]
            Kr = [sbuf(Bk, "Kr%d" % i, (128, S), BF16) for i in range(2)]
            Vh = [sbuf(Bk, "Vh%d" % i, (128, NCH, 128), BF16) for i in range(2)]
            BKV = [Buf() for i in range(2)]
            dkv = [sc.dma_sem("kv%d" % i) for i in range(2)]
            Qn = [sbuf(Bk, "Qn%d" % i, (128, 512), BF16) for i in range(2)]
            Qr = [sbuf(Bk, "Qr%d" % i, (128, 512), BF16) for i in range(2)]
            BQ = [Buf() for i in range(2)]
            for i in range(2):
                sc.op("pool", lambda e: e.memset(Kr[i][64:128, :], 0.0), writes=[BKV[i]])
                sc.op("pool", lambda e: e.memset(Qr[i][64:128, :], 0.0), writes=[BQ[i]])
            dq = [sc.dma_sem("q%d" % i) for i in range(2)]
            PT = [sbuf(Bk, "PT%d" % i, (128, 512), BF16) for i in range(4)]
            BPT = [Buf() for i in range(4)]
            rsum = sbuf(Bk, "rsum", (128, 512), F32)
            Brsum = Buf()
            pacc = sbuf(Bk, "pacc", (128, 512), F32)
            Bpacc = Buf()
            yo = [sbuf(Bk, "yo%d" % i, (128, 512), BF16) for i in range(2)]
            Byo = [Buf() for i in range(2)]
            dyo = [sc.dma_sem("yo%d" % i) for i in range(2)]
            NMU = 8 * NQB
            mscale = float(192.0 ** -0.5)

            def kv_load(h):
                if h >= 8:
                    return
                s_ = h % 2
                sc.dma("sp", dkv[s_], Kn[s_][:], KnT_s[h * 128:(h + 1) * 128, :], reads=[Bscr["KnT"]], writes=[BKV[s_]])
                sc.dma("sp", dkv[s_], Kr[s_][0:64, :], KrT_s[h * 64:(h + 1) * 64, :], reads=[Bscr["KrT"]], writes=[BKV[s_]])
                sc.dma("sp", dkv[s_], Vh[s_][:], V_s.rearrange("(g p) c -> p g c", p=128)[:, :, h * 128:(h + 1) * 128],
                       reads=[Bscr["V"]], writes=[BKV[s_]])

            def q_load(u):
                if u >= NMU:
                    return
                h, qb = u // NQB, u % NQB
                s_ = u % 2
                sc.dma("sp", dq[s_], Qn[s_][:], QnT_s[h * 128:(h + 1) * 128, qb * 512:(qb + 1) * 512],
                       reads=[Bscr["QnT"]], writes=[BQ[s_]])
                sc.dma("sp", dq[s_], Qr[s_][0:64, :], QrT_s[h * 64:(h + 1) * 64, qb * 512:(qb + 1) * 512],
                       reads=[Bscr["QrT"]], writes=[BQ[s_]])

            def mla_unit(u, gen):
                h, qb = u // NQB, u % NQB
                ks_ = h % 2
                qs = u % 2
                if qb == 0:
                    kv_load(h + 1)
                q_load(u + 1)
                step = max(1, NCH // 8)

                def score(kc):
                    bk = 4 + kc % 2
                    kcs = slice(kc * 128, (kc + 1) * 128)
                    sc.mm(lambda e: e.matmul(ps[:, bk, :], lhsT=Kn[ks_][:, kcs], rhs=Qn[qs][:], start=True, stop=False),
                          reads=[BKV[ks_], BQ[qs]], writes=[Bps[bk]], inc=False)
                    sc.mm(lambda e: e.matmul(ps[:, bk, :], lhsT=Kr[ks_][:, kcs], rhs=Qr[qs][:], start=False, stop=True),
                          reads=[BKV[ks_], BQ[qs]], writes=[Bps[bk]], inc=True)
                    p_ = kc % 4
                    sc.op("act", lambda e: e.activation(out=PT[p_][:], in_=ps[:, bk, :], func=AF.Exp, scale=mscale),
                          reads=[Bps[bk]], writes=[BPT[p_]])
                score(0)
                if NCH > 1:
                    score(1)
                for kc in range(NCH):
                    p_ = kc % 4
                    if kc + 2 < NCH:
                        score(kc + 2)
                    sc.mm(lambda e: e.matmul(ps[:, 6, :], lhsT=Vh[ks_][:, kc, :], rhs=PT[p_][:], start=(kc == 0),
                                             stop=(kc == NCH - 1)),
                          reads=[BKV[ks_], BPT[p_]], writes=[Bps[6]], inc=(kc == NCH - 1))
                    sc.mm(lambda e: e.matmul(ps[:, 7, :], lhsT=ones_bf[:], rhs=PT[p_][:], start=(kc == 0),
                                             stop=(kc == NCH - 1)),
                          reads=[Bc, BPT[p_]], writes=[Bps[7]], inc=True)
                    if gen is not None and kc % step == step - 1:
                        next(gen, None)
                sc.op("act", lambda e: e.activation(out=rsum[:], in_=ps[:, 7, :], func=AF.Ln), reads=[Bps[7]], writes=[Brsum])
                sc.op("act", lambda e: e.activation(out=rsum[:], in_=rsum[:], func=AF.Exp, scale=-1.0), reads=[Brsum], writes=[Brsum])
                ys = u % 2
                sc.op("dve", lambda e: e.tensor_tensor(out=yo[ys][:], in0=ps[:, 6, :], in1=rsum[:], op=ALU.mult),
                      reads=[Bps[6], Brsum], writes=[Byo[ys]])
                sc.dma("pool", dyo[ys], ymT_s[1024 + h * 128:1024 + (h + 1) * 128, qb * 512:(qb + 1) * 512], yo[ys][:],
                       reads=[Byo[ys]], writes=[Bscr["ymT"]])

            gla_load(0)
            kv_load(0)
            q_load(0)
            nun = max(NU, NMU)
            for u in range(nun):
                gen = None
                if u < NU:
                    gla_load(u + 1)
                    gen = gla_unit(u)
                if u < NMU:
                    mla_unit(u, gen)
                if gen is not None:
                    for _ in gen:
                        pass
                gla_load_of(u + 1)
            sc.barrier()
        if stop == "B":
            print("instructions", sc.n_ins, "waits", sc.n_wait)
            return nc

        h2T_s = dscr("s_h2T", (D, S), F32)
        Bh2 = Buf("scr_h2T", dram=True)
        xscale = float(512.0 ** -0.5)
        with ExitStack() as C:
            ws = WStream(C)
            ws.plan += [("xattn_wk", 0, 16, cb * 512, 512) for cb in range(4)]
            ws.plan += [("xattn_wv", 0, 16, cb * 512, 512) for cb in range(4)]
            for t in range(NT):
                ws.plan += [("w_out", 0, 16, cb * 512, 512) for cb in range(4)]
                ws.plan += [("xattn_wq", 0, 16, cb * 512, 512) for cb in range(4)]
                ws.plan += [("xattn_wo", 0, 16, cb * 512, 512) for cb in range(4)]
            kmn = sbuf(C, "kmn", (128, 16, 256), BF16)
            Bkmn = [Buf() for _ in range(4)]
            vm = sbuf(C, "vm", (128, 2, 2048), BF16)
            Bvm = Buf()
            sqr = [sbuf(C, "sqr4_%d" % i, (128, 512), BF16) for i in range(4)]
            Bsqr = [Buf() for _ in range(4)]
            rs = sbuf(C, "rs4", (128, 512), F32)
            Brs = Buf()
            with ExitStack() as M:
                memT = sbuf(M, "memT", (128, 16, 256), F32)
                BmemT = [Buf() for _ in range(16)]
                memn = sbuf(M, "memn", (128, 16, 256), BF16)
                Bmemn = [Buf() for _ in range(16)]
                kmT = sbuf(M, "kmT", (128, 16, 256), F32)
                BkmT = [Buf() for _ in range(16)]
                stg = [sbuf(M, "mstg%d" % i, (128, 2048), F32) for i in range(2)]
                Bstg = [Buf() for _ in range(2)]
                dstg = [sc.dma_sem("mstg%d" % i) for i in range(2)]
                load_T(lambda g: mem_d[g * 128:(g + 1) * 128, :], 2, memT, BmemT, stg, Bstg, dstg, 0)
                norm16(memT, BmemT, "g_mem", memn, Bmemn, sqr, Bsqr, rs, Brs, N=256)
                for cb in range(4):
                    w, Bwt = ws.next(("xattn_wk", 0, 16, cb * 512, 512))
                    for m in range(4):
                        ob = cb * 4 + m
                        pb = bank()
                        for k in range(16):
                            sc.mm(lambda e: e.matmul(ps[:, pb, 0:256], lhsT=w[:, k, m * 128:(m + 1) * 128],
                                                     rhs=memn[:, k, :], start=(k == 0), stop=(k == 15)),
                                  reads=[Bwt, Bmemn[k]], writes=[Bps[pb]], inc=(k == 15))
                        sc.op("act", lambda e: e.activation(out=kmT[:, ob, :], in_=ps[:, pb, 0:256], func=AF.Copy),
                              reads=[Bps[pb]], writes=[BkmT[ob]])
                    pb = bank()
                    for c in range(4):
                        q = sqr[c]
                        sc.op("act", lambda e: e.activation(out=q[:, 0:256], in_=kmT[:, cb * 4 + c, :], func=AF.Square),
                              reads=[BkmT[cb * 4 + c]], writes=[Bsqr[c]])
                        sc.mm(lambda e: e.matmul(ps[:, pb, 0:256], lhsT=ones_bf[:], rhs=q[:, 0:256], start=(c == 0),
                                                 stop=(c == 3)), reads=[Bsqr[c], Bc], writes=[Bps[pb]], inc=True)
                    sc.op("act", lambda e: e.activation(out=rs[:, 0:256], in_=ps[:, pb, 0:256], func=AF.Ln,
                                                        scale=1.0 / 512, bias=EPS), reads=[Bps[pb]], writes=[Brs])
                    sc.op("act", lambda e: e.activation(out=rs[:, 0:256], in_=rs[:, 0:256], func=AF.Exp, scale=-0.5),
                          reads=[Brs], writes=[Brs])
                    for c in range(4):
                        sc.op("dve", lambda e: e.scalar_tensor_tensor(
                            out=kmn[:, cb * 4 + c, :], in0=kmT[:, cb * 4 + c, :], scalar=cs["g_xk"][:, c:c + 1],
                            in1=rs[:, 0:256], op0=ALU.mult, op1=ALU.mult),
                            reads=[BkmT[cb * 4 + c], Brs, Bc], writes=[Bkmn[cb]])
                for cb in range(4):
                    w, Bwt = ws.next(("xattn_wv", 0, 16, cb * 512, 512))
                    for mg in range(2):
                        pb = bank()
                        for k in range(16):
                            sc.mm(lambda e: e.matmul(ps[:, pb, :], lhsT=memn[:, k, mg * 128:(mg + 1) * 128],
                                                     rhs=w[:, k, :], start=(k == 0), stop=(k == 15)),
                                  reads=[Bwt, Bmemn[k]], writes=[Bps[pb]], inc=(k == 15))
                        sc.op("act", lambda e: e.activation(out=vm[:, mg, cb * 512:(cb + 1) * 512], in_=ps[:, pb, :],
                                                            func=AF.Copy), reads=[Bps[pb]], writes=[Bvm])
                sc.barrier()
            xT = sbuf(C, "xT4", (128, 16, 512), F32)
            BxT = [Buf("xT%d" % k) for k in range(16)]
            dxT = sc.dma_sem("xT4load")
            ym = sbuf(C, "ym", (128, 16, 512), BF16)
            Bym = Buf()
            dym = sc.dma_sem("ymload")
            xn = sbuf(C, "xn4", (128, 16, 512), BF16)
            Bxn = [Buf("xn%d" % k) for k in range(16)]
            ox = sbuf(C, "ox", (128, 16, 512), BF16)
            Box = [Buf() for _ in range(16)]
            qhs = [sbuf(C, "qh%d" % i, (128, 4, 512), F32) for i in range(2)]
            Bqhs = [[Buf() for _ in range(4)] for i in range(2)]
            qn = sbuf(C, "qn", (128, 4, 512), BF16)
            Bqn = [Buf() for _ in range(4)]
            PTx = [sbuf(C, "PTx%d" % i, (128, 512), BF16) for i in range(2)]
            BPTx = [Buf() for _ in range(2)]
            rsx = sbuf(C, "rsx", (128, 512), F32)
            Brsx = Buf()
            dh2 = sc.dma_sem("h2store")
            def load_ym(t):
                if t < NT:
                    sc.dma("act", dym, ym[:], ymT_s.rearrange("(k p) s -> p k s", p=128)[:, :, t * 512:(t + 1) * 512],
                           reads=[Bscr["ymT"]], writes=[Bym])
            load_ym(0)
            for t in range(NT):
                tcols = slice(t * 512, (t + 1) * 512)
                sc.dma("act", dxT, xT[:], hT_s.rearrange("(k p) s -> p k s", p=128)[:, :, tcols],
                       reads=[Bscr["hT"]], writes=BxT)

                def proj_add(wname, src, Bsrc_of, store=False):
                    for cb in range(4):
                        w, Bwt = ws.next((wname, 0, 16, cb * 512, 512))
                        for m in range(4):
                            ob = cb * 4 + m
                            pb = bank()
                            for k in range(16):
                                sc.mm(lambda e: e.matmul(ps[:, pb, :], lhsT=w[:, k, m * 128:(m + 1) * 128],
                                                         rhs=src[:, k, :], start=(k == 0), stop=(k == 15)),
                                      reads=[Bwt, Bsrc_of(k)], writes=[Bps[pb]], inc=(k == 15))
                            sc.op("dve", lambda e: e.tensor_tensor(out=xT[:, ob, :], in0=ps[:, pb, :], in1=xT[:, ob, :],
                                                                   op=ALU.add),
                                  reads=[Bps[pb], BxT[ob]], writes=[BxT[ob]])
                        if store:
                            sc.dma("sp", dh2, h2T_s.rearrange("(k p) s -> p k s", p=128)[:, cb * 4:(cb + 1) * 4, tcols],
                                   xT[:, cb * 4:(cb + 1) * 4, :], reads=BxT[cb * 4:(cb + 1) * 4], writes=[Bh2])
                proj_add("w_out", ym, lambda k: Bym)
                load_ym(t + 1)
                if stop == "C1":
                    sc.dma("sp", dh2, h2T_s.rearrange("(k p) s -> p k s", p=128)[:, :, tcols], xT[:],
                           reads=BxT, writes=[Bh2])
                    continue
                norm16(xT, BxT, "g_xn", xn, Bxn, sqr, Bsqr, rs, Brs)
                def qproj(h):
                    w, Bwt = ws.next(("xattn_wq", 0, 16, h * 512, 512))
                    qh_ = qhs[h % 2]
                    Bqh_ = Bqhs[h % 2]
                    for m in range(4):
                        pb = bank()
                        for k in range(16):
                            sc.mm(lambda e: e.matmul(ps[:, pb, :], lhsT=w[:, k, m * 128:(m + 1) * 128], rhs=xn[:, k, :],
                                                     start=(k == 0), stop=(k == 15)),
                                  reads=[Bwt, Bxn[k]], writes=[Bps[pb]], inc=(k == 15))
                        if m % 2 == 0:
                            sc.op("act", lambda e: e.activation(out=qh_[:, m, :], in_=ps[:, pb, :], func=AF.Copy),
                                  reads=[Bps[pb]], writes=[Bqh_[m]])
                        else:
                            sc.op("dve", lambda e: e.tensor_copy(out=qh_[:, m, :], in_=ps[:, pb, :]),
                                  reads=[Bps[pb]], writes=[Bqh_[m]])
                qproj(0)
                for h in range(4):
                    if h + 1 < 4:
                        qproj(h + 1)
                    qh = qhs[h % 2]
                    Bqh = Bqhs[h % 2]
                    pb = bank()
                    for c in range(4):
                        q = sqr[c]
                        sc.op("act", lambda e: e.activation(out=q[:], in_=qh[:, c, :], func=AF.Square),
                              reads=[Bqh[c]], writes=[Bsqr[c]])
                        sc.mm(lambda e: e.matmul(ps[:, pb, :], lhsT=ones_bf[:], rhs=q[:], start=(c == 0), stop=(c == 3)),
                              reads=[Bsqr[c], Bc], writes=[Bps[pb]], inc=True)
                    rstd_from(pb, 512.0, rs, Brs)
                    for c in range(4):
                        sc.op("dve", lambda e: e.scalar_tensor_tensor(
                            out=qn[:, c, :], in0=qh[:, c, :], scalar=cs["g_xq"][:, c:c + 1], in1=rs[:],
                            op0=ALU.mult, op1=ALU.mult), reads=[Bqh[c], Brs, Bc], writes=[Bqn[c]])
                    for mc in range(2):
                        pb = bank()
                        for c in range(4):
                            sc.mm(lambda e: e.matmul(ps[:, pb, :], lhsT=kmn[:, h * 4 + c, mc * 128:(mc + 1) * 128],
                                                     rhs=qn[:, c, :], start=(c == 0), stop=(c == 3)),
                                  reads=[Bkmn[h], Bqn[c]], writes=[Bps[pb]], inc=(c == 3))
                        sc.op("act", lambda e: e.activation(out=PTx[mc][:], in_=ps[:, pb, :], func=AF.Exp, scale=xscale),
                              reads=[Bps[pb]], writes=[BPTx[mc]])
                    pb = bank()
                    for mc in range(2):
                        sc.mm(lambda e: e.matmul(ps[:, pb, :], lhsT=ones_bf[:], rhs=PTx[mc][:], start=(mc == 0), stop=(mc == 1)),
                              reads=[BPTx[mc], Bc], writes=[Bps[pb]], inc=(mc == 1))
                    sc.op("act", lambda e: e.activation(out=rsx[:], in_=ps[:, pb, :], func=AF.Ln), reads=[Bps[pb]], writes=[Brsx])
                    sc.op("act", lambda e: e.activation(out=rsx[:], in_=rsx[:], func=AF.Exp, scale=-1.0), reads=[Brsx], writes=[Brsx])
                    for c2 in range(4):
                        pb = bank()
                        for mc in range(2):
                            sc.mm(lambda e: e.matmul(ps[:, pb, :], lhsT=vm[:, mc, h * 512 + c2 * 128:h * 512 + (c2 + 1) * 128],
                                                     rhs=PTx[mc][:], start=(mc == 0), stop=(mc == 1)),
                                  reads=[Bvm, BPTx[mc]], writes=[Bps[pb]], inc=(mc == 1))
                        sc.op("dve", lambda e: e.tensor_tensor(out=ox[:, h * 4 + c2, :], in0=ps[:, pb, :], in1=rsx[:],
                                                               op=ALU.mult),
                              reads=[Bps[pb], Brsx], writes=[Box[h * 4 + c2]])
                proj_add("xattn_wo", ox, lambda k: Box[k], store=True)
            sc.barrier()
        if stop in ("C", "C1"):
            print("instructions", sc.n_ins, "waits", sc.n_wait)
            return nc

        with ExitStack() as E:
            ws = WStream(E)
            for t in range(NT):
                ws.plan += plan_ffn("ffn2")
            xT = sbuf(E, "xT5", (128, 16, 512), F32)
            BxT = [Buf("xT%d" % k) for k in range(16)]
            dxT = sc.dma_sem("xT5load")
            xn = sbuf(E, "xn5", (128, 16, 512), BF16)
            Bxn = [Buf("xn%d" % k) for k in range(16)]
            hid = sbuf(E, "hid5", (128, 43, 512), BF16)
            Bhid = [Buf("hid%d" % k) for k in range(43)]
            ost = [sbuf(E, "ost%d" % i, (128, 2048), F32) for i in range(2)]
            Bost = [Buf() for _ in range(2)]
            dost = [sc.dma_sem("ost%d" % i) for i in range(2)]
            sgt = [sbuf(E, "sgt5_%d" % i, (128, 512), F32) for i in range(3)]
            Bsgt = [Buf() for _ in range(3)]
            sqr = [sbuf(E, "sqr5_%d" % i, (128, 512), BF16) for i in range(4)]
            Bsqr = [Buf() for _ in range(4)]
            rs = sbuf(E, "rs5", (128, 512), F32)
            Brs = Buf()
            By = Buf("y", dram=True)
            ocnt = 0
            for t in range(NT):
                tcols = slice(t * 512, (t + 1) * 512)
                sc.dma("act", dxT, xT[:], h2T_s.rearrange("(k p) s -> p k s", p=128)[:, :, tcols],
                       reads=[Bh2], writes=BxT)
                norm16(xT, BxT, "g_ffn2", xn, Bxn, sqr, Bsqr, rs, Brs)
                ffn(ws, "ffn2", xn, Bxn, xT, BxT, hid, Bhid, sgt, Bsgt)
                for tg in range(4):
                    o_ = ocnt % 2
                    ocnt += 1
                    for kq in range(4):
                        pb = bank()
                        for c in range(4):
                            k = kq * 4 + c
                            sc.mm(lambda e: e.transpose(out=ps[:, pb, c * 128:(c + 1) * 128],
                                                        in_=xT[:, k, tg * 128:(tg + 1) * 128], identity=ident[:]),
                                  reads=[BxT[k], Bc], writes=[Bps[pb]], inc=(c == 3))
                        if kq % 2 == 0:
                            sc.op("act", lambda e: e.activation(out=ost[o_][:, kq * 512:(kq + 1) * 512], in_=ps[:, pb, :],
                                                                func=AF.Copy), reads=[Bps[pb]], writes=[Bost[o_]])
                        else:
                            sc.op("dve", lambda e: e.tensor_copy(out=ost[o_][:, kq * 512:(kq + 1) * 512], in_=ps[:, pb, :]),
                                  reads=[Bps[pb]], writes=[Bost[o_]])
                    r0 = t * 512 + tg * 128
                    sc.dma("sp", dost[o_], y_d[r0:r0 + 128, :], ost[o_][:], reads=[Bost[o_]], writes=[By])
            sc.barrier()
        print("instructions", sc.n_ins, "waits", sc.n_wait)
    return nc


def _consts(S):
    c = {}
    c["c_ident"] = np.eye(128, dtype=np.float32)
    ps = np.zeros((64, 64), np.float32)
    for m in range(32):
        ps[m + 32, m] = -1.0
        ps[m, m + 32] = 1.0
    c["c_pswap"] = ps
    inv = (np.float32(10000.0) ** (-np.arange(32, dtype=np.float32) / np.float32(32))).astype(np.float32)
    ang = (np.arange(S, dtype=np.float32)[None, :] * inv[:, None]).astype(np.float32)
    c["c_cos"] = np.concatenate([np.cos(ang), np.cos(ang)], 0).astype(np.float32)
    c["c_sin"] = np.concatenate([np.sin(ang), np.sin(ang)], 0).astype(np.float32)
    j = np.arange(128)[:, None]
    i = np.arange(128)[None, :]
    sc = np.float32(-1.0 / 16.0)
    c["c_LnF"] = (j <= i).astype(np.float32) * sc
    c["c_UnF"] = (j > i).astype(np.float32) * sc
    c["c_LnB"] = (j >= i).astype(np.float32) * sc
    c["c_UnB"] = (j < i).astype(np.float32) * sc
    c["c_maskF"] = np.tile((j <= i).astype(np.float32), (1, 4))
    c["c_maskB"] = np.tile((j >= i).astype(np.float32), (1, 4))
    return c


def _colmajor(v, n):
    return np.ascontiguousarray(np.asarray(v, np.float32).reshape(n, 128).T)


def _shared_inputs(inp, S):
    m = dict(_consts(S))
    for n in WSHAPES:
        m[n] = np.ascontiguousarray(np.asarray(inp[n], np.float32)[0])
    m["g_ffn1"] = _colmajor(inp["ffn1_norm"][0], 16)
    m["g_mix"] = _colmajor(inp["mix_norm"][0], 16)
    m["g_xn"] = _colmajor(inp["xattn_norm"][0], 16)
    m["g_mem"] = _colmajor(inp["mem_norm"][0], 16)
    m["g_ffn2"] = _colmajor(inp["ffn2_norm"][0], 16)
    m["g_mq"] = _colmajor(inp["mla_q_norm"][0], 4)
    m["g_mkv"] = _colmajor(inp["mla_kv_norm"][0], 2)
    for nm, src in (("g_qq", "mla_qk_q_norm"), ("g_qk", "mla_qk_k_norm")):
        g = np.zeros((128, 2), np.float32)
        v = np.asarray(inp[src], np.float32)[0]
        g[:, 0] = v[0:128]
        g[0:64, 1] = v[128:192]
        m[nm] = g
    m["g_xq"] = _colmajor(inp["xattn_q_norm"][0], 4)
    m["g_xk"] = _colmajor(inp["xattn_k_norm"][0], 4)
    m["gla_gain_bc"] = np.ascontiguousarray(
        np.broadcast_to(np.asarray(inp["gla_out_norm"], np.float32)[0][None, :], (128, 1024)))
    z = np.zeros((16, 512), np.float32)
    m["wa2c_f"] = np.concatenate([np.asarray(inp["gla_wa2_fwd"], np.float32)[0], z], 0)
    m["wa2c_b"] = np.concatenate([z, np.asarray(inp["gla_wa2_bwd"], np.float32)[0]], 0)
    m["ba_f"] = np.asarray(inp["gla_ba_fwd"], np.float32).reshape(1, 512)
    m["ba_b"] = np.asarray(inp["gla_ba_bwd"], np.float32).reshape(1, 512)
    return m


def kernel(**inputs):
    S = SEQ
    xs = [np.asarray(inputs["x_prompt"], np.float32)[b] for b in range(2)] + \
         [np.asarray(inputs["x_sample"], np.float32)[b] for b in range(4)]
    ms = [np.asarray(inputs["mem_prompt"], np.float32)[b] for b in range(2)] + \
         [np.asarray(inputs["mem_sample"], np.float32)[b] for b in range(4)]
    shared = _shared_inputs(inputs, S)
    nc = build(S)
    core_seq = {0: 0, 1: 1, 2: 2, 4: 3, 5: 4, 6: 5}
    zx = np.zeros((S, D), np.float32)
    zm = np.zeros((MEM, D), np.float32)
    in_maps = []
    for c in range(8):
        m = dict(shared)
        if c in core_seq:
            m["x"] = np.ascontiguousarray(xs[core_seq[c]])
            m["mem"] = np.ascontiguousarray(ms[core_seq[c]])
        else:
            m["x"] = zx
            m["mem"] = zm
        in_maps.append(m)
    res = run_bass_kernel_spmd(nc, in_maps, core_ids=list(range(8)))
    inv = {q: c for c, q in core_seq.items()}
    ys = [np.asarray(res.results[inv[q]]["y"], np.float32) for q in range(6)]
    return (np.stack(ys[0:2], 0), np.stack(ys[2:6], 0))
```
